# Optimizing a Trainium2 kernel written in Bass

```python
import math, functools
import jax, jax.numpy as jnp
from jax import lax
import numpy as np

D_MODEL = 1024
BATCH = 16
SEQ = 2048
DEPTH = 2

CTX_LEN = 256
GRID_W = 64
EPS = 1e-6
N_MOD = 6
MOD_INIT = 0.5

ATT_HEADS = 8
ATT_KV_HEADS = 2
ATT_GROUP = ATT_HEADS // ATT_KV_HEADS
ATT_HEAD_DIM = 64
ATT_WINDOW = 128
ATT_BLOCK = 128
ROPE_BASE = 10000.0
ROPE_FREQS = ATT_HEAD_DIM // 4
ATT_Q = ATT_HEADS * ATT_HEAD_DIM
ATT_KV = ATT_KV_HEADS * ATT_HEAD_DIM

ML_HEADS = 4
ML_HEAD_DIM = 128
ML_CHUNK = 128
ML_INNER = ML_HEADS * ML_HEAD_DIM

SSD_HEADS = 8
SSD_HEAD_DIM = 64
SSD_GROUPS = 2
SSD_HPG = SSD_HEADS // SSD_GROUPS
SSD_STATE = 128
SSD_CONV = 5
SSD_CHUNK = 128
SSD_INNER = SSD_HEADS * SSD_HEAD_DIM
SSD_XBC = SSD_INNER + 2 * SSD_GROUPS * SSD_STATE

D_FF = -((-8 * D_MODEL) // (3 * 256)) * 256

IN_SPLITS = (ATT_Q, ATT_KV, ATT_KV, ML_INNER, ML_INNER, ML_INNER, ML_INNER, 4 * ML_HEADS,
             SSD_INNER, SSD_XBC, 2 * SSD_HEADS, 3 * D_MODEL)
IN_WIDTH = sum(IN_SPLITS)

kernel_name = "hybrid_gqa_mlstm_ssd_prefix_block"


def rmsnorm(x, g):
    xf = x.astype(jnp.float32)
    y = xf * lax.rsqrt(jnp.mean(xf * xf, axis=-1, keepdims=True) + EPS)
    return (y * g.astype(jnp.float32)).astype(x.dtype)


def split_columns(p):
    offs = [int(o) for o in np.cumsum(IN_SPLITS)[:-1]]
    return jnp.split(p, offs, axis=-1)


def axial_rope_tables(n_tokens):
    rows = n_tokens // GRID_W
    row = jnp.repeat(jnp.arange(rows, dtype=jnp.float32), GRID_W)
    col = jnp.tile(jnp.arange(GRID_W, dtype=jnp.float32), rows)
    inv = ROPE_BASE ** (-jnp.arange(ROPE_FREQS, dtype=jnp.float32) / ROPE_FREQS)
    ang = jnp.stack([row[:, None] * inv, col[:, None] * inv], axis=1)
    return jnp.cos(ang), jnp.sin(ang)


def apply_axial_rope(x, cos, sin):
    shp = x.shape
    xr = x.astype(jnp.float32).reshape(*shp[:-1], 2, 2, ROPE_FREQS)
    bshape = (shp[1],) + (1,) * (len(shp) - 3) + (2, ROPE_FREQS)
    cs, sn = cos.reshape(bshape), sin.reshape(bshape)
    x1, x2 = xr[..., 0, :], xr[..., 1, :]
    out = jnp.stack([x1 * cs - x2 * sn, x1 * sn + x2 * cs], axis=-2)
    return out.reshape(shp).astype(x.dtype)


def latent_window_attention(q, k, v, k_ctx, v_ctx, sink):
    f32 = jnp.float32
    B, T = q.shape[:2]
    nb = T // ATT_BLOCK
    pad = ((0, 0), (ATT_BLOCK, ATT_BLOCK), (0, 0), (0, 0))
    kp, vp = jnp.pad(k, pad), jnp.pad(v, pad)
    scale = ATT_HEAD_DIM ** -0.5
    sk = sink.astype(f32).reshape(1, ATT_KV_HEADS, ATT_GROUP, 1, 1)

    def one_block(j):
        start = j * ATT_BLOCK
        qb = lax.dynamic_slice_in_dim(q, start, ATT_BLOCK, axis=1)
        kb = lax.dynamic_slice_in_dim(kp, start, 3 * ATT_BLOCK, axis=1)
        vb = lax.dynamic_slice_in_dim(vp, start, 3 * ATT_BLOCK, axis=1)
        qpos = start + jnp.arange(ATT_BLOCK)
        kpos = start - ATT_BLOCK + jnp.arange(3 * ATT_BLOCK)
        valid = ((jnp.abs(qpos[:, None] - kpos[None, :]) <= ATT_WINDOW)
                 & (kpos[None, :] >= 0) & (kpos[None, :] < T))
        s_loc = jnp.einsum('bqgrd,bkgd->bgrqk', qb, kb).astype(f32) * scale
        s_loc = jnp.where(valid, s_loc, -jnp.inf)
        s_ctx = jnp.einsum('bqgrd,bcgd->bgrqc', qb, k_ctx).astype(f32) * scale
        m = jnp.maximum(jnp.maximum(jnp.max(s_loc, -1, keepdims=True), jnp.max(s_ctx, -1, keepdims=True)), sk)
        p_loc = jnp.exp(s_loc - m)
        p_ctx = jnp.exp(s_ctx - m)
        den = jnp.sum(p_loc, -1, keepdims=True) + jnp.sum(p_ctx, -1, keepdims=True) + jnp.exp(sk - m)
        o = (jnp.einsum('bgrqk,bkgd->bgrqd', p_loc, vb)
             + jnp.einsum('bgrqc,bcgd->bgrqd', p_ctx, v_ctx)) / den
        return jnp.transpose(o, (0, 3, 1, 2, 4)).reshape(B, ATT_BLOCK, ATT_Q)

    out = lax.map(one_block, jnp.arange(nb))
    return jnp.moveaxis(out, 0, 1).reshape(B, T, ATT_Q)


def context_attention(q, k, v, sink):
    f32 = jnp.float32
    B, Lc = q.shape[:2]
    s = jnp.einsum('bqgrd,bkgd->bgrqk', q, k).astype(f32) * ATT_HEAD_DIM ** -0.5
    sk = sink.astype(f32).reshape(1, ATT_KV_HEADS, ATT_GROUP, 1, 1)
    m = jnp.maximum(jnp.max(s, -1, keepdims=True), sk)
    p = jnp.exp(s - m)
    den = jnp.sum(p, -1, keepdims=True) + jnp.exp(sk - m)
    o = jnp.einsum('bgrqk,bkgd->bqgrd', p / den, v)
    return o.reshape(B, Lc, ATT_Q)


def mlstm_chunk_scan(q, k, v, log_i, log_f, state):
    f32 = jnp.float32
    B, T, H, _ = q.shape
    Dv = v.shape[-1]
    nc = T // ML_CHUNK

    def chunks(a):
        a = a.astype(f32).reshape(B, nc, ML_CHUNK, *a.shape[2:])
        return jnp.moveaxis(a, 1, 0)

    qs, ks, vs = chunks(q), chunks(k), chunks(v)
    gi = jnp.moveaxis(chunks(log_i), -1, 2)
    gf = jnp.moveaxis(chunks(log_f), -1, 2)
    causal = jnp.tril(jnp.ones((ML_CHUNK, ML_CHUNK), bool))

    def step(carry, inp):
        C, n, m = carry
        qc, kc, vc, ic, fc = inp
        b = jnp.cumsum(fc, axis=-1)
        log_d = jnp.where(causal, b[..., :, None] - b[..., None, :] + ic[..., None, :], -jnp.inf)
        log_prev = b + m[..., None]
        m_t = jnp.maximum(jnp.max(log_d, axis=-1), log_prev)
        d = jnp.exp(log_d - m_t[..., None])
        prev = jnp.exp(log_prev - m_t)
        s = jnp.einsum('bthd,bshd->bhts', qc, kc) * d
        num = jnp.einsum('bhts,bshv->bhtv', s, vc) + prev[..., None] * jnp.einsum('bthd,bhvd->bhtv', qc, C)
        den = jnp.sum(s, axis=-1) + prev * jnp.einsum('bthd,bhd->bht', qc, n)
        h = num / jnp.maximum(jnp.abs(den), jnp.exp(-m_t))[..., None]
        b_end = b[..., -1]
        log_w = b_end[..., None] - b + ic
        m_new = jnp.maximum(b_end + m, jnp.max(log_w, axis=-1))
        w = jnp.exp(log_w - m_new[..., None])
        decay = jnp.exp(b_end + m - m_new)
        C_new = decay[..., None, None] * C + jnp.einsum('bhs,bshv,bshd->bhvd', w, vc, kc)
        n_new = decay[..., None] * n + jnp.einsum('bhs,bshd->bhd', w, kc)
        return (C_new, n_new, m_new), h

    final, hs = lax.scan(step, tuple(state), (qs, ks, vs, gi, gf))
    h = jnp.transpose(hs, (1, 0, 3, 2, 4)).reshape(B, T, H, Dv)
    return h, final


def ssd_chunk_scan(x, dt, bmat, cmat, state, a):
    f32 = jnp.float32
    B, T, G, R, P = x.shape
    nc = T // SSD_CHUNK

    def chunks(arr):
        arr = arr.astype(f32).reshape(B, nc, SSD_CHUNK, *arr.shape[2:])
        return jnp.moveaxis(arr, 1, 0)

    dt = dt.astype(f32)
    xs = chunks(x.astype(f32) * dt[..., None])
    las = chunks(dt * a.astype(f32))
    bs, cs = chunks(bmat), chunks(cmat)
    causal = jnp.tril(jnp.ones((SSD_CHUNK, SSD_CHUNK), bool))[:, :, None, None]

    def step(S, inp):
        xc, lc, bc, cc = inp
        acs = jnp.cumsum(lc, axis=1)
        seg = jnp.where(causal, acs[:, :, None] - acs[:, None, :], -jnp.inf)
        cb = jnp.einsum('btgn,bsgn->btsg', cc, bc)
        y = jnp.einsum('btsg,btsgr,bsgrp->btgrp', cb, jnp.exp(seg), xc)
        y = y + jnp.exp(acs)[..., None] * jnp.einsum('btgn,bgrpn->btgrp', cc, S)
        a_end = acs[:, -1]
        w = jnp.exp(a_end[:, None] - acs)
        S_new = jnp.exp(a_end)[..., None, None] * S + jnp.einsum('bsgn,bsgr,bsgrp->bgrpn', bc, w, xc)
        return S_new, y

    final, ys = lax.scan(step, state, (xs, las, bs, cs))
    y = jnp.moveaxis(ys, 0, 1).reshape(B, T, G, R, P)
    return y, final


def two_way_scan(scan_fwd, scan_bwd, init, ctx_fwd, lat_fwd, ctx_bwd, lat_bwd):
    rev = lambda arrs: tuple(t[:, ::-1] for t in arrs)
    hc_f, s_f = scan_fwd(*ctx_fwd, init)
    hl_f, _ = scan_fwd(*lat_fwd, s_f)
    hc_b, s_b = scan_bwd(*rev(ctx_bwd), init)
    hl_b, _ = scan_bwd(*rev(lat_bwd), s_b)
    return hc_f + hc_b[:, ::-1], hl_f + hl_b[:, ::-1]


def depthwise_conv(u, w, b):
    out = lax.conv_general_dilated(
        u, w[:, None, :].astype(u.dtype), window_strides=(1,),
        padding=((SSD_CONV // 2, SSD_CONV // 2),),
        dimension_numbers=('NWC', 'WIO', 'NWC'), feature_group_count=u.shape[-1])
    return out + b


def attention_mixer(lat, ctx, sink, rope, need_ctx):
    def heads(q, k, v):
        B, T = q.shape[:2]
        return (q.reshape(B, T, ATT_KV_HEADS, ATT_GROUP, ATT_HEAD_DIM),
                k.reshape(B, T, ATT_KV_HEADS, ATT_HEAD_DIM),
                v.reshape(B, T, ATT_KV_HEADS, ATT_HEAD_DIM))
    ql, kl, vl = heads(*lat)
    qc, kc, vc = heads(*ctx)
    cos, sin = rope
    ql, kl = apply_axial_rope(ql, cos, sin), apply_axial_rope(kl, cos, sin)
    out_l = latent_window_attention(ql, kl, vl, kc, vc, sink)
    out_c = context_attention(qc, kc, vc, sink) if need_ctx else None
    return out_l, out_c


def mlstm_mixer(lat, ctx, i_bias, f_bias, head_gain, need_ctx):
    f32 = jnp.float32
    ib, fb = i_bias.astype(f32), f_bias.astype(f32)

    def prep(q, k, v, o, g):
        B, T = q.shape[:2]
        shp = (B, T, ML_HEADS, ML_HEAD_DIM)
        g = g.astype(f32).reshape(B, T, 4, ML_HEADS)
        q, k, v = q.reshape(shp), k.reshape(shp) * ML_HEAD_DIM ** -0.5, v.reshape(shp)
        fwd = (q, k, v, g[:, :, 0] + ib[0], jax.nn.log_sigmoid(g[:, :, 1] + fb[0]))
        bwd = (q, k, v, g[:, :, 2] + ib[1], jax.nn.log_sigmoid(g[:, :, 3] + fb[1]))
        return fwd, bwd

    lat_f, lat_b = prep(*lat)
    ctx_f, ctx_b = prep(*ctx)
    B = lat[0].shape[0]
    init = (jnp.zeros((B, ML_HEADS, ML_HEAD_DIM, ML_HEAD_DIM), f32),
            jnp.zeros((B, ML_HEADS, ML_HEAD_DIM), f32),
            jnp.zeros((B, ML_HEADS), f32))
    hc, hl = two_way_scan(mlstm_chunk_scan, mlstm_chunk_scan, init, ctx_f, lat_f, ctx_b, lat_b)
    gain = head_gain.reshape(ML_HEADS, ML_HEAD_DIM)

    def finish(h, o):
        B_, T = h.shape[:2]
        return rmsnorm(h, gain).reshape(B_, T, ML_INNER) * jax.nn.sigmoid(o.astype(f32))

    out_l = finish(hl, lat[3])
    out_c = finish(hc, ctx[3]) if need_ctx else None
    return out_l, out_c


def ssd_mixer(lat, ctx, conv_w, conv_b, dt_bias, a_log, d_skip, norm_gain, need_ctx):
    f32 = jnp.float32
    a = -jnp.exp(a_log.astype(f32)).reshape(2, SSD_GROUPS, SSD_HPG)
    dtb = dt_bias.astype(f32).reshape(2 * SSD_HEADS)
    d = d_skip.astype(f32).reshape(SSD_GROUPS, SSD_HPG, 1)

    def prep(z, xbc, dt):
        B, T = z.shape[:2]
        xbc = jax.nn.silu(depthwise_conv(xbc, conv_w, conv_b))
        xs, bm, cm = jnp.split(xbc, [SSD_INNER, SSD_INNER + SSD_GROUPS * SSD_STATE], axis=-1)
        xs = xs.reshape(B, T, SSD_GROUPS, SSD_HPG, SSD_HEAD_DIM)
        bm = bm.reshape(B, T, SSD_GROUPS, SSD_STATE)
        cm = cm.reshape(B, T, SSD_GROUPS, SSD_STATE)
        dt = jax.nn.softplus(dt.astype(f32) + dtb).reshape(B, T, 2, SSD_GROUPS, SSD_HPG)
        return xs, (xs, dt[:, :, 0], bm, cm), (xs, dt[:, :, 1], bm, cm)

    xs_l, lat_f, lat_b = prep(*lat)
    xs_c, ctx_f, ctx_b = prep(*ctx)
    B = lat[0].shape[0]
    init = jnp.zeros((B, SSD_GROUPS, SSD_HPG, SSD_HEAD_DIM, SSD_STATE), f32)
    yc, yl = two_way_scan(functools.partial(ssd_chunk_scan, a=a[0]), functools.partial(ssd_chunk_scan, a=a[1]),
                          init, ctx_f, lat_f, ctx_b, lat_b)

    def finish(y, xs, z):
        B_, T = z.shape[:2]
        y = (y + d * xs.astype(f32)).reshape(B_, T, SSD_INNER)
        return rmsnorm(y * jax.nn.silu(z.astype(f32)), norm_gain)

    out_l = finish(yl, xs_l, lat[0])
    out_c = finish(yc, xs_c, ctx[0]) if need_ctx else None
    return out_l, out_c


def hybrid_layer(xl, xc, mod_l, mod_c, prm, rope, need_ctx):
    def pre(x, mod, g, i):
        return rmsnorm(x, g) * (1 + mod[..., i + 1, :]) + mod[..., i, :]

    pl = split_columns(pre(xl, mod_l, prm['g_norm1'], 0) @ prm['w_in'])
    pc = split_columns(pre(xc, mod_c, prm['g_norm1'], 0) @ prm['w_in'])
    att_l, att_c = attention_mixer(pl[0:3], pc[0:3], prm['att_sink'], rope, need_ctx)
    ml_l, ml_c = mlstm_mixer(pl[3:8], pc[3:8], prm['ml_i_bias'], prm['ml_f_bias'], prm['ml_head_gain'], need_ctx)
    ss_l, ss_c = ssd_mixer(pl[8:11], pc[8:11], prm['ssd_conv_w'], prm['ssd_conv_b'], prm['ssd_dt_bias'],
                           prm['ssd_a_log'], prm['ssd_d'], prm['ssd_norm_gain'], need_ctx)

    def merge_and_ffn(x, mod, att, ml, ss, gates):
        ga, gm, gs = jnp.split(jax.nn.sigmoid(gates.astype(jnp.float32)), 3, axis=-1)
        y = (ga * (att.astype(x.dtype) @ prm['w_att_out'])
             + gm * (ml.astype(x.dtype) @ prm['w_ml_out'])
             + gs * (ss.astype(x.dtype) @ prm['w_ssd_out']))
        x = x + mod[..., 2, :] * (y.astype(x.dtype) @ prm['w_o'])
        h = pre(x, mod, prm['g_norm2'], 3)
        gate, up = jnp.split(h @ prm['w_up'], 2, axis=-1)
        return x + mod[..., 5, :] * ((jax.nn.silu(gate) * up) @ prm['w_down'])

    xl = merge_and_ffn(xl, mod_l, att_l, ml_l, ss_l, pl[11])
    if need_ctx:
        xc = merge_and_ffn(xc, mod_c, att_c, ml_c, ss_c, pc[11])
    return xl, xc


def setup_inputs(seed: int = 0) -> dict:
    key = jax.random.key(seed)
    ks = iter(jax.random.split(key, 40))
    f32 = jnp.float32
    L, D = DEPTH, D_MODEL

    def normal(shape, scale):
        return jax.random.normal(next(ks), shape, f32) * scale

    def gain(shape):
        return 1.0 + normal(shape, 0.02)

    dt0 = jnp.exp(jax.random.uniform(next(ks), (L, 2, SSD_HEADS), f32, math.log(1e-3), math.log(1e-1)))
    a0 = jax.random.uniform(next(ks), (L, 2, SSD_HEADS), f32, 1.0, 16.0)
    return {
        "x": normal((BATCH, SEQ, D), 1.0),
        "c": normal((BATCH, D), 1.0),
        "ctx": normal((BATCH, CTX_LEN, D), 1.0),
        "c_ctx": normal((D,), 1.0),
        "w_mod": normal((L, D, N_MOD * D), MOD_INIT * D ** -0.5),
        "b_mod": normal((L, N_MOD * D), 0.02),
        "g_norm1": gain((L, D)),
        "w_in": normal((L, D, IN_WIDTH), D ** -0.5),
        "att_sink": normal((L, ATT_HEADS), 0.5),
        "ml_i_bias": normal((L, 2, ML_HEADS), 0.1),
        "ml_f_bias": jnp.linspace(3.0, 6.0, ML_HEADS, dtype=f32) + normal((L, 2, ML_HEADS), 0.1),
        "ml_head_gain": gain((L, ML_INNER)),
        "ssd_conv_w": normal((L, SSD_CONV, SSD_XBC), SSD_CONV ** -0.5),
        "ssd_conv_b": normal((L, SSD_XBC), 0.02),
        "ssd_dt_bias": dt0 + jnp.log(-jnp.expm1(-dt0)),
        "ssd_a_log": jnp.log(a0),
        "ssd_d": 1.0 + normal((L, SSD_HEADS), 0.1),
        "ssd_norm_gain": gain((L, SSD_INNER)),
        "w_att_out": normal((L, ATT_Q, D), ATT_Q ** -0.5),
        "w_ml_out": normal((L, ML_INNER, D), ML_INNER ** -0.5),
        "w_ssd_out": normal((L, SSD_INNER, D), SSD_INNER ** -0.5),
        "w_o": normal((L, D, D), D ** -0.5),
        "g_norm2": gain((L, D)),
        "w_up": normal((L, D, 2 * D_FF), D ** -0.5),
        "w_down": normal((L, D_FF, D), D_FF ** -0.5),
        "g_final": gain((D,)),
    }


def reference(x, c, ctx, c_ctx, w_mod, b_mod, g_norm1, w_in, att_sink, ml_i_bias, ml_f_bias, ml_head_gain,
              ssd_conv_w, ssd_conv_b, ssd_dt_bias, ssd_a_log, ssd_d, ssd_norm_gain,
              w_att_out, w_ml_out, w_ssd_out, w_o, g_norm2, w_up, w_down, g_final):
    rope = axial_rope_tables(x.shape[1])
    xl, xc = x, ctx
    for l in range(DEPTH):
        mod_l = (jax.nn.silu(c) @ w_mod[l] + b_mod[l]).reshape(c.shape[0], 1, N_MOD, -1)
        mod_c = (jax.nn.silu(c_ctx) @ w_mod[l] + b_mod[l]).reshape(1, 1, N_MOD, -1)
        prm = dict(g_norm1=g_norm1[l], w_in=w_in[l], att_sink=att_sink[l],
                   ml_i_bias=ml_i_bias[l], ml_f_bias=ml_f_bias[l], ml_head_gain=ml_head_gain[l],
                   ssd_conv_w=ssd_conv_w[l], ssd_conv_b=ssd_conv_b[l], ssd_dt_bias=ssd_dt_bias[l],
                   ssd_a_log=ssd_a_log[l], ssd_d=ssd_d[l], ssd_norm_gain=ssd_norm_gain[l],
                   w_att_out=w_att_out[l], w_ml_out=w_ml_out[l], w_ssd_out=w_ssd_out[l], w_o=w_o[l],
                   g_norm2=g_norm2[l], w_up=w_up[l], w_down=w_down[l])
        xl, xc = hybrid_layer(xl, xc, mod_l, mod_c, prm, rope, need_ctx=(l < DEPTH - 1))
    return rmsnorm(xl, g_final)
```

```python
import numpy as np
import concourse.bass as bass
import concourse.mybir as mybir

F32 = mybir.dt.float32
BF16 = mybir.dt.bfloat16
AF = mybir.ActivationFunctionType
ALU = mybir.AluOpType
AX = mybir.AxisListType

NS = 4
ND = 24


class Instr:
    __slots__ = ("eng", "idx", "fn", "dma", "deps", "signal", "clock", "dma_id", "sidx")

    def __init__(self, eng, idx, fn, dma):
        self.eng = eng
        self.idx = idx
        self.fn = fn
        self.dma = dma
        self.deps = []
        self.signal = False
        self.clock = None
        self.dma_id = None
        self.sidx = None


def ap_box(ap):
    t = ap.tensor
    dims = list(ap.ap)
    pstride, pn = dims[0]
    shape = list(t.shape)
    per_part = 1
    for s in shape[1:]:
        per_part *= int(s)
    off = int(ap.offset)
    if pstride == 0:
        pstride_eff = per_part
    else:
        pstride_eff = pstride
    p0 = off // per_part
    f0 = off - p0 * per_part
    lo = 0
    hi = 0
    for st, cn in dims[1:]:
        ext = (int(cn) - 1) * int(st)
        if ext >= 0:
            hi += ext
        else:
            lo += ext
    esz = mybir.dt.size(t.dtype) if hasattr(mybir.dt, "size") else None
    return (t.name, p0, p0 + int(pn), f0 + lo, f0 + hi + 1)


class Prog:
    def __init__(self, nc):
        self.nc = nc
        self.engs = {"pe": nc.tensor, "dve": nc.vector, "act": nc.scalar, "pool": nc.gpsimd, "sp": nc.sync}
        self.ins = {e: [] for e in self.engs}
        self.track = {}
        self.known = {e: {} for e in self.engs}
        self.ndma = 0
        self.dmas = []
        self.esize = {}
        self.skip_dram = set()

    def _box(self, ap):
        t = ap.tensor
        sp = str(type(t).__name__)
        if sp.startswith("DRam"):
            if t.name in self.skip_dram:
                return None
            off = int(ap.offset); lo = 0; hi = 0
            for st, cn in ap.ap:
                ext = (int(cn) - 1) * int(st)
                if ext >= 0: hi += ext
                else: lo += ext
            es = {F32: 4, BF16: 2}.get(t.dtype, 4)
            return (t.name, 0, 1, (off + lo) * es, (off + hi + 1) * es)
        name, p0, p1, f0, f1 = ap_box(ap)
        if sp.startswith("PSum"):
            return (name, 0, 128, 0, 2048)
        es = {F32: 4, BF16: 2}.get(t.dtype, None)
        if es is None:
            es = int(ap.nbytes // max(1, ap.size)) if hasattr(ap, "nbytes") else 4
        return (name, p0, p1, f0 * es, f1 * es)

    def _collect(self, ins, box, write):
        name, p0, p1, f0, f1 = box
        BIN = 4096 if not name.startswith("dr_") else (1 << 22)
        bins = range(f0 // BIN, (f1 - 1) // BIN + 1)
        tr = self.track.setdefault(name, {})
        deps = []
        seen = set()
        newrec = (ins, write, p0, p1, f0, f1)
        for bi in bins:
            recs = tr.get(bi)
            if recs is None:
                tr[bi] = [newrec]
                continue
            keep = self._scan(recs, ins, write, p0, p1, f0, f1, deps, seen, bi * BIN, (bi + 1) * BIN,
                              name_is_psum=name.startswith("ps"))
            keep.append(newrec)
            tr[bi] = keep
        return deps

    def _scan(self, recs, ins, write, p0, p1, f0, f1, deps, seen, b0, b1, name_is_psum=False):
        keep = []
        for r in recs:
            rin, rw, q0, q1, g0, g1 = r
            ov = not (q1 <= p0 or p1 <= q0 or g1 <= f0 or f1 <= g0)
            if not ov:
                keep.append(r)
                continue
            if (write or rw or (name_is_psum and rin.eng != ins.eng)) and id(rin) not in seen:
                seen.add(id(rin))
                deps.append((rin, rw))
            covered = write and p0 <= q0 and q1 <= p1 and f0 <= max(g0, b0) and min(g1, b1) <= f1
            if covered:
                continue
            if (not write) and (not rw) and rin.eng == ins.eng and not ins.dma and not rin.dma \
                    and q0 == p0 and q1 == p1 and g0 == f0 and g1 == f1:
                continue
            keep.append(r)
        return keep

    def add(self, eng, fn, reads=(), writes=(), dma=False):
        ins = Instr(eng, len(self.ins[eng]), fn, dma)
        alld = {}
        for ap in reads:
            b = self._box(ap)
            if b is None:
                continue
            for d, dw in self._collect(ins, b, False):
                alld[id(d)] = (d, True)
        for ap in writes:
            b = self._box(ap)
            if b is None:
                continue
            for d, dw in self._collect(ins, b, True):
                prev = alld.get(id(d))
                alld[id(d)] = (d, (prev[1] if prev else False) or False)
        known = self.known[eng]
        if dma:
            ins.dma_id = self.ndma
            self.ndma += 1
            self.dmas.append(ins)
            if ins.dma_id >= ND:
                d = self.dmas[ins.dma_id - ND]
                alld[id(d)] = (d, True)
        need = []
        for d, raw in sorted(alld.values(), key=lambda x: -(x[0].dma_id if x[0].dma else x[0].idx)):
            if d is ins:
                continue
            if d.dma:
                key = ("dma", d.dma_id % ND)
                val = d.dma_id // ND + 1
                if known.get(key, 0) >= val:
                    continue
                need.append(d)
            else:
                if d.eng == eng and not dma:
                    if eng == "pe":
                        continue
                    if not raw:
                        continue
                if known.get(d.eng, -1) >= d.idx:
                    continue
                need.append(d)
        for d in need:
            d.signal = True
            if d.dma:
                key = ("dma", d.dma_id % ND)
                known[key] = max(known.get(key, 0), d.dma_id // ND + 1)
            else:
                known[d.eng] = max(known.get(d.eng, -1), d.idx)
            if d.clock:
                for k, v in d.clock.items():
                    if known.get(k, -1) < v:
                        known[k] = v
        ins.deps = need
        ins.clock = dict(known)
        if not dma:
            ins.clock[eng] = max(ins.clock.get(eng, -1), ins.idx - 1)
        self.ins[eng].append(ins)
        return ins

    def mm(self, out, lhsT, rhs, start=True, stop=True, **kw):
        return self.add("pe", lambda e: e.matmul(out, lhsT, rhs, start=start, stop=stop, **kw),
                        reads=[lhsT, rhs], writes=[out])

    def transpose(self, out, in_, ident):
        return self.add("pe", lambda e: e.transpose(out, in_, ident), reads=[in_, ident], writes=[out])

    def act(self, out, in_, func, bias=None, scale=None, accum_out=None, eng="act"):
        kw = {}
        reads = [in_]
        if bias is not None:
            kw["bias"] = bias
            if not isinstance(bias, (int, float)):
                reads.append(bias)
        if scale is not None:
            kw["scale"] = scale
            if not isinstance(scale, (int, float)):
                reads.append(scale)
        writes = [out]
        if accum_out is not None:
            kw["accum_out"] = accum_out
            writes.append(accum_out)
        return self.add(eng, lambda e: e.activation(out, in_, func, **kw), reads=reads, writes=writes)

    def tt(self, out, in0, in1, op, eng="dve"):
        return self.add(eng, lambda e: e.tensor_tensor(out, in0, in1, op), reads=[in0, in1], writes=[out])

    def ts(self, out, in0, s1, s2, op0, op1=None, eng="dve", accum_out=None):
        reads = [in0]
        for s in (s1, s2):
            if s is not None and not isinstance(s, (int, float)):
                reads.append(s)
        writes = [out]
        kw = {}
        if accum_out is not None:
            kw["accum_out"] = accum_out
            writes.append(accum_out)
        if op1 is None:
            return self.add(eng, lambda e: e.tensor_scalar(out, in0, s1, None, op0, **kw), reads=reads, writes=writes)
        return self.add(eng, lambda e: e.tensor_scalar(out, in0, s1, s2, op0, op1, **kw), reads=reads, writes=writes)

    def stt(self, out, in0, scalar, in1, op0, op1, eng="dve"):
        reads = [in0, in1]
        if not isinstance(scalar, (int, float)):
            reads.append(scalar)
        return self.add(eng, lambda e: e.scalar_tensor_tensor(out, in0, scalar, in1, op0, op1),
                        reads=reads, writes=[out])

    def copy(self, out, in_, eng="dve"):
        if eng == "act":
            return self.add(eng, lambda e: e.copy(out, in_), reads=[in_], writes=[out])
        return self.add(eng, lambda e: e.tensor_copy(out, in_), reads=[in_], writes=[out])

    def reduce(self, out, in_, op, axis=AX.X, eng="dve"):
        return self.add(eng, lambda e: e.tensor_reduce(out, in_, axis, op), reads=[in_], writes=[out])

    def scan(self, out, d0, d1, initial, op0, op1):
        reads = [d0, d1]
        if not isinstance(initial, (int, float)):
            reads.append(initial)
        return self.add("dve", lambda e: e.tensor_tensor_scan(out, d0, d1, initial, op0, op1),
                        reads=reads, writes=[out])

    def recip(self, out, in_):
        return self.add("dve", lambda e: e.reciprocal(out, in_), reads=[in_], writes=[out])

    def memset(self, ap, val, eng="dve"):
        return self.add(eng, lambda e: e.memset(ap, val), reads=[], writes=[ap])

    def dma(self, out, in_, q="sp", **kw):
        return self.add(q, lambda e: e.dma_start(out=out, in_=in_, **kw), reads=[in_], writes=[out], dma=True)

    def emit(self, final_wait_dmas=()):
        nc = self.nc
        dsem = [nc.alloc_semaphore(name=f"d_{i}") for i in range(ND)]
        nsig = {}
        for e, lst in self.ins.items():
            c = 0
            for ins in lst:
                if ins.dma:
                    continue
                if ins.signal:
                    ins.sidx = c
                    c += 1
            nsig[e] = c
        nsr = {e: max(1, -(-nsig[e] // 1800)) for e in self.engs}
        self.nsr = nsr
        rings = {e: [nc.alloc_semaphore(name=f"r_{e}_{i}") for i in range(nsr[e])] for e in self.engs}
        prog = self

        def body(ename):
            def f(engine):
                for ins in prog.ins[ename]:
                    for d in ins.deps:
                        if d.dma:
                            engine.wait_ge(dsem[d.dma_id % ND], 16 * (d.dma_id // ND + 1))
                        else:
                            engine.wait_ge(rings[d.eng][d.sidx % nsr[d.eng]], d.sidx // nsr[d.eng] + 1)
                    r = ins.fn(engine)
                    if ins.dma:
                        r.then_inc(dsem[ins.dma_id % ND], 16)
                    elif ins.signal:
                        r.then_inc(rings[ename][ins.sidx % nsr[ename]], 1)
                if ename == "sp":
                    for d in final_wait_dmas:
                        engine.wait_ge(dsem[d.dma_id % ND], 16 * (d.dma_id // ND + 1))
            return f

        with nc.Block() as block:
            block.sync(body("sp"))
            block.tensor(body("pe"))
            block.vector(body("dve"))
            block.scalar(body("act"))
            block.gpsimd(body("pool"))

from concourse.bass_utils import run_bass_kernel_spmd
import math

D = 1024
T = 2048
LC = 256
TT = T + LC
NCH = TT // 128
DEPTH = 2
DFF = 2816
EPS = 1e-6
NEG = -1.0e30
SLOT_EL = 8192

TBLOCKS = [(0, 256, True), (256, 768, False), (768, 1280, False), (1280, 1792, False), (1792, 2304, False)]


def _piece_list():
    pcs = [("A1", 8 * 768), ("A2", 8 * 640), ("SM1", 8 * 16)]
    for h in range(4):
        pcs.append((f"MH{h}", 8 * 512))
    pcs.append(("SM2", 8 * 16))
    for g in range(2):
        pcs.append((f"SG{g}", 8 * 768))
    for i in range(8):
        pcs.append((f"MG{i}", 8 * 384 + 3 * 4 * 128))
    pcs.append(("WO", 8 * 1024))
    for q in range(11):
        pcs.append((f"FF{q}", 8 * 512 + 2 * 1024))
    return pcs


def _k8(w, cols):
    sub = w[:, cols]
    K, n = sub.shape
    return np.ascontiguousarray(sub.reshape(K // 128, 128, n).transpose(1, 0, 2)).reshape(128, -1)


def _host_layer_weights(l, w_in, w_att_out, w_ml_out, w_ssd_out, w_o, w_up, w_down):
    wi = w_in[l]
    ar = np.arange
    qcols = np.concatenate([np.concatenate([c * 64 + ar(64), (4 + c) * 64 + ar(64)]) for c in range(4)])
    perm64 = np.concatenate([16 + ar(16), ar(16), 48 + ar(16), 32 + ar(16)])
    qperm = (qcols // 64) * 64 + perm64[qcols % 64]
    kcols = 512 + ar(128)
    kperm = 512 + (ar(128) // 64) * 64 + perm64[ar(128) % 64]
    vcols = 640 + ar(128)
    out = []
    out.append(_k8(wi, np.concatenate([qcols, kcols, vcols])))
    out.append(_k8(wi, np.concatenate([qperm, kperm])))
    g0 = 2816
    out.append(_k8(wi, np.concatenate([g0 + ar(4), g0 + 8 + ar(4), g0 + 4 + ar(4), g0 + 12 + ar(4)])))
    for h in range(4):
        out.append(_k8(wi, np.concatenate([768 + h * 128 + ar(128), 1280 + h * 128 + ar(128),
                                           1792 + h * 128 + ar(128), 2304 + h * 128 + ar(128)])))
    z0, x0, d0, m0 = 2832, 3344, 4368, 4384
    out.append(_k8(wi, d0 + ar(16)))
    for g in range(2):
        out.append(_k8(wi, np.concatenate([x0 + g * 256 + ar(256), x0 + 512 + g * 128 + ar(128),
                                           x0 + 768 + g * 128 + ar(128), z0 + g * 256 + ar(256)])))
    arow = np.concatenate([np.concatenate([c * 64 + ar(64), (4 + c) * 64 + ar(64)]) for c in range(4)])
    for i in range(8):
        cc = i * 128 + ar(128)
        a = _k8(wi, np.concatenate([m0 + cc, m0 + 1024 + cc, m0 + 2048 + cc]))
        b0 = _k8(w_att_out[l][arow], cc)
        b1 = _k8(w_ml_out[l], cc)
        b2 = _k8(w_ssd_out[l], cc)
        out.append(np.concatenate([a, b0, b1, b2], axis=1))
    out.append(_k8(w_o[l], ar(1024)))
    for q in range(11):
        j0 = 2 * q * 128
        a = _k8(w_up[l], np.concatenate([j0 + ar(256), DFF + j0 + ar(256)]))
        b = _k8(w_down[l][j0:j0 + 256], ar(1024))
        out.append(np.concatenate([a, b], axis=1))
    cat = np.concatenate(out, axis=1)
    return np.ascontiguousarray(cat, dtype=np.float32)


def _host_consts():
    ar = np.arange
    t = ar(T)
    inv = (10000.0 ** (-ar(16, dtype=np.float32) / 16)).astype(np.float32)
    row = (t // 64).astype(np.float32)
    col = (t % 64).astype(np.float32)
    ang_r = row[None, :] * inv[:, None]
    ang_c = col[None, :] * inv[:, None]
    cos64 = np.concatenate([np.cos(ang_r), np.cos(ang_r), np.cos(ang_c), np.cos(ang_c)], 0)
    sin64 = np.concatenate([-np.sin(ang_r), np.sin(ang_r), -np.sin(ang_c), np.sin(ang_c)], 0)
    cosT = np.concatenate([cos64, cos64], 0).astype(np.float32)
    sinT = np.concatenate([sin64, sin64], 0).astype(np.float32)
    q = ar(128)[:, None]
    k = ar(128)[None, :]
    maskA = np.zeros((128, 384), np.float32)
    maskA[:, 0:128] = np.where(k >= q, 0.0, -1.0e5)
    maskA[:, 256:384] = np.where(k <= q, 0.0, -1.0e5)
    s = ar(128)[:, None]
    tt = ar(128)[None, :]
    maskF = np.where(s <= tt, 0.0, NEG).astype(np.float32)
    maskB = np.where(s >= tt, 0.0, NEG).astype(np.float32)
    sel = np.zeros((8, 8, 128), np.float32)
    for r in range(8):
        sel[r, r, :] = 1.0
    ident = np.eye(128, dtype=np.float32)
    return dict(cosT=cosT, sinT=sinT, maskA=maskA, maskF=maskF, maskB=maskB, sel=sel.reshape(8, 1024), ident=ident)


class Arena:
    def __init__(self, nc, nbytes):
        self.t = nc.alloc_sbuf_tensor("arena", [128, nbytes // 2], BF16)
        self.top = 0
        self.cap = nbytes
        self.peak = 0

    def alloc(self, free, dtype, parts=128):
        if isinstance(free, int):
            free = (free,)
        n = 1
        for f in free:
            n *= f
        es = 4 if dtype == F32 else 2
        off = (self.top + 63) // 64 * 64
        self.top = off + n * es
        self.peak = max(self.peak, self.top)
        assert self.top <= self.cap, f"arena overflow {self.top} > {self.cap}"
        v = self.t[0:parts, off // 2:(off + n * es) // 2]
        if dtype == F32:
            v = v.bitcast(F32)
        if len(free) == 2:
            v = v.rearrange("p (a b) -> p a b", a=free[0])
        elif len(free) == 3:
            v = v.rearrange("p (a b c) -> p a b c", a=free[0], b=free[1])
        return v

    def mark(self):
        return self.top

    def release(self, m):
        self.top = m


def build(NB=2, L=DEPTH, dbg=False, stop_after=None):
    nc = bass.Bass("TRN2", target_bir_lowering=False)
    P = Prog(nc)

    def din(name, shape, dt=F32):
        P.skip_dram.add(name)
        return nc.dram_tensor(name, list(shape), dt, kind="ExternalInput").ap()

    def dscr(name, shape, dt):
        return nc.dram_tensor(name, list(shape), dt, kind="ExternalOutput" if dbg else "Internal").ap()

    pcs = _piece_list()
    poff = {}
    o = 0
    for nm, ne in pcs:
        poff[nm] = (o, ne)
        o += ne
    WTOT = o

    xT_d = din("xT", [NB, D, T])
    ctxT_d = din("ctxT", [NB, D, LC])
    cT_d = din("cT", [128, 24])
    wmod_d = din("wmod", [L, 128, 8 * 6144])
    bmod_d = din("bmod", [L, 128, 48])
    gn_d = din("gn", [128, (2 * L + 1) * 8])
    wl_d = [din(f"wl{l}", [128, WTOT]) for l in range(L)]
    cos_d = din("cosT", [128, T]); sin_d = din("sinT", [128, T])
    maskA_d = din("maskA", [128, 384]); maskF_d = din("maskF", [128, 128]); maskB_d = din("maskB", [128, 128])
    sel_d = din("sel", [8, 1024]); ident_d = din("ident", [128, 128])
    sink_d = din("sinkb", [128, L * 8])
    mlb_d = din("mlb", [8, L * 2])
    mlg_d = din("mlgain", [128, L * 512])
    cw_d = din("convw", [128, L * 8 * 6])
    sdt_d = din("ssddt", [8, L * 4])
    sdk_d = din("ssdskip", [128, L * 512])
    sng_d = din("ssdng", [128, L * 512])
    outT_d = nc.dram_tensor("outT", [NB, D, T], F32, kind="ExternalOutput").ap()

    brs_d = [dscr(f"dr_br{i}", [512, TT], BF16) for i in range(3)]
    yt_d = dscr("dr_yt", [D, TT], BF16)
    xs_d = dscr("dr_xs", [D, TT], F32)
    hdbg_d = dscr("dr_h", [D, TT], BF16) if dbg else None

    AR = Arena(nc, 204 * 1024)
    banks = [nc.alloc_psum_tensor(f"ps{i}", [128, 512], F32) for i in range(8)]
    st = {"rr": 0, "acc": 0, "kq": 0}

    def rr():
        b = banks[st["rr"] % 3]
        st["rr"] += 1
        return b

    def kqb():
        b = banks[3 + st["kq"] % 2]
        st["kq"] += 1
        return b

    def accb():
        b = banks[5 + st["acc"] % 3]
        st["acc"] += 1
        return b

    IDF = AR.alloc(128, F32); P.dma(IDF, ident_d)
    IDB = AR.alloc(128, BF16); P.dma(IDB, ident_d, q="pool")
    ONESB = AR.alloc(128, BF16); P.memset(ONESB, 1.0)
    COS = AR.alloc(T, BF16); P.dma(COS.rearrange("p (a b) -> p a b", b=512), cos_d.rearrange("p (a b) -> p a b", b=512), q="pool")
    SIN = AR.alloc(T, BF16); P.dma(SIN.rearrange("p (a b) -> p a b", b=512), sin_d.rearrange("p (a b) -> p a b", b=512), q="pool")
    MASKA = AR.alloc(384, F32); P.dma(MASKA, maskA_d)
    MASKF = AR.alloc(128, F32); P.dma(MASKF, maskF_d)
    MASKB = AR.alloc(128, F32); P.dma(MASKB, maskB_d)
    SEL = AR.alloc((8, 128), F32, parts=8); P.dma(SEL, sel_d.rearrange("p (a b) -> p a b", a=8))
    SINKB = AR.alloc(L * 8, F32); P.dma(SINKB, sink_d)
    MLB = AR.alloc(L * 2, F32, parts=8); P.dma(MLB, mlb_d)
    NMLB = AR.alloc(L * 2, F32, parts=8); P.ts(NMLB, MLB, -1.0, None, ALU.mult)
    MLG = AR.alloc(L * 512, F32); P.dma(MLG, mlg_d)
    CW = AR.alloc(L * 48, F32); P.dma(CW, cw_d)
    SDT = AR.alloc(L * 4, F32, parts=8); P.dma(SDT, sdt_d)
    SA = AR.alloc(L * 4, F32, parts=8)
    P.act(SA, SDT, AF.Exp)
    P.ts(SA, SA, -1.0, None, ALU.mult)
    SDK = AR.alloc(L * 512, BF16); P.dma(SDK, sdk_d, q="pool")
    SNG = AR.alloc(L * 512, F32); P.dma(SNG, sng_d)
    GN = AR.alloc((2 * L + 1) * 8, F32); P.dma(GN, gn_d)
    EPST = AR.alloc(1, F32); P.memset(EPST, EPS)
    ONEF = AR.alloc(TT, F32, parts=8); P.memset(ONEF, 1.0)
    MOD = [AR.alloc((48, 3), F32) for _ in range(L)]
    GS = [AR.alloc((2, 8, 3), F32) for _ in range(L)]
    H = AR.alloc((8, TT), BF16)
    SLOTS = [AR.alloc(SLOT_EL, BF16) for _ in range(2)]

    wq = []
    for b in range(NB):
        for l in range(L):
            for nm, ne in pcs:
                wq.append((l, nm))
    wst = {"issued": 0, "cur": -1}

    def issue_next():
        i = wst["issued"]
        if i >= len(wq):
            return
        l, nm = wq[i]
        o, ne = poff[nm]
        CH = 512 if ne % 512 == 0 else 128
        P.dma(SLOTS[i % 2][:, 0:ne].rearrange("p (a b) -> p a b", b=CH),
              wl_d[l][:, o:o + ne].rearrange("p (a b) -> p a b", b=CH), q="pool")
        wst["issued"] += 1

    def next_piece(l, nm):
        wst["cur"] += 1
        i = wst["cur"]
        assert wq[i] == (l, nm), (wq[i], l, nm)
        while wst["issued"] <= i:
            issue_next()
        return SLOTS[i % 2]

    def prefetch():
        if wst["issued"] <= wst["cur"] + 1:
            issue_next()

    def wview(slot, off, kc, n):
        return slot[:, off:off + kc * n].rearrange("p (a b) -> p a b", a=kc)

    CT = AR.alloc(24, F32); P.dma(CT, cT_d)
    SC = AR.alloc(24, F32); P.act(SC, CT, AF.Silu)
    BM = AR.alloc(L * 48, F32)
    for l in range(L):
        P.dma(BM[:, l * 48:(l + 1) * 48], bmod_d[l])
    for l in range(L):
        for pc in range(12):
            WM = SLOTS[pc % 2].bitcast(F32).rearrange("p (a b) -> p a b", a=8)
            P.dma(WM, wmod_d[l].rearrange("p (a b) -> p a b", a=8)[:, :, pc * 512:(pc + 1) * 512])
            for cc in range(4):
                ps = rr()
                for kc in range(8):
                    P.mm(ps[:, 0:3], WM[:, kc, cc * 128:(cc + 1) * 128], SC[:, kc * 3:(kc + 1) * 3],
                         start=(kc == 0), stop=(kc == 7))
                col = pc * 4 + cc
                P.ts(MOD[l][:, col, :], ps[:, 0:3], BM[:, l * 48 + col:l * 48 + col + 1], None, ALU.add)
        for n in range(2):
            for kc in range(8):
                P.ts(GS[l][:, n, kc, :], MOD[l][:, (1 + 3 * n) * 8 + kc, :], 1.0,
                     GN[:, (2 * l + n) * 8 + kc:(2 * l + n) * 8 + kc + 1], ALU.add, ALU.mult)

    def modv(l, which, kc, who):
        return MOD[l][:, which * 8 + kc, who:who + 1]

    def norm_to(Xv, l, n, b, dst, gvec=None, blocks=TBLOCKS, dst_f32=False, toff=0):
        for (t0, t1, isctx) in blocks:
            who = 2 if isctx else b
            nt = t1 - t0
            m = AR.mark()
            SQ = AR.alloc((8, nt), BF16)
            for kc in range(8):
                P.act(SQ[:, kc, :], Xv[:, kc, t0:t1], AF.Square)
            ps = rr()
            for kc in range(8):
                P.mm(ps[:, 0:nt], ONESB, SQ[:, kc, :], start=(kc == 0), stop=(kc == 7))
            RS = AR.alloc(nt, F32)
            P.act(RS, ps[:, 0:nt], AF.Sqrt, bias=EPST, scale=1.0 / D)
            P.recip(RS, RS)
            TMP = AR.alloc((2, nt), F32)
            for kc in range(8):
                tv = TMP[:, kc % 2, :]
                P.tt(tv, Xv[:, kc, t0:t1], RS, ALU.mult)
                if gvec is None:
                    P.act(dst[:, kc, t0 - toff:t1 - toff], tv, AF.Identity, scale=GS[l][:, n, kc, who:who + 1],
                          bias=modv(l, 3 * n, kc, who))
                else:
                    P.act(dst[:, kc, t0 - toff:t1 - toff], tv, AF.Identity, scale=gvec[:, kc:kc + 1])
            AR.release(m)

    def proj_fm(W, c0, ncols, t0, t1):
        ps = rr()
        for kc in range(8):
            P.mm(ps[0:ncols, 0:t1 - t0], W[:, kc, c0:c0 + ncols], H[:, kc, t0:t1], start=(kc == 0), stop=(kc == 7))
        return ps[0:ncols, 0:t1 - t0]

    def proj_tm(W, c0, ncols, c):
        ps = rr()
        for kc in range(8):
            P.mm(ps[:, 0:ncols], H[:, kc, c * 128:(c + 1) * 128], W[:, kc, c0:c0 + ncols],
                 start=(kc == 0), stop=(kc == 7))
        return ps[:, 0:ncols]

    def rows_scan(dst, src, op0, op1, d0, reverse):
        if not reverse:
            P.scan(dst[:, 0:LC], d0[:, 0:LC], src[:, 0:LC], 0.0, op0, op1)
            P.scan(dst[:, LC:TT], d0[:, LC:TT], src[:, LC:TT], dst[:, LC - 1:LC], op0, op1)
        else:
            P.scan(dst[:, 0:LC][:, ::-1], d0[:, 0:LC][:, ::-1], src[:, 0:LC][:, ::-1], 0.0, op0, op1)
            P.scan(dst[:, LC:TT][:, ::-1], d0[:, LC:TT][:, ::-1], src[:, LC:TT][:, ::-1], dst[:, 0:1], op0, op1)

    def rows_to_cols(COLS, ci, R):
        for c0 in range(0, NCH, 6):
            ps = rr()
            for c in range(c0, c0 + 6):
                P.transpose(ps[:, (c - c0) * 8:(c - c0) * 8 + 8], R[:, c * 128:(c + 1) * 128], IDF[0:8, 0:8])
            P.copy(COLS[:, c0:c0 + 6, ci, :], ps[:, 0:48].rearrange("p (a b) -> p a b", a=6))

    def contribs(t):
        out = []
        for s in range(0, t):
            out.append((s, 0, False))
        out.append((t, 0, True))
        if t >= 2:
            for s in (0, 1):
                out.append((s, 1, False))
            for s in range(t + 1, NCH):
                out.append((s, 1, False))
        else:
            for s in range(t + 1, 2):
                out.append((s, 1, False))
        out.append((t, 1, True))
        return out

    MASKD = [MASKF, MASKB]

    def decay_T(rowtile, r, t, s, masked, d, biascol):
        ps = rr()
        if masked:
            P.mm(ps[:, 0:128], IDF, MASKD[d], start=True, stop=False)
        P.mm(ps[:, 0:128], SEL[:, r, :], rowtile[:, t * 128:(t + 1) * 128], start=(not masked), stop=True)
        return ps

    for b in range(NB):
        for l in range(L):
            last = (l == L - 1)
            need_ctx = not last
            blocks = TBLOCKS if need_ctx else TBLOCKS[1:]
            m_layer = AR.mark()
            if l == 0:
                X = AR.alloc((8, TT), F32)
                P.dma(X[:, :, 0:LC], ctxT_d[b].rearrange("(a p) t -> p a t", p=128))
                P.dma(X[:, :, LC:TT], xT_d[b].rearrange("(a p) t -> p a t", p=128))
            norm_to(X, l, 0, b, H)
            if l > 0:
                P.dma(xs_d.rearrange("(a p) t -> p a t", p=128), X)
                AR.release(m_x)
                m_layer = m_x
            if dbg and l == 0 and b == 0:
                P.dma(hdbg_d.rearrange("(a p) t -> p a t", p=128), H)
            AR.release(m_layer)
            if stop_after == "norm1":
                break

            m0 = AR.mark()
            W1 = next_piece(l, "A1"); prefetch()
            W1v = wview(W1, 0, 8, 768)
            QT = AR.alloc((4, TT), BF16)
            KT = AR.alloc(TT, BF16)
            VP = [AR.alloc((NCH, 128), BF16) for _ in range(2)]
            P.memset(VP[0], 0.0); P.memset(VP[1], 0.0)
            W2 = next_piece(l, "A2")
            W2v = wview(W2, 0, 8, 640)
            if stop_after == "attv":
                for c in range(NCH):
                    ps = proj_tm(W1v, 640, 128, c)
                    P.copy(VP[0][:, c, 0:64], ps[:, 0:64], eng="act")
                    P.copy(VP[1][:, c, 64:128], ps[:, 64:128], eng="act")
                P.dma(brs_d[0][0:128, :], VP[0].rearrange("p a b -> p (a b)"))
                break
            if stop_after == "attr":
                ps = proj_fm(W1v, 0, 128, 256, 768)
                ps2 = proj_fm(W2v, 0, 128, 256, 768)
                T1 = AR.alloc(512, F32); T2 = AR.alloc(512, F32)
                P.tt(T1, ps, COS[:, 0:512], ALU.mult)
                P.tt(T2, ps2, SIN[:, 0:512], ALU.mult)
                P.tt(QT[:, 0, 256:768], T1, T2, ALU.add)
                ps = proj_fm(W1v, 0, 128, 0, 256)
                P.copy(QT[:, 0, 0:256], ps, eng="act")
                P.dma(brs_d[0].rearrange("(a p) t -> p a t", p=128), QT)
                break
            if stop_after == "attw":
                ps = proj_fm(W1v, 0, 128, 256, 768)
                P.copy(QT[:, 0, 256:768], ps, eng="act")
                ps = proj_tm(W1v, 640, 128, 3)
                P.copy(VP[0][:, 3, 0:64], ps[:, 0:64], eng="act")
                P.dma(brs_d[0].rearrange("(a p) t -> p a t", p=128), QT)
                break
            for (t0, t1, isctx) in TBLOCKS:
                nt = t1 - t0
                for c in range(5):
                    if isctx and c < 4 and not need_ctx:
                        continue
                    dst = QT[:, c, t0:t1] if c < 4 else KT[:, t0:t1]
                    ps = proj_fm(W1v, c * 128, 128, t0, t1)
                    if isctx:
                        P.copy(dst, ps, eng="act")
                    else:
                        ps2 = proj_fm(W2v, c * 128, 128, t0, t1)
                        m = AR.mark()
                        T1 = AR.alloc(nt, F32); T2 = AR.alloc(nt, F32)
                        P.tt(T1, ps, COS[:, t0 - LC:t1 - LC], ALU.mult)
                        P.tt(T2, ps2, SIN[:, t0 - LC:t1 - LC], ALU.mult)
                        P.tt(dst, T1, T2, ALU.add)
                        AR.release(m)
            for c in range(NCH):
                ps = proj_tm(W1v, 640, 128, c)
                P.copy(VP[0][:, c, 0:64], ps[:, 0:64], eng="act")
                P.copy(VP[1][:, c, 64:128], ps[:, 64:128], eng="act")
            prefetch()
            ATs = AR.alloc((4, TT), BF16)
            if not need_ctx:
                P.memset(ATs[:, :, 0:LC], 0.0)
            if stop_after == "attproj":
                P.dma(brs_d[0].rearrange("(a p) t -> p a t", p=128), QT)
                break
            qblocks = list(range(0 if need_ctx else 2, NCH))
            if stop_after == "att1":
                qblocks = qblocks[:1]
            for qb in qblocks:
                lat = qb >= 2
                if lat:
                    loc = [k for k in (qb - 1, qb, qb + 1) if 2 <= k < NCH]
                else:
                    loc = []
                nl = len(loc) * 128
                keych = [0, 1] + loc
                mo = 0 if (not lat or qb - 1 >= 2) else 128
                for c in range(4):
                    acc = accb()
                    for half in range(2):
                        head = c + 4 * half
                        psl = slice(64 * half, 64 * half + 64)
                        lhsT = QT[psl, c, qb * 128:(qb + 1) * 128]
                        m = AR.mark()
                        SM = AR.alloc(640, F32)
                        psB = rr()
                        P.mm(psB[:, 0:256], lhsT, KT[psl, 0:256])
                        P.copy(SM[:, 0:256], psB[:, 0:256], eng="act")
                        if lat:
                            psA = rr()
                            P.mm(psA[:, 0:nl], lhsT, KT[psl, loc[0] * 128:(loc[-1] + 1) * 128])
                            P.tt(SM[:, 256:256 + nl], psA[:, 0:nl], MASKA[:, mo:mo + nl], ALU.add)
                        ntot = 256 + nl
                        sk = SINKB[:, l * 8 + head:l * 8 + head + 1]
                        ST4 = AR.alloc(8, F32)
                        P.reduce(ST4[:, 0:1], SM[:, 0:ntot], ALU.max)
                        P.ts(ST4[:, 1:2], ST4[:, 0:1], 0.125, sk, ALU.mult, ALU.max)
                        P.ts(ST4[:, 2:3], ST4[:, 1:2], -1.0, None, ALU.mult)
                        PB = AR.alloc(640, BF16)
                        P.act(PB[:, 0:ntot], SM[:, 0:ntot], AF.Exp, bias=ST4[:, 2:3], scale=0.125, accum_out=ST4[:, 3:4])
                        P.act(ST4[:, 4:5], ST4[:, 2:3], AF.Exp, bias=sk)
                        P.tt(ST4[:, 5:6], ST4[:, 3:4], ST4[:, 4:5], ALU.add)
                        P.recip(ST4[:, 6:7], ST4[:, 5:6])
                        PN = AR.alloc(640, BF16)
                        P.ts(PN[:, 0:ntot], PB[:, 0:ntot], ST4[:, 6:7], None, ALU.mult)
                        psT = rr().bitcast(BF16)
                        nk = len(keych)
                        for j in range(nk):
                            P.transpose(psT[:, j * 128:(j + 1) * 128], PN[:, j * 128:(j + 1) * 128], IDB)
                        PTs = AR.alloc(640, BF16)
                        P.copy(PTs[:, 0:nk * 128], psT[:, 0:nk * 128], eng="act")
                        for j, kc_ in enumerate(keych):
                            P.mm(acc[:, 0:128], VP[half][:, kc_, :], PTs[:, j * 128:(j + 1) * 128],
                                 start=(half == 0 and j == 0), stop=(half == 1 and j == nk - 1))
                        AR.release(m)
                    P.copy(ATs[:, c, qb * 128:(qb + 1) * 128], acc[:, 0:128])
            P.dma(brs_d[0].rearrange("(a p) t -> p a t", p=128), ATs)
            AR.release(m0)
            if stop_after in ("att", "att1"):
                break

            m0 = AR.mark()
            WS = next_piece(l, "SM1"); prefetch()
            WSv = wview(WS, 0, 8, 16)
            LI = AR.alloc(TT, F32, parts=8); LF = AR.alloc(TT, F32, parts=8)
            for (t0, t1, isctx) in TBLOCKS:
                ps = proj_fm(WSv, 0, 8, t0, t1)
                P.act(LI[:, t0:t1], ps, AF.Identity, bias=MLB[:, 2 * l:2 * l + 1])
                ps = proj_fm(WSv, 8, 8, t0, t1)
                P.act(LF[:, t0:t1], ps, AF.Exp, bias=NMLB[:, 2 * l + 1:2 * l + 2], scale=-1.0)
            P.act(LF, LF, AF.Ln, bias=1.0)
            P.ts(LF, LF, -1.0, None, ALU.mult)
            NMR = [AR.alloc(TT, F32, parts=8) for _ in range(2)]
            COLS = AR.alloc((NCH, 4, 8), F32)
            m1 = AR.mark()
            for d in range(2):
                T1 = AR.alloc(TT, F32, parts=8); T4 = AR.alloc(TT, F32, parts=8)
                T2 = NMR[d]
                rows_scan(T1, LF, ALU.mult, ALU.add, ONEF, d == 1)
                pass
                P.tt(T1, LI, T1, ALU.subtract)
                rows_scan(T2, T1, ALU.max, ALU.max, T1, d == 1)
                P.tt(T4, LI, T1, ALU.subtract)
                P.tt(T4, T4, T2, ALU.add)
                P.act(T4, T4, AF.Exp, scale=-1.0)
                rows_to_cols(COLS, d * 2 + 0, T1)
                rows_to_cols(COLS, d * 2 + 1, T4)
                P.ts(T2, T2, -1.0, None, ALU.mult)
                AR.release(m1)
            if stop_after == "mlrows":
                P.dma(brs_d[1][0:128, 0:NCH * 32], COLS.rearrange("p a b c -> p (a b c)").bitcast(BF16)[:, 0:NCH * 32])
                P.dma(brs_d[1][128:136, :], NMR[0].bitcast(BF16)[:, 0:TT])
                break
            MTs = AR.alloc((4, TT), BF16)
            if not need_ctx:
                P.memset(MTs[:, :, 0:LC], 0.0)
            for h in range(4):
                WH = next_piece(l, f"MH{h}"); prefetch()
                WHv = wview(WH, 0, 8, 512)
                m2 = AR.mark()
                QTh = AR.alloc(TT, BF16); KTh = AR.alloc(TT, BF16)
                VA = AR.alloc((NCH, 132), BF16); OS = AR.alloc((NCH, 128), BF16)
                P.memset(VA[:, :, 128:129], 1.0)
                for (t0, t1, isctx) in TBLOCKS:
                    ps = proj_fm(WHv, 0, 128, t0, t1)
                    P.copy(QTh[:, t0:t1], ps, eng="act")
                    ps = proj_fm(WHv, 128, 128, t0, t1)
                    P.ts(KTh[:, t0:t1], ps, 128.0 ** -0.5, None, ALU.mult)
                for c in range(NCH):
                    ps = proj_tm(WHv, 256, 256, c)
                    P.copy(VA[:, c, 0:128], ps[:, 0:128])
                    P.act(OS[:, c, :], ps[:, 128:256], AF.Sigmoid)
                for t in (range(NCH) if need_ctx else range(2, NCH)):
                    if stop_after == "mlh1" and t > 2:
                        break
                    accs = [accb(), accb()]
                    cl = contribs(t)
                    cnt = [sum(1 for x in cl if x[1] == d) for d in range(2)]
                    seen = [0, 0]
                    for (s, d, masked) in cl:
                        pk = kqb()
                        P.mm(pk[:, 0:128], KTh[:, s * 128:(s + 1) * 128], QTh[:, t * 128:(t + 1) * 128])
                        r = d * 4 + h
                        psD = decay_T(NMR[d], r, t, s, masked, d, None)
                        m3 = AR.mark()
                        DT_ = AR.alloc(128, F32)
                        P.act(DT_, psD[:, 0:128], AF.Exp, bias=COLS[:, s, d * 2, r:r + 1])
                        ST_ = AR.alloc(128, BF16)
                        P.tt(ST_, pk[:, 0:128], DT_, ALU.mult)
                        P.mm(accs[d][:, 0:129], ST_, VA[:, s, 0:129], start=(seen[d] == 0), stop=(seen[d] == cnt[d] - 1))
                        seen[d] += 1
                        AR.release(m3)
                    m3 = AR.mark()
                    HD = []
                    for d in range(2):
                        r = d * 4 + h
                        A_ = AR.alloc(132, F32)
                        P.copy(A_[:, 0:129], accs[d][:, 0:129], eng="act")
                        S4 = AR.alloc(4, F32)
                        P.stt(S4[:, 0:1], A_[:, 128:129], -1.0, A_[:, 128:129], ALU.mult, ALU.max)
                        P.tt(S4[:, 1:2], S4[:, 0:1], COLS[:, t, d * 2 + 1, r:r + 1], ALU.max)
                        P.recip(S4[:, 2:3], S4[:, 1:2])
                        Hd = AR.alloc(128, F32)
                        P.ts(Hd, A_[:, 0:128], S4[:, 2:3], None, ALU.mult)
                        HD.append(Hd)
                    HS = AR.alloc(128, F32)
                    P.tt(HS, HD[0], HD[1], ALU.add)
                    S4 = AR.alloc(4, F32)
                    JK = AR.alloc(128, F32)
                    P.act(JK, HS, AF.Square, accum_out=S4[:, 0:1])
                    P.act(S4[:, 1:2], S4[:, 0:1], AF.Sqrt, bias=EPST, scale=1.0 / 128)
                    P.recip(S4[:, 2:3], S4[:, 1:2])
                    O1 = AR.alloc(128, F32)
                    P.stt(O1, HS, S4[:, 2:3], MLG[:, l * 512 + h * 128:l * 512 + (h + 1) * 128], ALU.mult, ALU.mult)
                    O2 = AR.alloc(128, BF16)
                    P.tt(O2, O1, OS[:, t, :], ALU.mult)
                    pT = rr().bitcast(BF16)
                    P.transpose(pT[:, 0:128], O2, IDB)
                    P.copy(MTs[:, h, t * 128:(t + 1) * 128], pT[:, 0:128], eng="act")
                    AR.release(m3)
                AR.release(m2)
                if stop_after == "mlh1":
                    break
            P.dma(brs_d[1].rearrange("(a p) t -> p a t", p=128), MTs)
            AR.release(m0)
            if stop_after in ("ml", "mlh1"):
                break

            m0 = AR.mark()
            WS = next_piece(l, "SM2"); prefetch()
            WSv = wview(WS, 0, 8, 16)
            ACS = [AR.alloc(TT, F32, parts=8) for _ in range(2)]
            COLS = AR.alloc((NCH, 4, 8), F32)
            m1 = AR.mark()
            for d in range(2):
                DTr = AR.alloc(TT, F32, parts=8); T1 = AR.alloc(TT, F32, parts=8)
                for (t0, t1, isctx) in TBLOCKS:
                    ps = proj_fm(WSv, 8 * d, 8, t0, t1)
                    P.act(DTr[:, t0:t1], ps, AF.Exp, bias=SDT[:, 4 * l + d:4 * l + d + 1])
                P.act(DTr, DTr, AF.Ln, bias=1.0)
                P.ts(T1, DTr, SA[:, 4 * l + 2 + d:4 * l + 3 + d], None, ALU.mult)
                rows_scan(ACS[d], T1, ALU.mult, ALU.add, ONEF, d == 1)
                P.ts(T1, ACS[d], -1.0, None, ALU.mult)
                rows_to_cols(COLS, d * 2 + 0, T1)
                rows_to_cols(COLS, d * 2 + 1, DTr)
                AR.release(m1)
            YG = AR.alloc((NCH, 512), BF16)
            SSQ = AR.alloc((NCH, 8), F32)
            trange = list(range(NCH) if need_ctx else range(2, NCH))
            for g in range(2):
                WG = next_piece(l, f"SG{g}"); prefetch()
                WGv = wview(WG, 0, 8, 768)
                m2 = AR.mark()
                XTOK = AR.alloc((NCH, 256), BF16)
                BTg = AR.alloc(TT, BF16); CTg = AR.alloc(TT, BF16)
                ZS = AR.alloc((NCH, 256), BF16)
                m3 = AR.mark()
                PAD = AR.alloc(2312, F32); ACC = AR.alloc(2308, F32); XF = AR.alloc(TT, BF16)
                P.memset(PAD[:, 0:2], 0.0); P.memset(PAD[:, 258:262], 0.0); P.memset(PAD[:, 2310:2312], 0.0)
                for ch in range(4):
                    cidx = [g * 2, g * 2 + 1, 4 + g, 6 + g][ch]
                    cw = CW[:, (l * 8 + cidx) * 6:(l * 8 + cidx) * 6 + 6]
                    for (t0, t1, isctx) in TBLOCKS:
                        ps = proj_fm(WGv, ch * 128, 128, t0, t1)
                        o0 = 2 + t0 if isctx else 262 + (t0 - LC)
                        P.copy(PAD[:, o0:o0 + (t1 - t0)], ps, eng="act")
                    P.ts(ACC, PAD[:, 0:2308], cw[:, 0:1], None, ALU.mult)
                    for j in range(1, 5):
                        P.stt(ACC, PAD[:, j:j + 2308], cw[:, j:j + 1], ACC, ALU.mult, ALU.add)
                    dst = XF if ch < 2 else (BTg if ch == 2 else CTg)
                    P.act(dst[:, 0:LC], ACC[:, 0:LC], AF.Silu, bias=cw[:, 5:6])
                    P.act(dst[:, LC:TT], ACC[:, 260:2308], AF.Silu, bias=cw[:, 5:6])
                    if ch < 2:
                        for c0 in range(0, NCH, 4):
                            cs = list(range(c0, min(c0 + 4, NCH)))
                            pT = rr().bitcast(BF16)
                            for c in cs:
                                P.transpose(pT[:, (c - c0) * 128:(c - c0 + 1) * 128], XF[:, c * 128:(c + 1) * 128], IDB)
                            P.copy(XTOK[:, c0:c0 + len(cs), ch * 128:(ch + 1) * 128],
                                   pT[:, 0:len(cs) * 128].rearrange("p (a b) -> p a b", a=len(cs)))
                AR.release(m3)
                BTOK = None
                for c in range(NCH):
                    ps = proj_tm(WGv, 512, 256, c)
                    P.act(ZS[:, c, :], ps, AF.Silu)
                ring = [(AR.alloc(128, F32), AR.alloc(128, BF16)) for _ in range(6)]
                rc = 0
                for r4 in range(4):
                    hd = g * 4 + r4
                    for t in trange:
                        acc = accb()
                        cl = contribs(t)
                        cnt = len(cl)
                        for ci_, (s, d, masked) in enumerate(cl):
                            pk = kqb()
                            P.mm(pk[:, 0:128], BTg[:, s * 128:(s + 1) * 128], CTg[:, t * 128:(t + 1) * 128])
                            psD = decay_T(ACS[d], hd, t, s, masked, d, None)
                            DT_, MT_ = ring[rc % 6]
                            rc += 1
                            P.act(DT_, psD[:, 0:128], AF.Exp, bias=COLS[:, s, d * 2, hd:hd + 1])
                            P.stt(MT_, pk[:, 0:128], COLS[:, s, d * 2 + 1, hd:hd + 1], DT_, ALU.mult, ALU.mult)
                            P.mm(acc[:, 0:64], MT_, XTOK[:, s, r4 * 64:(r4 + 1) * 64],
                                 start=(ci_ == 0), stop=(ci_ == cnt - 1))
                        m4 = AR.mark()
                        Y1 = AR.alloc(64, F32)
                        P.tt(Y1, XTOK[:, t, r4 * 64:(r4 + 1) * 64],
                             SDK[:, l * 512 + hd * 64:l * 512 + (hd + 1) * 64], ALU.mult)
                        P.tt(Y1, Y1, acc[:, 0:64], ALU.add)
                        P.tt(YG[:, t, hd * 64:(hd + 1) * 64], Y1, ZS[:, t, r4 * 64:(r4 + 1) * 64], ALU.mult)
                        JK = AR.alloc(64, F32)
                        P.act(JK, YG[:, t, hd * 64:(hd + 1) * 64], AF.Square, accum_out=SSQ[:, t, hd:hd + 1])
                        AR.release(m4)
                AR.release(m2)
            STs = AR.alloc((4, TT), BF16)
            if not need_ctx:
                P.memset(STs[:, :, 0:LC], 0.0)
            for t in trange:
                m4 = AR.mark()
                S4 = AR.alloc(4, F32)
                P.reduce(S4[:, 0:1], SSQ[:, t, :], ALU.add)
                P.act(S4[:, 1:2], S4[:, 0:1], AF.Sqrt, bias=EPST, scale=1.0 / 512)
                P.recip(S4[:, 2:3], S4[:, 1:2])
                O2 = AR.alloc(512, BF16)
                P.stt(O2, YG[:, t, :], S4[:, 2:3], SNG[:, l * 512:(l + 1) * 512], ALU.mult, ALU.mult)
                pT = rr().bitcast(BF16)
                for ch in range(4):
                    P.transpose(pT[:, ch * 128:(ch + 1) * 128], O2[:, ch * 128:(ch + 1) * 128], IDB)
                P.copy(STs[:, :, t * 128:(t + 1) * 128], pT[:, 0:512].rearrange("p (a b) -> p a b", a=4), eng="act")
                AR.release(m4)
            P.dma(brs_d[2].rearrange("(a p) t -> p a t", p=128), STs)
            AR.release(m0)
            if stop_after == "ssd":
                break

            m0 = AR.mark()
            BR = [AR.alloc((4, TT), BF16) for _ in range(3)]
            for i in range(3):
                P.dma(BR[i], brs_d[i].rearrange("(a p) t -> p a t", p=128))
            for i in range(8):
                WMp = next_piece(l, f"MG{i}"); prefetch()
                Wg = wview(WMp, 0, 8, 384)
                Wo_ = WMp[:, 3072:3072 + 1536].rearrange("p (b a c) -> p b a c", b=3, a=4)
                m_i = AR.mark()
                YTi = AR.alloc(TT, BF16)
                for (t0, t1, isctx) in blocks:
                    nt = t1 - t0
                    m1 = AR.mark()
                    YA = AR.alloc(nt, F32); TM = AR.alloc(nt, F32)
                    for br in range(3):
                        psG = proj_fm(Wg, br * 128, 128, t0, t1)
                        SG = AR.alloc(nt, F32)
                        P.act(SG, psG, AF.Sigmoid)
                        psO = rr()
                        for kc in range(4):
                            P.mm(psO[:, 0:nt], Wo_[:, br, kc, :], BR[br][:, kc, t0:t1], start=(kc == 0), stop=(kc == 3))
                        if br == 0:
                            P.tt(YA, SG, psO[:, 0:nt], ALU.mult)
                        elif br == 1:
                            P.tt(TM, SG, psO[:, 0:nt], ALU.mult)
                            P.tt(YA, YA, TM, ALU.add, eng="pool")
                        else:
                            P.tt(TM, SG, psO[:, 0:nt], ALU.mult)
                            P.tt(YTi[:, t0:t1], YA, TM, ALU.add, eng="pool")
                    AR.release(m1)
                P.dma(yt_d[i * 128:(i + 1) * 128, blocks[0][0]:TT], YTi[:, blocks[0][0]:TT])
                AR.release(m_i)
            AR.release(m0)

            m_x = AR.mark()
            X = AR.alloc((8, TT), F32)
            if l == 0:
                P.dma(X[:, :, 0:LC], ctxT_d[b].rearrange("(a p) t -> p a t", p=128))
                P.dma(X[:, :, LC:TT], xT_d[b].rearrange("(a p) t -> p a t", p=128))
            else:
                P.dma(X, xs_d.rearrange("(a p) t -> p a t", p=128))
            WOp = next_piece(l, "WO"); prefetch()
            WOv = wview(WOp, 0, 8, 1024)
            for (t0, t1, isctx) in blocks:
                nt = t1 - t0
                who = 2 if isctx else b
                m1 = AR.mark()
                Yb = AR.alloc((8, nt), BF16)
                P.dma(Yb, yt_d.rearrange("(a p) t -> p a t", p=128)[:, :, t0:t1])
                for o in range(8):
                    ps = rr()
                    for kc in range(8):
                        P.mm(ps[:, 0:nt], WOv[:, kc, o * 128:(o + 1) * 128], Yb[:, kc, :], start=(kc == 0), stop=(kc == 7))
                    P.stt(X[:, o, t0:t1], ps[:, 0:nt], modv(l, 2, o, who), X[:, o, t0:t1], ALU.mult, ALU.add)
                AR.release(m1)
            if stop_after == "wo":
                break

            norm_to(X, l, 1, b, H, blocks=blocks)
            for q in range(11):
                WF = next_piece(l, f"FF{q}"); prefetch()
                Wu = wview(WF, 0, 8, 512)
                Wd = wview(WF, 4096, 2, 1024)
                for (t0, t1, isctx) in blocks:
                    nt = t1 - t0
                    who = 2 if isctx else b
                    m1 = AR.mark()
                    A_ = AR.alloc((2, nt), BF16)
                    for jj in range(2):
                        psg = proj_fm(Wu, jj * 128, 128, t0, t1)
                        SGt = AR.alloc(nt, F32)
                        P.act(SGt, psg, AF.Silu)
                        psu = proj_fm(Wu, 256 + jj * 128, 128, t0, t1)
                        P.tt(A_[:, jj, :], SGt, psu, ALU.mult)
                    for o in range(8):
                        ps = rr()
                        for jj in range(2):
                            P.mm(ps[:, 0:nt], Wd[:, jj, o * 128:(o + 1) * 128], A_[:, jj, :], start=(jj == 0), stop=(jj == 1))
                        P.stt(X[:, o, t0:t1], ps[:, 0:nt], modv(l, 5, o, who), X[:, o, t0:t1], ALU.mult, ALU.add)
                    AR.release(m1)
            if last:
                for (t0, t1, isctx) in [(LC + 256 * i, LC + 256 * (i + 1), False) for i in range(8)]:
                    m1 = AR.mark()
                    OUT = AR.alloc((8, t1 - t0), F32)
                    norm_to(X, l, 0, b, OUT, gvec=GN[:, 2 * L * 8:(2 * L + 1) * 8], blocks=[(t0, t1, isctx)], toff=t0)
                    P.dma(outT_d[b].rearrange("(a p) t -> p a t", p=128)[:, :, t0 - LC:t1 - LC], OUT)
                    AR.release(m1)
                AR.release(m_x)
        else:
            continue
        break

    outs = [d for d in P.dmas if False]
    final = [d for d in P.dmas]
    P.emit(final_wait_dmas=final[-ND:])
    AR.P = P
    return nc, AR


_CACHE = {}


def host_prep(inputs, NB=2, L=DEPTH):
    f32 = np.float32
    g = {k: np.asarray(v, dtype=f32) for k, v in inputs.items()}
    consts = _host_consts()
    wl = [_host_layer_weights(l, g["w_in"], g["w_att_out"], g["w_ml_out"], g["w_ssd_out"], g["w_o"], g["w_up"], g["w_down"])
          for l in range(L)]
    wmod = np.stack([_k8(g["w_mod"][l], np.arange(6144)) for l in range(L)])
    bmod = np.stack([g["b_mod"][l].reshape(48, 128).T for l in range(L)])
    gn = np.concatenate([np.concatenate([g["g_norm1"][l].reshape(8, 128).T, g["g_norm2"][l].reshape(8, 128).T], 1)
                         for l in range(L)] + [g["g_final"].reshape(8, 128).T], 1)
    sinkb = np.broadcast_to(g["att_sink"][:L].reshape(1, L * 8), (128, L * 8))
    mlb = np.concatenate([np.stack([g["ml_i_bias"][l].reshape(8), g["ml_f_bias"][l].reshape(8)], 1) for l in range(L)], 1)
    mlgain = np.broadcast_to(g["ml_head_gain"][:L].reshape(1, L * 512), (128, L * 512))
    cw = []
    for l in range(L):
        for ci in range(8):
            ch = slice(ci * 128, (ci + 1) * 128)
            cw.append(np.concatenate([g["ssd_conv_w"][l][:, ch].T, g["ssd_conv_b"][l][ch][:, None]], 1))
    convw = np.concatenate(cw, 1)
    sdt = np.concatenate([np.stack([g["ssd_dt_bias"][l][0], g["ssd_dt_bias"][l][1], g["ssd_a_log"][l][0], g["ssd_a_log"][l][1]], 1)
                          for l in range(L)], 1)
    ssdskip = np.broadcast_to(np.repeat(g["ssd_d"][:L], 64, axis=1).reshape(1, L * 512), (128, L * 512))
    ssdng = np.broadcast_to(g["ssd_norm_gain"][:L].reshape(1, L * 512), (128, L * 512))
    shared = dict(wmod=wmod, bmod=bmod, gn=gn, sinkb=sinkb, mlb=mlb, mlgain=mlgain, convw=convw, ssddt=sdt,
                  ssdskip=ssdskip, ssdng=ssdng, **consts)
    for l in range(L):
        shared[f"wl{l}"] = wl[l]
    shared = {k: np.ascontiguousarray(v, dtype=f32) for k, v in shared.items()}
    ncores = g["x"].shape[0] // NB
    maps = []
    for c in range(ncores):
        bs = slice(c * NB, (c + 1) * NB)
        m = dict(shared)
        m["xT"] = np.ascontiguousarray(g["x"][bs].transpose(0, 2, 1))
        m["ctxT"] = np.ascontiguousarray(g["ctx"][bs].transpose(0, 2, 1))
        cc = np.stack([g["c"][c * NB + i] if i < NB else g["c_ctx"] for i in range(2)] + [g["c_ctx"]], 0)
        if NB == 1:
            cc = np.stack([g["c"][c], g["c"][c], g["c_ctx"]], 0)
        silu_in = cc.reshape(3, 8, 128).transpose(2, 1, 0).reshape(128, 24)
        m["cT"] = np.ascontiguousarray(silu_in)
        maps.append(m)
    return maps


def kernel(**inputs):
    NB = 2
    if "nc" not in _CACHE:
        _CACHE["nc"] = build(NB=NB)[0]
    nc = _CACHE["nc"]
    maps = host_prep(inputs, NB=NB)
    res = run_bass_kernel_spmd(nc, maps, core_ids=list(range(len(maps))))
    outs = [r["outT"] for r in res.results]
    full = np.concatenate(outs, 0)
    return np.ascontiguousarray(full.transpose(0, 2, 1)).astype(np.float32)
```

```python
import numpy as np
import concourse.bass as bass
import concourse.mybir as mybir

F32 = mybir.dt.float32
BF16 = mybir.dt.bfloat16
AF = mybir.ActivationFunctionType
ALU = mybir.AluOpType
AX = mybir.AxisListType

NS = 4
ND = 24


class Instr:
    __slots__ = ("eng", "idx", "fn", "dma", "deps", "signal", "clock", "dma_id", "sidx")

    def __init__(self, eng, idx, fn, dma):
        self.eng = eng
        self.idx = idx
        self.fn = fn
        self.dma = dma
        self.deps = []
        self.signal = False
        self.clock = None
        self.dma_id = None
        self.sidx = None


def ap_box(ap):
    t = ap.tensor
    dims = list(ap.ap)
    pstride, pn = dims[0]
    shape = list(t.shape)
    per_part = 1
    for s in shape[1:]:
        per_part *= int(s)
    off = int(ap.offset)
    if pstride == 0:
        pstride_eff = per_part
    else:
        pstride_eff = pstride
    p0 = off // per_part
    f0 = off - p0 * per_part
    lo = 0
    hi = 0
    for st, cn in dims[1:]:
        ext = (int(cn) - 1) * int(st)
        if ext >= 0:
            hi += ext
        else:
            lo += ext
    esz = mybir.dt.size(t.dtype) if hasattr(mybir.dt, "size") else None
    return (t.name, p0, p0 + int(pn), f0 + lo, f0 + hi + 1)


class Prog:
    def __init__(self, nc):
        self.nc = nc
        self.engs = {"pe": nc.tensor, "dve": nc.vector, "act": nc.scalar, "pool": nc.gpsimd, "sp": nc.sync}
        self.ins = {e: [] for e in self.engs}
        self.track = {}
        self.known = {e: {} for e in self.engs}
        self.ndma = 0
        self.dmas = []
        self.esize = {}
        self.skip_dram = set()

    def _box(self, ap):
        t = ap.tensor
        sp = str(type(t).__name__)
        if sp.startswith("DRam"):
            if t.name in self.skip_dram:
                return None
            off = int(ap.offset); lo = 0; hi = 0
            for st, cn in ap.ap:
                ext = (int(cn) - 1) * int(st)
                if ext >= 0: hi += ext
                else: lo += ext
            es = {F32: 4, BF16: 2}.get(t.dtype, 4)
            return (t.name, 0, 1, (off + lo) * es, (off + hi + 1) * es)
        name, p0, p1, f0, f1 = ap_box(ap)
        if sp.startswith("PSum"):
            return (name, 0, 128, 0, 2048)
        es = {F32: 4, BF16: 2}.get(t.dtype, None)
        if es is None:
            es = int(ap.nbytes // max(1, ap.size)) if hasattr(ap, "nbytes") else 4
        return (name, p0, p1, f0 * es, f1 * es)

    def _collect(self, ins, box, write):
        name, p0, p1, f0, f1 = box
        BIN = 4096 if not name.startswith("dr_") else (1 << 22)
        bins = range(f0 // BIN, (f1 - 1) // BIN + 1)
        tr = self.track.setdefault(name, {})
        deps = []
        seen = set()
        newrec = (ins, write, p0, p1, f0, f1)
        for bi in bins:
            recs = tr.get(bi)
            if recs is None:
                tr[bi] = [newrec]
                continue
            keep = self._scan(recs, ins, write, p0, p1, f0, f1, deps, seen, bi * BIN, (bi + 1) * BIN,
                              name_is_psum=name.startswith("ps"))
            keep.append(newrec)
            tr[bi] = keep
        return deps

    def _scan(self, recs, ins, write, p0, p1, f0, f1, deps, seen, b0, b1, name_is_psum=False):
        keep = []
        for r in recs:
            rin, rw, q0, q1, g0, g1 = r
            ov = not (q1 <= p0 or p1 <= q0 or g1 <= f0 or f1 <= g0)
            if not ov:
                keep.append(r)
                continue
            if (write or rw or (name_is_psum and rin.eng != ins.eng)) and id(rin) not in seen:
                seen.add(id(rin))
                deps.append((rin, rw))
            covered = write and p0 <= q0 and q1 <= p1 and f0 <= max(g0, b0) and min(g1, b1) <= f1
            if covered:
                continue
            if (not write) and (not rw) and rin.eng == ins.eng and not ins.dma and not rin.dma \
                    and q0 == p0 and q1 == p1 and g0 == f0 and g1 == f1:
                continue
            keep.append(r)
        return keep

    def add(self, eng, fn, reads=(), writes=(), dma=False):
        ins = Instr(eng, len(self.ins[eng]), fn, dma)
        alld = {}
        for ap in reads:
            b = self._box(ap)
            if b is None:
                continue
            for d, dw in self._collect(ins, b, False):
                alld[id(d)] = (d, True)
        for ap in writes:
            b = self._box(ap)
            if b is None:
                continue
            for d, dw in self._collect(ins, b, True):
                prev = alld.get(id(d))
                alld[id(d)] = (d, (prev[1] if prev else False) or False)
        known = self.known[eng]
        if dma:
            ins.dma_id = self.ndma
            self.ndma += 1
            self.dmas.append(ins)
            if ins.dma_id >= ND:
                d = self.dmas[ins.dma_id - ND]
                alld[id(d)] = (d, True)
        need = []
        for d, raw in sorted(alld.values(), key=lambda x: -(x[0].dma_id if x[0].dma else x[0].idx)):
            if d is ins:
                continue
            if d.dma:
                key = ("dma", d.dma_id % ND)
                val = d.dma_id // ND + 1
                if known.get(key, 0) >= val:
                    continue
                need.append(d)
            else:
                if d.eng == eng and not dma:
                    if eng == "pe":
                        continue
                    if not raw:
                        continue
                if known.get(d.eng, -1) >= d.idx:
                    continue
                need.append(d)
        for d in need:
            d.signal = True
            if d.dma:
                key = ("dma", d.dma_id % ND)
                known[key] = max(known.get(key, 0), d.dma_id // ND + 1)
            else:
                known[d.eng] = max(known.get(d.eng, -1), d.idx)
            if d.clock:
                for k, v in d.clock.items():
                    if known.get(k, -1) < v:
                        known[k] = v
        ins.deps = need
        ins.clock = dict(known)
        if not dma:
            ins.clock[eng] = max(ins.clock.get(eng, -1), ins.idx - 1)
        self.ins[eng].append(ins)
        return ins

    def mm(self, out, lhsT, rhs, start=True, stop=True, **kw):
        return self.add("pe", lambda e: e.matmul(out, lhsT, rhs, start=start, stop=stop, **kw),
                        reads=[lhsT, rhs], writes=[out])

    def transpose(self, out, in_, ident):
        return self.add("pe", lambda e: e.transpose(out, in_, ident), reads=[in_, ident], writes=[out])

    def act(self, out, in_, func, bias=None, scale=None, accum_out=None, eng="act"):
        kw = {}
        reads = [in_]
        if bias is not None:
            kw["bias"] = bias
            if not isinstance(bias, (int, float)):
                reads.append(bias)
        if scale is not None:
            kw["scale"] = scale
            if not isinstance(scale, (int, float)):
                reads.append(scale)
        writes = [out]
        if accum_out is not None:
            kw["accum_out"] = accum_out
            writes.append(accum_out)
        return self.add(eng, lambda e: e.activation(out, in_, func, **kw), reads=reads, writes=writes)

    def tt(self, out, in0, in1, op, eng="dve"):
        return self.add(eng, lambda e: e.tensor_tensor(out, in0, in1, op), reads=[in0, in1], writes=[out])

    def ts(self, out, in0, s1, s2, op0, op1=None, eng="dve", accum_out=None):
        reads = [in0]
        for s in (s1, s2):
            if s is not None and not isinstance(s, (int, float)):
                reads.append(s)
        writes = [out]
        kw = {}
        if accum_out is not None:
            kw["accum_out"] = accum_out
            writes.append(accum_out)
        if op1 is None:
            return self.add(eng, lambda e: e.tensor_scalar(out, in0, s1, None, op0, **kw), reads=reads, writes=writes)
        return self.add(eng, lambda e: e.tensor_scalar(out, in0, s1, s2, op0, op1, **kw), reads=reads, writes=writes)

    def stt(self, out, in0, scalar, in1, op0, op1, eng="dve"):
        reads = [in0, in1]
        if not isinstance(scalar, (int, float)):
            reads.append(scalar)
        return self.add(eng, lambda e: e.scalar_tensor_tensor(out, in0, scalar, in1, op0, op1),
                        reads=reads, writes=[out])

    def copy(self, out, in_, eng="dve"):
        if eng == "act":
            return self.add(eng, lambda e: e.copy(out, in_), reads=[in_], writes=[out])
        return self.add(eng, lambda e: e.tensor_copy(out, in_), reads=[in_], writes=[out])

    def reduce(self, out, in_, op, axis=AX.X, eng="dve"):
        return self.add(eng, lambda e: e.tensor_reduce(out, in_, axis, op), reads=[in_], writes=[out])

    def scan(self, out, d0, d1, initial, op0, op1):
        reads = [d0, d1]
        if not isinstance(initial, (int, float)):
            reads.append(initial)
        return self.add("dve", lambda e: e.tensor_tensor_scan(out, d0, d1, initial, op0, op1),
                        reads=reads, writes=[out])

    def recip(self, out, in_):
        return self.add("dve", lambda e: e.reciprocal(out, in_), reads=[in_], writes=[out])

    def memset(self, ap, val, eng="dve"):
        return self.add(eng, lambda e: e.memset(ap, val), reads=[], writes=[ap])

    def dma(self, out, in_, q="sp", **kw):
        return self.add(q, lambda e: e.dma_start(out=out, in_=in_, **kw), reads=[in_], writes=[out], dma=True)

    def emit(self, final_wait_dmas=()):
        nc = self.nc
        dsem = [nc.alloc_semaphore(name=f"d_{i}") for i in range(ND)]
        nsig = {}
        for e, lst in self.ins.items():
            c = 0
            for ins in lst:
                if ins.dma:
                    continue
                if ins.signal:
                    ins.sidx = c
                    c += 1
            nsig[e] = c
        nsr = {e: max(1, -(-nsig[e] // 1800)) for e in self.engs}
        self.nsr = nsr
        rings = {e: [nc.alloc_semaphore(name=f"r_{e}_{i}") for i in range(nsr[e])] for e in self.engs}
        prog = self

        def body(ename):
            def f(engine):
                for ins in prog.ins[ename]:
                    for d in ins.deps:
                        if d.dma:
                            engine.wait_ge(dsem[d.dma_id % ND], 16 * (d.dma_id // ND + 1))
                        else:
                            engine.wait_ge(rings[d.eng][d.sidx % nsr[d.eng]], d.sidx // nsr[d.eng] + 1)
                    r = ins.fn(engine)
                    if ins.dma:
                        r.then_inc(dsem[ins.dma_id % ND], 16)
                    elif ins.signal:
                        r.then_inc(rings[ename][ins.sidx % nsr[ename]], 1)
                if ename == "sp":
                    for d in final_wait_dmas:
                        engine.wait_ge(dsem[d.dma_id % ND], 16 * (d.dma_id // ND + 1))
            return f

        with nc.Block() as block:
            block.sync(body("sp"))
            block.tensor(body("pe"))
            block.vector(body("dve"))
            block.scalar(body("act"))
            block.gpsimd(body("pool"))

from concourse.bass_utils import run_bass_kernel_spmd
import math

D = 1024
T = 2048
LC = 256
TT = T + LC
NCH = TT // 128
DEPTH = 2
DFF = 2816
EPS = 1e-6
NEG = -1.0e30
SLOT_EL = 8192

TBLOCKS = [(0, 256, True), (256, 768, False), (768, 1280, False), (1280, 1792, False), (1792, 2304, False)]


def _piece_list():
    pcs = [("A1", 8 * 768), ("A2", 8 * 640), ("SM1", 8 * 16)]
    for h in range(4):
        pcs.append((f"MH{h}", 8 * 512))
    pcs.append(("SM2", 8 * 16))
    for g in range(2):
        pcs.append((f"SG{g}", 8 * 768))
    for i in range(8):
        pcs.append((f"MG{i}", 8 * 384 + 3 * 4 * 128))
    pcs.append(("WO", 8 * 1024))
    for q in range(11):
        pcs.append((f"FF{q}", 8 * 512 + 2 * 1024))
    return pcs


def _k8(w, cols):
    sub = w[:, cols]
    K, n = sub.shape
    return np.ascontiguousarray(sub.reshape(K // 128, 128, n).transpose(1, 0, 2)).reshape(128, -1)


def _host_layer_weights(l, w_in, w_att_out, w_ml_out, w_ssd_out, w_o, w_up, w_down):
    wi = w_in[l]
    ar = np.arange
    qcols = np.concatenate([np.concatenate([c * 64 + ar(64), (4 + c) * 64 + ar(64)]) for c in range(4)])
    perm64 = np.concatenate([16 + ar(16), ar(16), 48 + ar(16), 32 + ar(16)])
    qperm = (qcols // 64) * 64 + perm64[qcols % 64]
    kcols = 512 + ar(128)
    kperm = 512 + (ar(128) // 64) * 64 + perm64[ar(128) % 64]
    vcols = 640 + ar(128)
    out = []
    out.append(_k8(wi, np.concatenate([qcols, kcols, vcols])))
    out.append(_k8(wi, np.concatenate([qperm, kperm])))
    g0 = 2816
    out.append(_k8(wi, np.concatenate([g0 + ar(4), g0 + 8 + ar(4), g0 + 4 + ar(4), g0 + 12 + ar(4)])))
    for h in range(4):
        out.append(_k8(wi, np.concatenate([768 + h * 128 + ar(128), 1280 + h * 128 + ar(128),
                                           1792 + h * 128 + ar(128), 2304 + h * 128 + ar(128)])))
    z0, x0, d0, m0 = 2832, 3344, 4368, 4384
    out.append(_k8(wi, d0 + ar(16)))
    for g in range(2):
        out.append(_k8(wi, np.concatenate([x0 + g * 256 + ar(256), x0 + 512 + g * 128 + ar(128),
                                           x0 + 768 + g * 128 + ar(128), z0 + g * 256 + ar(256)])))
    arow = np.concatenate([np.concatenate([c * 64 + ar(64), (4 + c) * 64 + ar(64)]) for c in range(4)])
    for i in range(8):
        cc = i * 128 + ar(128)
        a = _k8(wi, np.concatenate([m0 + cc, m0 + 1024 + cc, m0 + 2048 + cc]))
        b0 = _k8(w_att_out[l][arow], cc)
        b1 = _k8(w_ml_out[l], cc)
        b2 = _k8(w_ssd_out[l], cc)
        out.append(np.concatenate([a, b0, b1, b2], axis=1))
    out.append(_k8(w_o[l], ar(1024)))
    for q in range(11):
        j0 = 2 * q * 128
        a = _k8(w_up[l], np.concatenate([j0 + ar(256), DFF + j0 + ar(256)]))
        b = _k8(w_down[l][j0:j0 + 256], ar(1024))
        out.append(np.concatenate([a, b], axis=1))
    cat = np.concatenate(out, axis=1)
    return np.ascontiguousarray(cat, dtype=np.float32)


def _host_consts():
    ar = np.arange
    t = ar(T)
    inv = (10000.0 ** (-ar(16, dtype=np.float32) / 16)).astype(np.float32)
    row = (t // 64).astype(np.float32)
    col = (t % 64).astype(np.float32)
    ang_r = row[None, :] * inv[:, None]
    ang_c = col[None, :] * inv[:, None]
    cos64 = np.concatenate([np.cos(ang_r), np.cos(ang_r), np.cos(ang_c), np.cos(ang_c)], 0)
    sin64 = np.concatenate([-np.sin(ang_r), np.sin(ang_r), -np.sin(ang_c), np.sin(ang_c)], 0)
    cosT = np.concatenate([cos64, cos64], 0).astype(np.float32)
    sinT = np.concatenate([sin64, sin64], 0).astype(np.float32)
    q = ar(128)[:, None]
    k = ar(128)[None, :]
    maskA = np.zeros((128, 384), np.float32)
    maskA[:, 0:128] = np.where(k >= q, 0.0, -1.0e5)
    maskA[:, 256:384] = np.where(k <= q, 0.0, -1.0e5)
    s = ar(128)[:, None]
    tt = ar(128)[None, :]
    maskF = np.where(s <= tt, 0.0, NEG).astype(np.float32)
    maskB = np.where(s >= tt, 0.0, NEG).astype(np.float32)
    sel = np.zeros((8, 8, 128), np.float32)
    for r in range(8):
        sel[r, r, :] = 1.0
    ident = np.eye(128, dtype=np.float32)
    return dict(cosT=cosT, sinT=sinT, maskA=maskA, maskF=maskF, maskB=maskB, sel=sel.reshape(8, 1024), ident=ident)


class Arena:
    def __init__(self, nc, nbytes):
        self.t = nc.alloc_sbuf_tensor("arena", [128, nbytes // 2], BF16)
        self.top = 0
        self.cap = nbytes
        self.peak = 0

    def alloc(self, free, dtype, parts=128):
        if isinstance(free, int):
            free = (free,)
        n = 1
        for f in free:
            n *= f
        es = 4 if dtype == F32 else 2
        off = (self.top + 63) // 64 * 64
        self.top = off + n * es
        self.peak = max(self.peak, self.top)
        assert self.top <= self.cap, f"arena overflow {self.top} > {self.cap}"
        v = self.t[0:parts, off // 2:(off + n * es) // 2]
        if dtype == F32:
            v = v.bitcast(F32)
        if len(free) == 2:
            v = v.rearrange("p (a b) -> p a b", a=free[0])
        elif len(free) == 3:
            v = v.rearrange("p (a b c) -> p a b c", a=free[0], b=free[1])
        return v

    def mark(self):
        return self.top

    def release(self, m):
        self.top = m


def build(NB=2, L=DEPTH, dbg=False, stop_after=None):
    nc = bass.Bass("TRN2", target_bir_lowering=False)
    P = Prog(nc)

    def din(name, shape, dt=F32):
        P.skip_dram.add(name)
        return nc.dram_tensor(name, list(shape), dt, kind="ExternalInput").ap()

    def dscr(name, shape, dt):
        return nc.dram_tensor(name, list(shape), dt, kind="ExternalOutput" if dbg else "Internal").ap()

    pcs = _piece_list()
    poff = {}
    o = 0
    for nm, ne in pcs:
        poff[nm] = (o, ne)
        o += ne
    WTOT = o

    xT_d = din("xT", [NB, D, T])
    ctxT_d = din("ctxT", [NB, D, LC])
    cT_d = din("cT", [128, 24])
    wmod_d = din("wmod", [L, 128, 8 * 6144])
    bmod_d = din("bmod", [L, 128, 48])
    gn_d = din("gn", [128, (2 * L + 1) * 8])
    wl_d = [din(f"wl{l}", [128, WTOT]) for l in range(L)]
    cos_d = din("cosT", [128, T]); sin_d = din("sinT", [128, T])
    maskA_d = din("maskA", [128, 384]); maskF_d = din("maskF", [128, 128]); maskB_d = din("maskB", [128, 128])
    sel_d = din("sel", [8, 1024]); ident_d = din("ident", [128, 128])
    sink_d = din("sinkb", [128, L * 8])
    mlb_d = din("mlb", [8, L * 2])
    mlg_d = din("mlgain", [128, L * 512])
    cw_d = din("convw", [128, L * 8 * 6])
    sdt_d = din("ssddt", [8, L * 4])
    sdk_d = din("ssdskip", [128, L * 512])
    sng_d = din("ssdng", [128, L * 512])
    outT_d = nc.dram_tensor("outT", [NB, D, T], F32, kind="ExternalOutput").ap()

    brs_d = [dscr(f"dr_br{i}", [512, TT], BF16) for i in range(3)]
    yt_d = dscr("dr_yt", [D, TT], BF16)
    xs_d = dscr("dr_xs", [D, TT], F32)
    hdbg_d = dscr("dr_h", [D, TT], BF16) if dbg else None

    AR = Arena(nc, 204 * 1024)
    banks = [nc.alloc_psum_tensor(f"ps{i}", [128, 512], F32) for i in range(8)]
    st = {"rr": 0, "acc": 0, "kq": 0}

    def rr():
        b = banks[st["rr"] % 3]
        st["rr"] += 1
        return b

    def kqb():
        b = banks[3 + st["kq"] % 2]
        st["kq"] += 1
        return b

    def accb():
        b = banks[5 + st["acc"] % 3]
        st["acc"] += 1
        return b

    IDF = AR.alloc(128, F32); P.dma(IDF, ident_d)
    IDB = AR.alloc(128, BF16); P.dma(IDB, ident_d, q="pool")
    ONESB = AR.alloc(128, BF16); P.memset(ONESB, 1.0)
    COS = AR.alloc(T, BF16); P.dma(COS.rearrange("p (a b) -> p a b", b=512), cos_d.rearrange("p (a b) -> p a b", b=512), q="pool")
    SIN = AR.alloc(T, BF16); P.dma(SIN.rearrange("p (a b) -> p a b", b=512), sin_d.rearrange("p (a b) -> p a b", b=512), q="pool")
    MASKA = AR.alloc(384, F32); P.dma(MASKA, maskA_d)
    MASKF = AR.alloc(128, F32); P.dma(MASKF, maskF_d)
    MASKB = AR.alloc(128, F32); P.dma(MASKB, maskB_d)
    SEL = AR.alloc((8, 128), F32, parts=8); P.dma(SEL, sel_d.rearrange("p (a b) -> p a b", a=8))
    SINKB = AR.alloc(L * 8, F32); P.dma(SINKB, sink_d)
    MLB = AR.alloc(L * 2, F32, parts=8); P.dma(MLB, mlb_d)
    NMLB = AR.alloc(L * 2, F32, parts=8); P.ts(NMLB, MLB, -1.0, None, ALU.mult)
    MLG = AR.alloc(L * 512, F32); P.dma(MLG, mlg_d)
    CW = AR.alloc(L * 48, F32); P.dma(CW, cw_d)
    SDT = AR.alloc(L * 4, F32, parts=8); P.dma(SDT, sdt_d)
    SA = AR.alloc(L * 4, F32, parts=8)
    P.act(SA, SDT, AF.Exp)
    P.ts(SA, SA, -1.0, None, ALU.mult)
    SDK = AR.alloc(L * 512, BF16); P.dma(SDK, sdk_d, q="pool")
    SNG = AR.alloc(L * 512, F32); P.dma(SNG, sng_d)
    GN = AR.alloc((2 * L + 1) * 8, F32); P.dma(GN, gn_d)
    EPST = AR.alloc(1, F32); P.memset(EPST, EPS)
    ONEF = AR.alloc(TT, F32, parts=8); P.memset(ONEF, 1.0)
    MOD = [AR.alloc((48, 3), F32) for _ in range(L)]
    GS = [AR.alloc((2, 8, 3), F32) for _ in range(L)]
    H = AR.alloc((8, TT), BF16)
    SLOTS = [AR.alloc(SLOT_EL, BF16) for _ in range(2)]

    wq = []
    for b in range(NB):
        for l in range(L):
            for nm, ne in pcs:
                wq.append((l, nm))
    wst = {"issued": 0, "cur": -1}

    def issue_next():
        i = wst["issued"]
        if i >= len(wq):
            return
        l, nm = wq[i]
        o, ne = poff[nm]
        CH = 512 if ne % 512 == 0 else 128
        P.dma(SLOTS[i % 2][:, 0:ne].rearrange("p (a b) -> p a b", b=CH),
              wl_d[l][:, o:o + ne].rearrange("p (a b) -> p a b", b=CH), q="pool")
        wst["issued"] += 1

    def next_piece(l, nm):
        wst["cur"] += 1
        i = wst["cur"]
        assert wq[i] == (l, nm), (wq[i], l, nm)
        while wst["issued"] <= i:
            issue_next()
        return SLOTS[i % 2]

    def prefetch():
        if wst["issued"] <= wst["cur"] + 1:
            issue_next()

    def wview(slot, off, kc, n):
        return slot[:, off:off + kc * n].rearrange("p (a b) -> p a b", a=kc)

    CT = AR.alloc(24, F32); P.dma(CT, cT_d)
    SC = AR.alloc(24, F32); P.act(SC, CT, AF.Silu)
    BM = AR.alloc(L * 48, F32)
    for l in range(L):
        P.dma(BM[:, l * 48:(l + 1) * 48], bmod_d[l])
    for l in range(L):
        for pc in range(12):
            WM = SLOTS[pc % 2].bitcast(F32).rearrange("p (a b) -> p a b", a=8)
            P.dma(WM, wmod_d[l].rearrange("p (a b) -> p a b", a=8)[:, :, pc * 512:(pc + 1) * 512])
            for cc in range(4):
                ps = rr()
                for kc in range(8):
                    P.mm(ps[:, 0:3], WM[:, kc, cc * 128:(cc + 1) * 128], SC[:, kc * 3:(kc + 1) * 3],
                         start=(kc == 0), stop=(kc == 7))
                col = pc * 4 + cc
                P.ts(MOD[l][:, col, :], ps[:, 0:3], BM[:, l * 48 + col:l * 48 + col + 1], None, ALU.add)
        for n in range(2):
            for kc in range(8):
                P.ts(GS[l][:, n, kc, :], MOD[l][:, (1 + 3 * n) * 8 + kc, :], 1.0,
                     GN[:, (2 * l + n) * 8 + kc:(2 * l + n) * 8 + kc + 1], ALU.add, ALU.mult)

    def modv(l, which, kc, who):
        return MOD[l][:, which * 8 + kc, who:who + 1]

    def norm_to(Xv, l, n, b, dst, gvec=None, blocks=TBLOCKS, dst_f32=False, toff=0):
        for (t0, t1, isctx) in blocks:
            who = 2 if isctx else b
            nt = t1 - t0
            m = AR.mark()
            SQ = AR.alloc((8, nt), BF16)
            for kc in range(8):
                P.act(SQ[:, kc, :], Xv[:, kc, t0:t1], AF.Square)
            ps = rr()
            for kc in range(8):
                P.mm(ps[:, 0:nt], ONESB, SQ[:, kc, :], start=(kc == 0), stop=(kc == 7))
            RS = AR.alloc(nt, F32)
            P.act(RS, ps[:, 0:nt], AF.Sqrt, bias=EPST, scale=1.0 / D)
            P.recip(RS, RS)
            TMP = AR.alloc((2, nt), F32)
            for kc in range(8):
                tv = TMP[:, kc % 2, :]
                P.tt(tv, Xv[:, kc, t0:t1], RS, ALU.mult)
                if gvec is None:
                    P.act(dst[:, kc, t0 - toff:t1 - toff], tv, AF.Identity, scale=GS[l][:, n, kc, who:who + 1],
                          bias=modv(l, 3 * n, kc, who))
                else:
                    P.act(dst[:, kc, t0 - toff:t1 - toff], tv, AF.Identity, scale=gvec[:, kc:kc + 1])
            AR.release(m)

    def proj_fm(W, c0, ncols, t0, t1):
        ps = rr()
        for kc in range(8):
            P.mm(ps[0:ncols, 0:t1 - t0], W[:, kc, c0:c0 + ncols], H[:, kc, t0:t1], start=(kc == 0), stop=(kc == 7))
        return ps[0:ncols, 0:t1 - t0]

    def proj_tm(W, c0, ncols, c):
        ps = rr()
        for kc in range(8):
            P.mm(ps[:, 0:ncols], H[:, kc, c * 128:(c + 1) * 128], W[:, kc, c0:c0 + ncols],
                 start=(kc == 0), stop=(kc == 7))
        return ps[:, 0:ncols]

    def rows_scan(dst, src, op0, op1, d0, reverse):
        if not reverse:
            P.scan(dst[:, 0:LC], d0[:, 0:LC], src[:, 0:LC], 0.0, op0, op1)
            P.scan(dst[:, LC:TT], d0[:, LC:TT], src[:, LC:TT], dst[:, LC - 1:LC], op0, op1)
        else:
            P.scan(dst[:, 0:LC][:, ::-1], d0[:, 0:LC][:, ::-1], src[:, 0:LC][:, ::-1], 0.0, op0, op1)
            P.scan(dst[:, LC:TT][:, ::-1], d0[:, LC:TT][:, ::-1], src[:, LC:TT][:, ::-1], dst[:, 0:1], op0, op1)

    def rows_to_cols(COLS, ci, R):
        for c0 in range(0, NCH, 6):
            ps = rr()
            for c in range(c0, c0 + 6):
                P.transpose(ps[:, (c - c0) * 8:(c - c0) * 8 + 8], R[:, c * 128:(c + 1) * 128], IDF[0:8, 0:8])
            P.copy(COLS[:, c0:c0 + 6, ci, :], ps[:, 0:48].rearrange("p (a b) -> p a b", a=6))

    def contribs(t):
        out = []
        for s in range(0, t):
            out.append((s, 0, False))
        out.append((t, 0, True))
        if t >= 2:
            for s in (0, 1):
                out.append((s, 1, False))
            for s in range(t + 1, NCH):
                out.append((s, 1, False))
        else:
            for s in range(t + 1, 2):
                out.append((s, 1, False))
        out.append((t, 1, True))
        return out

    MASKD = [MASKF, MASKB]

    def build_bcast(rowtile, r, dst):
        for bi, (t0, t1, isctx) in enumerate(TBLOCKS):
            ps = rr()
            P.mm(ps[:, 0:t1 - t0], SEL[:, r, :], rowtile[:, t0:t1])
            P.copy(dst[:, t0:t1], ps[:, 0:t1 - t0], eng=("act" if bi % 2 == 0 else "dve"))

    def run_pipelined(items, stage1, stage2, fin):
        cidx = [k for k, it in enumerate(items) if it[0] == "c"]
        nxt = {cidx[j]: (cidx[j + 1] if j + 1 < len(cidx) else None) for j in range(len(cidx))}
        hnd = {}
        if cidx:
            hnd[cidx[0]] = stage1(items[cidx[0]])
        for k, it in enumerate(items):
            if it[0] == "c":
                n = nxt[k]
                if n is not None:
                    hnd[n] = stage1(items[n])
                stage2(it, hnd.pop(k))
            else:
                fin(it)

    def decay_T(rowtile, r, t, s, masked, d, biascol):
        ps = rr()
        if masked:
            P.mm(ps[:, 0:128], IDF, MASKD[d], start=True, stop=False)
        P.mm(ps[:, 0:128], SEL[:, r, :], rowtile[:, t * 128:(t + 1) * 128], start=(not masked), stop=True)
        return ps

    for b in range(NB):
        for l in range(L):
            last = (l == L - 1)
            need_ctx = not last
            blocks = TBLOCKS if need_ctx else TBLOCKS[1:]
            m_layer = AR.mark()
            if l == 0:
                X = AR.alloc((8, TT), F32)
                P.dma(X[:, :, 0:LC], ctxT_d[b].rearrange("(a p) t -> p a t", p=128))
                P.dma(X[:, :, LC:TT], xT_d[b].rearrange("(a p) t -> p a t", p=128))
            norm_to(X, l, 0, b, H)
            if l > 0:
                P.dma(xs_d.rearrange("(a p) t -> p a t", p=128), X)
                AR.release(m_x)
                m_layer = m_x
            if dbg and l == 0 and b == 0:
                P.dma(hdbg_d.rearrange("(a p) t -> p a t", p=128), H)
            AR.release(m_layer)
            if stop_after == "norm1":
                break

            m0 = AR.mark()
            W1 = next_piece(l, "A1"); prefetch()
            W1v = wview(W1, 0, 8, 768)
            QT = AR.alloc((4, TT), BF16)
            KT = AR.alloc(TT, BF16)
            VP = [AR.alloc((NCH, 128), BF16) for _ in range(2)]
            P.memset(VP[0], 0.0); P.memset(VP[1], 0.0)
            W2 = next_piece(l, "A2")
            W2v = wview(W2, 0, 8, 640)
            if stop_after == "attv":
                for c in range(NCH):
                    ps = proj_tm(W1v, 640, 128, c)
                    P.copy(VP[0][:, c, 0:64], ps[:, 0:64], eng="act")
                    P.copy(VP[1][:, c, 64:128], ps[:, 64:128], eng="act")
                P.dma(brs_d[0][0:128, :], VP[0].rearrange("p a b -> p (a b)"))
                break
            if stop_after == "attr":
                ps = proj_fm(W1v, 0, 128, 256, 768)
                ps2 = proj_fm(W2v, 0, 128, 256, 768)
                T1 = AR.alloc(512, F32); T2 = AR.alloc(512, F32)
                P.tt(T1, ps, COS[:, 0:512], ALU.mult)
                P.tt(T2, ps2, SIN[:, 0:512], ALU.mult)
                P.tt(QT[:, 0, 256:768], T1, T2, ALU.add)
                ps = proj_fm(W1v, 0, 128, 0, 256)
                P.copy(QT[:, 0, 0:256], ps, eng="act")
                P.dma(brs_d[0].rearrange("(a p) t -> p a t", p=128), QT)
                break
            if stop_after == "attw":
                ps = proj_fm(W1v, 0, 128, 256, 768)
                P.copy(QT[:, 0, 256:768], ps, eng="act")
                ps = proj_tm(W1v, 640, 128, 3)
                P.copy(VP[0][:, 3, 0:64], ps[:, 0:64], eng="act")
                P.dma(brs_d[0].rearrange("(a p) t -> p a t", p=128), QT)
                break
            for (t0, t1, isctx) in TBLOCKS:
                nt = t1 - t0
                for c in range(5):
                    if isctx and c < 4 and not need_ctx:
                        continue
                    dst = QT[:, c, t0:t1] if c < 4 else KT[:, t0:t1]
                    ps = proj_fm(W1v, c * 128, 128, t0, t1)
                    if isctx:
                        P.copy(dst, ps, eng="act")
                    else:
                        ps2 = proj_fm(W2v, c * 128, 128, t0, t1)
                        m = AR.mark()
                        T1 = AR.alloc(nt, F32); T2 = AR.alloc(nt, F32)
                        P.tt(T1, ps, COS[:, t0 - LC:t1 - LC], ALU.mult)
                        P.tt(T2, ps2, SIN[:, t0 - LC:t1 - LC], ALU.mult)
                        P.tt(dst, T1, T2, ALU.add)
                        AR.release(m)
            for c in range(NCH):
                ps = proj_tm(W1v, 640, 128, c)
                P.copy(VP[0][:, c, 0:64], ps[:, 0:64], eng="act")
                P.copy(VP[1][:, c, 64:128], ps[:, 64:128], eng="act")
            prefetch()
            ATs = AR.alloc((4, TT), BF16)
            if not need_ctx:
                P.memset(ATs[:, :, 0:LC], 0.0)
            if stop_after == "attproj":
                P.dma(brs_d[0].rearrange("(a p) t -> p a t", p=128), QT)
                break
            qblocks = list(range(0 if need_ctx else 2, NCH))
            if stop_after == "att1":
                qblocks = qblocks[:1]
            for qb in qblocks:
                lat = qb >= 2
                if lat:
                    loc = [k for k in (qb - 1, qb, qb + 1) if 2 <= k < NCH]
                else:
                    loc = []
                nl = len(loc) * 128
                keych = [0, 1] + loc
                mo = 0 if (not lat or qb - 1 >= 2) else 128
                for c in range(4):
                    acc = accb()
                    for half in range(2):
                        head = c + 4 * half
                        psl = slice(64 * half, 64 * half + 64)
                        lhsT = QT[psl, c, qb * 128:(qb + 1) * 128]
                        m = AR.mark()
                        SM = AR.alloc(640, F32)
                        psB = rr()
                        P.mm(psB[:, 0:256], lhsT, KT[psl, 0:256])
                        P.copy(SM[:, 0:256], psB[:, 0:256], eng="act")
                        if lat:
                            psA = rr()
                            P.mm(psA[:, 0:nl], lhsT, KT[psl, loc[0] * 128:(loc[-1] + 1) * 128])
                            P.tt(SM[:, 256:256 + nl], psA[:, 0:nl], MASKA[:, mo:mo + nl], ALU.add)
                        ntot = 256 + nl
                        sk = SINKB[:, l * 8 + head:l * 8 + head + 1]
                        ST4 = AR.alloc(8, F32)
                        P.reduce(ST4[:, 0:1], SM[:, 0:ntot], ALU.max)
                        P.ts(ST4[:, 1:2], ST4[:, 0:1], 0.125, sk, ALU.mult, ALU.max)
                        P.ts(ST4[:, 2:3], ST4[:, 1:2], -1.0, None, ALU.mult)
                        PB = AR.alloc(640, BF16)
                        P.act(PB[:, 0:ntot], SM[:, 0:ntot], AF.Exp, bias=ST4[:, 2:3], scale=0.125, accum_out=ST4[:, 3:4])
                        P.act(ST4[:, 4:5], ST4[:, 2:3], AF.Exp, bias=sk)
                        P.tt(ST4[:, 5:6], ST4[:, 3:4], ST4[:, 4:5], ALU.add)
                        P.recip(ST4[:, 6:7], ST4[:, 5:6])
                        PN = AR.alloc(640, BF16)
                        P.ts(PN[:, 0:ntot], PB[:, 0:ntot], ST4[:, 6:7], None, ALU.mult)
                        psT = rr().bitcast(BF16)
                        nk = len(keych)
                        for j in range(nk):
                            P.transpose(psT[:, j * 128:(j + 1) * 128], PN[:, j * 128:(j + 1) * 128], IDB)
                        PTs = AR.alloc(640, BF16)
                        P.copy(PTs[:, 0:nk * 128], psT[:, 0:nk * 128], eng="act")
                        for j, kc_ in enumerate(keych):
                            P.mm(acc[:, 0:128], VP[half][:, kc_, :], PTs[:, j * 128:(j + 1) * 128],
                                 start=(half == 0 and j == 0), stop=(half == 1 and j == nk - 1))
                        AR.release(m)
                    P.copy(ATs[:, c, qb * 128:(qb + 1) * 128], acc[:, 0:128])
            P.dma(brs_d[0].rearrange("(a p) t -> p a t", p=128), ATs)
            AR.release(m0)
            if stop_after in ("att", "att1"):
                break

            m0 = AR.mark()
            WS = next_piece(l, "SM1"); prefetch()
            WSv = wview(WS, 0, 8, 16)
            NMR = [AR.alloc(TT, F32, parts=8) for _ in range(2)]
            COLS = AR.alloc((NCH, 4, 8), F32)
            MTs = AR.alloc((4, TT), BF16)
            m_rows = AR.mark()
            LI = AR.alloc(TT, F32, parts=8); LF = AR.alloc(TT, F32, parts=8)
            for (t0, t1, isctx) in TBLOCKS:
                ps = proj_fm(WSv, 0, 8, t0, t1)
                P.act(LI[:, t0:t1], ps, AF.Identity, bias=MLB[:, 2 * l:2 * l + 1])
                ps = proj_fm(WSv, 8, 8, t0, t1)
                P.act(LF[:, t0:t1], ps, AF.Exp, bias=NMLB[:, 2 * l + 1:2 * l + 2], scale=-1.0)
            P.act(LF, LF, AF.Ln, bias=1.0)
            P.ts(LF, LF, -1.0, None, ALU.mult)
            m1 = AR.mark()
            for d in range(2):
                T1 = AR.alloc(TT, F32, parts=8); T4 = AR.alloc(TT, F32, parts=8)
                T2 = NMR[d]
                rows_scan(T1, LF, ALU.mult, ALU.add, ONEF, d == 1)
                pass
                P.tt(T1, LI, T1, ALU.subtract)
                rows_scan(T2, T1, ALU.max, ALU.max, T1, d == 1)
                P.tt(T4, LI, T1, ALU.subtract)
                P.tt(T4, T4, T2, ALU.add)
                P.act(T4, T4, AF.Exp, scale=-1.0)
                rows_to_cols(COLS, d * 2 + 0, T1)
                rows_to_cols(COLS, d * 2 + 1, T4)
                P.ts(T2, T2, -1.0, None, ALU.mult)
                AR.release(m1)
            AR.release(m_rows)
            if stop_after == "mlrows":
                P.dma(brs_d[1][0:128, 0:NCH * 32], COLS.rearrange("p a b c -> p (a b c)").bitcast(BF16)[:, 0:NCH * 32])
                P.dma(brs_d[1][128:136, :], NMR[0].bitcast(BF16)[:, 0:TT])
                break
            if not need_ctx:
                P.memset(MTs[:, :, 0:LC], 0.0)
            for h in range(4):
                WH = next_piece(l, f"MH{h}"); prefetch()
                WHv = wview(WH, 0, 8, 512)
                m2 = AR.mark()
                QTh = AR.alloc(TT, BF16); KTh = AR.alloc(TT, BF16)
                VA = AR.alloc((NCH, 132), BF16); OS = AR.alloc((NCH, 128), BF16)
                P.memset(VA[:, :, 128:129], 1.0)
                for (t0, t1, isctx) in TBLOCKS:
                    ps = proj_fm(WHv, 0, 128, t0, t1)
                    P.copy(QTh[:, t0:t1], ps, eng="act")
                    ps = proj_fm(WHv, 128, 128, t0, t1)
                    P.ts(KTh[:, t0:t1], ps, 128.0 ** -0.5, None, ALU.mult)
                for c in range(NCH):
                    ps = proj_tm(WHv, 256, 256, c)
                    P.copy(VA[:, c, 0:128], ps[:, 0:128])
                    P.act(OS[:, c, :], ps[:, 128:256], AF.Sigmoid)
                ring = [(AR.alloc(128, F32), AR.alloc(128, BF16)) for _ in range(6)]
                mring = [AR.alloc(128, F32) for _ in range(2)]
                BC = [AR.alloc(TT, F32) for _ in range(2)]
                for d in range(2):
                    build_bcast(NMR[d], d * 4 + h, BC[d])
                mst = {"rc": 0, "accs": None, "mc": 0}
                items = []
                for t in (range(NCH) if need_ctx else range(2, NCH)):
                    cl = contribs(t)
                    cnt = [sum(1 for x in cl if x[1] == d) for d in range(2)]
                    seen = [0, 0]
                    for ci_, (s_, d, masked) in enumerate(cl):
                        items.append(("c", t, s_, d, masked, seen[d] == 0, seen[d] == cnt[d] - 1, ci_ == 0))
                        seen[d] += 1
                    items.append(("fin", t))

                def ml_s1(it):
                    _, t, s_, d, masked, first, lastc, newt = it
                    pk = kqb()
                    P.mm(pk[:, 0:128], KTh[:, s_ * 128:(s_ + 1) * 128], QTh[:, t * 128:(t + 1) * 128])
                    return pk, None

                def ml_s2(it, hh):
                    _, t, s_, d, masked, first, lastc, newt = it
                    pk, psD = hh
                    if newt:
                        mst["accs"] = [accb(), accb()]
                    r = d * 4 + h
                    DT_, ST_ = ring[mst["rc"] % 6]
                    mst["rc"] += 1
                    src = BC[d][:, t * 128:(t + 1) * 128]
                    if masked:
                        tm = mring[mst["mc"] % 2]
                        mst["mc"] += 1
                        P.tt(tm, src, MASKD[d], ALU.add)
                        src = tm
                    P.act(DT_, src, AF.Exp, bias=COLS[:, s_, d * 2, r:r + 1])
                    P.tt(ST_, pk[:, 0:128], DT_, ALU.mult)
                    P.mm(mst["accs"][d][:, 0:129], ST_, VA[:, s_, 0:129], start=first, stop=lastc)

                def ml_fin(it):
                    t = it[1]
                    accs = mst["accs"]
                    m3 = AR.mark()
                    HD = []
                    for d in range(2):
                        r = d * 4 + h
                        A_ = AR.alloc(132, F32)
                        P.copy(A_[:, 0:129], accs[d][:, 0:129], eng="act")
                        S4 = AR.alloc(4, F32)
                        P.stt(S4[:, 0:1], A_[:, 128:129], -1.0, A_[:, 128:129], ALU.mult, ALU.max)
                        P.tt(S4[:, 1:2], S4[:, 0:1], COLS[:, t, d * 2 + 1, r:r + 1], ALU.max)
                        P.recip(S4[:, 2:3], S4[:, 1:2])
                        Hd = AR.alloc(128, F32)
                        P.ts(Hd, A_[:, 0:128], S4[:, 2:3], None, ALU.mult)
                        HD.append(Hd)
                    HS = AR.alloc(128, F32)
                    P.tt(HS, HD[0], HD[1], ALU.add)
                    S4 = AR.alloc(4, F32)
                    JK = AR.alloc(128, F32)
                    P.act(JK, HS, AF.Square, accum_out=S4[:, 0:1])
                    P.act(S4[:, 1:2], S4[:, 0:1], AF.Sqrt, bias=EPST, scale=1.0 / 128)
                    P.recip(S4[:, 2:3], S4[:, 1:2])
                    O1 = AR.alloc(128, F32)
                    P.stt(O1, HS, S4[:, 2:3], MLG[:, l * 512 + h * 128:l * 512 + (h + 1) * 128], ALU.mult, ALU.mult)
                    O2 = AR.alloc(128, BF16)
                    P.tt(O2, O1, OS[:, t, :], ALU.mult)
                    pT = rr().bitcast(BF16)
                    P.transpose(pT[:, 0:128], O2, IDB)
                    P.copy(MTs[:, h, t * 128:(t + 1) * 128], pT[:, 0:128], eng="act")
                    AR.release(m3)

                run_pipelined(items, ml_s1, ml_s2, ml_fin)
                AR.release(m2)
                if stop_after == "mlh1":
                    break
            P.dma(brs_d[1].rearrange("(a p) t -> p a t", p=128), MTs)
            AR.release(m0)
            if stop_after in ("ml", "mlh1"):
                break

            m0 = AR.mark()
            WS = next_piece(l, "SM2"); prefetch()
            WSv = wview(WS, 0, 8, 16)
            ACS = [AR.alloc(TT, F32, parts=8) for _ in range(2)]
            COLS = AR.alloc((NCH, 4, 8), F32)
            m1 = AR.mark()
            for d in range(2):
                DTr = AR.alloc(TT, F32, parts=8); T1 = AR.alloc(TT, F32, parts=8)
                for (t0, t1, isctx) in TBLOCKS:
                    ps = proj_fm(WSv, 8 * d, 8, t0, t1)
                    P.act(DTr[:, t0:t1], ps, AF.Exp, bias=SDT[:, 4 * l + d:4 * l + d + 1])
                P.act(DTr, DTr, AF.Ln, bias=1.0)
                P.ts(T1, DTr, SA[:, 4 * l + 2 + d:4 * l + 3 + d], None, ALU.mult)
                rows_scan(ACS[d], T1, ALU.mult, ALU.add, ONEF, d == 1)
                P.ts(T1, ACS[d], -1.0, None, ALU.mult)
                rows_to_cols(COLS, d * 2 + 0, T1)
                rows_to_cols(COLS, d * 2 + 1, DTr)
                AR.release(m1)
            YG = AR.alloc((NCH, 512), BF16)
            SSQ = AR.alloc((NCH, 8), F32)
            trange = list(range(NCH) if need_ctx else range(2, NCH))
            for g in range(2):
                WG = next_piece(l, f"SG{g}"); prefetch()
                WGv = wview(WG, 0, 8, 768)
                m2 = AR.mark()
                XTOK = AR.alloc((NCH, 256), BF16)
                BTg = AR.alloc(TT, BF16); CTg = AR.alloc(TT, BF16)
                ZS = AR.alloc((NCH, 256), BF16)
                m3 = AR.mark()
                PAD = AR.alloc(2312, F32); ACC = AR.alloc(2308, F32); XF = AR.alloc(TT, BF16)
                P.memset(PAD[:, 0:2], 0.0); P.memset(PAD[:, 258:262], 0.0); P.memset(PAD[:, 2310:2312], 0.0)
                for ch in range(4):
                    cidx = [g * 2, g * 2 + 1, 4 + g, 6 + g][ch]
                    cw = CW[:, (l * 8 + cidx) * 6:(l * 8 + cidx) * 6 + 6]
                    for (t0, t1, isctx) in TBLOCKS:
                        ps = proj_fm(WGv, ch * 128, 128, t0, t1)
                        o0 = 2 + t0 if isctx else 262 + (t0 - LC)
                        P.copy(PAD[:, o0:o0 + (t1 - t0)], ps, eng="act")
                    P.ts(ACC, PAD[:, 0:2308], cw[:, 0:1], None, ALU.mult)
                    for j in range(1, 5):
                        P.stt(ACC, PAD[:, j:j + 2308], cw[:, j:j + 1], ACC, ALU.mult, ALU.add)
                    dst = XF if ch < 2 else (BTg if ch == 2 else CTg)
                    P.act(dst[:, 0:LC], ACC[:, 0:LC], AF.Silu, bias=cw[:, 5:6])
                    P.act(dst[:, LC:TT], ACC[:, 260:2308], AF.Silu, bias=cw[:, 5:6])
                    if ch < 2:
                        for c0 in range(0, NCH, 4):
                            cs = list(range(c0, min(c0 + 4, NCH)))
                            pT = rr().bitcast(BF16)
                            for c in cs:
                                P.transpose(pT[:, (c - c0) * 128:(c - c0 + 1) * 128], XF[:, c * 128:(c + 1) * 128], IDB)
                            P.copy(XTOK[:, c0:c0 + len(cs), ch * 128:(ch + 1) * 128],
                                   pT[:, 0:len(cs) * 128].rearrange("p (a b) -> p a b", a=len(cs)))
                AR.release(m3)
                BTOK = None
                for c in range(NCH):
                    ps = proj_tm(WGv, 512, 256, c)
                    P.act(ZS[:, c, :], ps, AF.Silu)
                ring = [(AR.alloc(128, F32), AR.alloc(128, BF16)) for _ in range(6)]
                mring = [AR.alloc(128, F32) for _ in range(2)]
                BC = [AR.alloc(TT, F32) for _ in range(2)]
                sst = {"rc": 0, "acc": None, "mc": 0}
                for r4 in range(4):
                    hd = g * 4 + r4
                    for d in range(2):
                        build_bcast(ACS[d], hd, BC[d])
                    items = []
                    for t in trange:
                        cl = contribs(t)
                        for ci_, (s_, d, masked) in enumerate(cl):
                            items.append(("c", t, s_, d, masked, ci_ == 0, ci_ == len(cl) - 1))
                        items.append(("fin", t))

                    def sd_s1(it, hd=hd):
                        _, t, s_, d, masked, first, lastc = it
                        pk = kqb()
                        P.mm(pk[:, 0:128], BTg[:, s_ * 128:(s_ + 1) * 128], CTg[:, t * 128:(t + 1) * 128])
                        return pk, None

                    def sd_s2(it, hh, hd=hd, r4=r4):
                        _, t, s_, d, masked, first, lastc = it
                        pk, psD = hh
                        if first:
                            sst["acc"] = accb()
                        DT_, MT_ = ring[sst["rc"] % 6]
                        sst["rc"] += 1
                        src = BC[d][:, t * 128:(t + 1) * 128]
                        if masked:
                            tm = mring[sst["mc"] % 2]
                            sst["mc"] += 1
                            P.tt(tm, src, MASKD[d], ALU.add)
                            src = tm
                        P.act(DT_, src, AF.Exp, bias=COLS[:, s_, d * 2, hd:hd + 1])
                        P.stt(MT_, pk[:, 0:128], COLS[:, s_, d * 2 + 1, hd:hd + 1], DT_, ALU.mult, ALU.mult)
                        P.mm(sst["acc"][:, 0:64], MT_, XTOK[:, s_, r4 * 64:(r4 + 1) * 64], start=first, stop=lastc)

                    def sd_fin(it, hd=hd, r4=r4):
                        t = it[1]
                        acc = sst["acc"]
                        m4 = AR.mark()
                        Y1 = AR.alloc(64, F32)
                        P.tt(Y1, XTOK[:, t, r4 * 64:(r4 + 1) * 64],
                             SDK[:, l * 512 + hd * 64:l * 512 + (hd + 1) * 64], ALU.mult)
                        P.tt(Y1, Y1, acc[:, 0:64], ALU.add)
                        P.tt(YG[:, t, hd * 64:(hd + 1) * 64], Y1, ZS[:, t, r4 * 64:(r4 + 1) * 64], ALU.mult)
                        JK = AR.alloc(64, F32)
                        P.act(JK, YG[:, t, hd * 64:(hd + 1) * 64], AF.Square, accum_out=SSQ[:, t, hd:hd + 1])
                        AR.release(m4)

                    run_pipelined(items, sd_s1, sd_s2, sd_fin)
                AR.release(m2)
            STs = AR.alloc((4, TT), BF16)
            if not need_ctx:
                P.memset(STs[:, :, 0:LC], 0.0)
            for t in trange:
                m4 = AR.mark()
                S4 = AR.alloc(4, F32)
                P.reduce(S4[:, 0:1], SSQ[:, t, :], ALU.add)
                P.act(S4[:, 1:2], S4[:, 0:1], AF.Sqrt, bias=EPST, scale=1.0 / 512)
                P.recip(S4[:, 2:3], S4[:, 1:2])
                O2 = AR.alloc(512, BF16)
                P.stt(O2, YG[:, t, :], S4[:, 2:3], SNG[:, l * 512:(l + 1) * 512], ALU.mult, ALU.mult)
                pT = rr().bitcast(BF16)
                for ch in range(4):
                    P.transpose(pT[:, ch * 128:(ch + 1) * 128], O2[:, ch * 128:(ch + 1) * 128], IDB)
                P.copy(STs[:, :, t * 128:(t + 1) * 128], pT[:, 0:512].rearrange("p (a b) -> p a b", a=4), eng="act")
                AR.release(m4)
            P.dma(brs_d[2].rearrange("(a p) t -> p a t", p=128), STs)
            AR.release(m0)
            if stop_after == "ssd":
                break

            m0 = AR.mark()
            BR = [AR.alloc((4, TT), BF16) for _ in range(3)]
            for i in range(3):
                P.dma(BR[i], brs_d[i].rearrange("(a p) t -> p a t", p=128))
            for i in range(8):
                WMp = next_piece(l, f"MG{i}"); prefetch()
                Wg = wview(WMp, 0, 8, 384)
                Wo_ = WMp[:, 3072:3072 + 1536].rearrange("p (b a c) -> p b a c", b=3, a=4)
                m_i = AR.mark()
                YTi = AR.alloc(TT, BF16)
                for (t0, t1, isctx) in blocks:
                    nt = t1 - t0
                    m1 = AR.mark()
                    YA = AR.alloc(nt, F32); TM = AR.alloc(nt, F32)
                    for br in range(3):
                        psG = proj_fm(Wg, br * 128, 128, t0, t1)
                        SG = AR.alloc(nt, F32)
                        P.act(SG, psG, AF.Sigmoid)
                        psO = rr()
                        for kc in range(4):
                            P.mm(psO[:, 0:nt], Wo_[:, br, kc, :], BR[br][:, kc, t0:t1], start=(kc == 0), stop=(kc == 3))
                        if br == 0:
                            P.tt(YA, SG, psO[:, 0:nt], ALU.mult)
                        elif br == 1:
                            P.tt(TM, SG, psO[:, 0:nt], ALU.mult)
                            P.tt(YA, YA, TM, ALU.add, eng="pool")
                        else:
                            P.tt(TM, SG, psO[:, 0:nt], ALU.mult)
                            P.tt(YTi[:, t0:t1], YA, TM, ALU.add, eng="pool")
                    AR.release(m1)
                P.dma(yt_d[i * 128:(i + 1) * 128, blocks[0][0]:TT], YTi[:, blocks[0][0]:TT])
                AR.release(m_i)
            AR.release(m0)

            m_x = AR.mark()
            X = AR.alloc((8, TT), F32)
            if l == 0:
                P.dma(X[:, :, 0:LC], ctxT_d[b].rearrange("(a p) t -> p a t", p=128))
                P.dma(X[:, :, LC:TT], xT_d[b].rearrange("(a p) t -> p a t", p=128))
            else:
                P.dma(X, xs_d.rearrange("(a p) t -> p a t", p=128))
            WOp = next_piece(l, "WO"); prefetch()
            WOv = wview(WOp, 0, 8, 1024)
            for (t0, t1, isctx) in blocks:
                nt = t1 - t0
                who = 2 if isctx else b
                m1 = AR.mark()
                Yb = AR.alloc((8, nt), BF16)
                P.dma(Yb, yt_d.rearrange("(a p) t -> p a t", p=128)[:, :, t0:t1])
                for o in range(8):
                    ps = rr()
                    for kc in range(8):
                        P.mm(ps[:, 0:nt], WOv[:, kc, o * 128:(o + 1) * 128], Yb[:, kc, :], start=(kc == 0), stop=(kc == 7))
                    P.stt(X[:, o, t0:t1], ps[:, 0:nt], modv(l, 2, o, who), X[:, o, t0:t1], ALU.mult, ALU.add)
                AR.release(m1)
            if stop_after == "wo":
                break

            norm_to(X, l, 1, b, H, blocks=blocks)
            for q in range(11):
                WF = next_piece(l, f"FF{q}"); prefetch()
                Wu = wview(WF, 0, 8, 512)
                Wd = wview(WF, 4096, 2, 1024)
                for (t0, t1, isctx) in blocks:
                    nt = t1 - t0
                    who = 2 if isctx else b
                    m1 = AR.mark()
                    A_ = AR.alloc((2, nt), BF16)
                    for jj in range(2):
                        psg = proj_fm(Wu, jj * 128, 128, t0, t1)
                        SGt = AR.alloc(nt, F32)
                        P.act(SGt, psg, AF.Silu)
                        psu = proj_fm(Wu, 256 + jj * 128, 128, t0, t1)
                        P.tt(A_[:, jj, :], SGt, psu, ALU.mult)
                    for o in range(8):
                        ps = rr()
                        for jj in range(2):
                            P.mm(ps[:, 0:nt], Wd[:, jj, o * 128:(o + 1) * 128], A_[:, jj, :], start=(jj == 0), stop=(jj == 1))
                        P.stt(X[:, o, t0:t1], ps[:, 0:nt], modv(l, 5, o, who), X[:, o, t0:t1], ALU.mult, ALU.add)
                    AR.release(m1)
            if last:
                for (t0, t1, isctx) in [(LC + 256 * i, LC + 256 * (i + 1), False) for i in range(8)]:
                    m1 = AR.mark()
                    OUT = AR.alloc((8, t1 - t0), F32)
                    norm_to(X, l, 0, b, OUT, gvec=GN[:, 2 * L * 8:(2 * L + 1) * 8], blocks=[(t0, t1, isctx)], toff=t0)
                    P.dma(outT_d[b].rearrange("(a p) t -> p a t", p=128)[:, :, t0 - LC:t1 - LC], OUT)
                    AR.release(m1)
                AR.release(m_x)
        else:
            continue
        break

    outs = [d for d in P.dmas if False]
    final = [d for d in P.dmas]
    P.emit(final_wait_dmas=final[-ND:])
    AR.P = P
    return nc, AR


_CACHE = {}


def host_prep(inputs, NB=2, L=DEPTH):
    f32 = np.float32
    g = {k: np.asarray(v, dtype=f32) for k, v in inputs.items()}
    consts = _host_consts()
    wl = [_host_layer_weights(l, g["w_in"], g["w_att_out"], g["w_ml_out"], g["w_ssd_out"], g["w_o"], g["w_up"], g["w_down"])
          for l in range(L)]
    wmod = np.stack([_k8(g["w_mod"][l], np.arange(6144)) for l in range(L)])
    bmod = np.stack([g["b_mod"][l].reshape(48, 128).T for l in range(L)])
    gn = np.concatenate([np.concatenate([g["g_norm1"][l].reshape(8, 128).T, g["g_norm2"][l].reshape(8, 128).T], 1)
                         for l in range(L)] + [g["g_final"].reshape(8, 128).T], 1)
    sinkb = np.broadcast_to(g["att_sink"][:L].reshape(1, L * 8), (128, L * 8))
    mlb = np.concatenate([np.stack([g["ml_i_bias"][l].reshape(8), g["ml_f_bias"][l].reshape(8)], 1) for l in range(L)], 1)
    mlgain = np.broadcast_to(g["ml_head_gain"][:L].reshape(1, L * 512), (128, L * 512))
    cw = []
    for l in range(L):
        for ci in range(8):
            ch = slice(ci * 128, (ci + 1) * 128)
            cw.append(np.concatenate([g["ssd_conv_w"][l][:, ch].T, g["ssd_conv_b"][l][ch][:, None]], 1))
    convw = np.concatenate(cw, 1)
    sdt = np.concatenate([np.stack([g["ssd_dt_bias"][l][0], g["ssd_dt_bias"][l][1], g["ssd_a_log"][l][0], g["ssd_a_log"][l][1]], 1)
                          for l in range(L)], 1)
    ssdskip = np.broadcast_to(np.repeat(g["ssd_d"][:L], 64, axis=1).reshape(1, L * 512), (128, L * 512))
    ssdng = np.broadcast_to(g["ssd_norm_gain"][:L].reshape(1, L * 512), (128, L * 512))
    shared = dict(wmod=wmod, bmod=bmod, gn=gn, sinkb=sinkb, mlb=mlb, mlgain=mlgain, convw=convw, ssddt=sdt,
                  ssdskip=ssdskip, ssdng=ssdng, **consts)
    for l in range(L):
        shared[f"wl{l}"] = wl[l]
    shared = {k: np.ascontiguousarray(v, dtype=f32) for k, v in shared.items()}
    ncores = g["x"].shape[0] // NB
    maps = []
    for c in range(ncores):
        bs = slice(c * NB, (c + 1) * NB)
        m = dict(shared)
        m["xT"] = np.ascontiguousarray(g["x"][bs].transpose(0, 2, 1))
        m["ctxT"] = np.ascontiguousarray(g["ctx"][bs].transpose(0, 2, 1))
        cc = np.stack([g["c"][c * NB + i] if i < NB else g["c_ctx"] for i in range(2)] + [g["c_ctx"]], 0)
        if NB == 1:
            cc = np.stack([g["c"][c], g["c"][c], g["c_ctx"]], 0)
        silu_in = cc.reshape(3, 8, 128).transpose(2, 1, 0).reshape(128, 24)
        m["cT"] = np.ascontiguousarray(silu_in)
        maps.append(m)
    return maps


def kernel(**inputs):
    NB = 2
    if "nc" not in _CACHE:
        _CACHE["nc"] = build(NB=NB)[0]
    nc = _CACHE["nc"]
    maps = host_prep(inputs, NB=NB)
    res = run_bass_kernel_spmd(nc, maps, core_ids=list(range(len(maps))))
    outs = [r["outT"] for r in res.results]
    full = np.concatenate(outs, 0)
    return np.ascontiguousarray(full.transpose(0, 2, 1)).astype(np.float32)
```

```python
import numpy as np
import concourse.bass as bass
import concourse.mybir as mybir

F32 = mybir.dt.float32
BF16 = mybir.dt.bfloat16
AF = mybir.ActivationFunctionType
ALU = mybir.AluOpType
AX = mybir.AxisListType

NS = 4
ND = 24


class Instr:
    __slots__ = ("eng", "idx", "fn", "dma", "deps", "signal", "clock", "dma_id", "sidx")

    def __init__(self, eng, idx, fn, dma):
        self.eng = eng
        self.idx = idx
        self.fn = fn
        self.dma = dma
        self.deps = []
        self.signal = False
        self.clock = None
        self.dma_id = None
        self.sidx = None


def ap_box(ap):
    t = ap.tensor
    dims = list(ap.ap)
    pstride, pn = dims[0]
    shape = list(t.shape)
    per_part = 1
    for s in shape[1:]:
        per_part *= int(s)
    off = int(ap.offset)
    if pstride == 0:
        pstride_eff = per_part
    else:
        pstride_eff = pstride
    p0 = off // per_part
    f0 = off - p0 * per_part
    lo = 0
    hi = 0
    for st, cn in dims[1:]:
        ext = (int(cn) - 1) * int(st)
        if ext >= 0:
            hi += ext
        else:
            lo += ext
    esz = mybir.dt.size(t.dtype) if hasattr(mybir.dt, "size") else None
    return (t.name, p0, p0 + int(pn), f0 + lo, f0 + hi + 1)


class Prog:
    def __init__(self, nc):
        self.nc = nc
        self.engs = {"pe": nc.tensor, "dve": nc.vector, "act": nc.scalar, "pool": nc.gpsimd, "sp": nc.sync}
        self.ins = {e: [] for e in self.engs}
        self.track = {}
        self.known = {e: {} for e in self.engs}
        self.ndma = 0
        self.dmas = []
        self.esize = {}
        self.skip_dram = set()

    def _box(self, ap):
        t = ap.tensor
        sp = str(type(t).__name__)
        if sp.startswith("DRam"):
            if t.name in self.skip_dram:
                return None
            off = int(ap.offset); lo = 0; hi = 0
            for st, cn in ap.ap:
                ext = (int(cn) - 1) * int(st)
                if ext >= 0: hi += ext
                else: lo += ext
            es = {F32: 4, BF16: 2}.get(t.dtype, 4)
            return (t.name, 0, 1, (off + lo) * es, (off + hi + 1) * es)
        name, p0, p1, f0, f1 = ap_box(ap)
        if sp.startswith("PSum"):
            return (name, 0, 128, 0, 2048)
        es = {F32: 4, BF16: 2}.get(t.dtype, None)
        if es is None:
            es = int(ap.nbytes // max(1, ap.size)) if hasattr(ap, "nbytes") else 4
        return (name, p0, p1, f0 * es, f1 * es)

    def _collect(self, ins, box, write):
        name, p0, p1, f0, f1 = box
        BIN = 4096 if not name.startswith("dr_") else (1 << 22)
        bins = range(f0 // BIN, (f1 - 1) // BIN + 1)
        tr = self.track.setdefault(name, {})
        deps = []
        seen = set()
        newrec = (ins, write, p0, p1, f0, f1)
        for bi in bins:
            recs = tr.get(bi)
            if recs is None:
                tr[bi] = [newrec]
                continue
            keep = self._scan(recs, ins, write, p0, p1, f0, f1, deps, seen, bi * BIN, (bi + 1) * BIN,
                              name_is_psum=name.startswith("ps"))
            keep.append(newrec)
            tr[bi] = keep
        return deps

    def _scan(self, recs, ins, write, p0, p1, f0, f1, deps, seen, b0, b1, name_is_psum=False):
        keep = []
        for r in recs:
            rin, rw, q0, q1, g0, g1 = r
            ov = not (q1 <= p0 or p1 <= q0 or g1 <= f0 or f1 <= g0)
            if not ov:
                keep.append(r)
                continue
            if (write or rw or (name_is_psum and rin.eng != ins.eng)) and id(rin) not in seen:
                seen.add(id(rin))
                deps.append((rin, rw))
            covered = write and p0 <= q0 and q1 <= p1 and f0 <= max(g0, b0) and min(g1, b1) <= f1
            if covered:
                continue
            if (not write) and (not rw) and rin.eng == ins.eng and not ins.dma and not rin.dma \
                    and q0 == p0 and q1 == p1 and g0 == f0 and g1 == f1:
                continue
            keep.append(r)
        return keep

    def add(self, eng, fn, reads=(), writes=(), dma=False):
        ins = Instr(eng, len(self.ins[eng]), fn, dma)
        alld = {}
        for ap in reads:
            b = self._box(ap)
            if b is None:
                continue
            for d, dw in self._collect(ins, b, False):
                alld[id(d)] = (d, True)
        for ap in writes:
            b = self._box(ap)
            if b is None:
                continue
            for d, dw in self._collect(ins, b, True):
                prev = alld.get(id(d))
                alld[id(d)] = (d, (prev[1] if prev else False) or False)
        known = self.known[eng]
        if dma:
            ins.dma_id = self.ndma
            self.ndma += 1
            self.dmas.append(ins)
            if ins.dma_id >= ND:
                d = self.dmas[ins.dma_id - ND]
                alld[id(d)] = (d, True)
        need = []
        for d, raw in sorted(alld.values(), key=lambda x: -(x[0].dma_id if x[0].dma else x[0].idx)):
            if d is ins:
                continue
            if d.dma:
                key = ("dma", d.dma_id % ND)
                val = d.dma_id // ND + 1
                if known.get(key, 0) >= val:
                    continue
                need.append(d)
            else:
                if d.eng == eng and not dma:
                    if eng == "pe":
                        continue
                    if not raw:
                        continue
                if known.get(d.eng, -1) >= d.idx:
                    continue
                need.append(d)
        for d in need:
            d.signal = True
            if d.dma:
                key = ("dma", d.dma_id % ND)
                known[key] = max(known.get(key, 0), d.dma_id // ND + 1)
            else:
                known[d.eng] = max(known.get(d.eng, -1), d.idx)
            if d.clock:
                for k, v in d.clock.items():
                    if known.get(k, -1) < v:
                        known[k] = v
        ins.deps = need
        ins.clock = dict(known)
        if not dma:
            ins.clock[eng] = max(ins.clock.get(eng, -1), ins.idx - 1)
        self.ins[eng].append(ins)
        return ins

    def mm(self, out, lhsT, rhs, start=True, stop=True, **kw):
        return self.add("pe", lambda e: e.matmul(out, lhsT, rhs, start=start, stop=stop, **kw),
                        reads=[lhsT, rhs], writes=[out])

    def transpose(self, out, in_, ident):
        return self.add("pe", lambda e: e.transpose(out, in_, ident), reads=[in_, ident], writes=[out])

    def act(self, out, in_, func, bias=None, scale=None, accum_out=None, eng="act"):
        kw = {}
        reads = [in_]
        if bias is not None:
            kw["bias"] = bias
            if not isinstance(bias, (int, float)):
                reads.append(bias)
        if scale is not None:
            kw["scale"] = scale
            if not isinstance(scale, (int, float)):
                reads.append(scale)
        writes = [out]
        if accum_out is not None:
            kw["accum_out"] = accum_out
            writes.append(accum_out)
        return self.add(eng, lambda e: e.activation(out, in_, func, **kw), reads=reads, writes=writes)

    def tt(self, out, in0, in1, op, eng="dve"):
        return self.add(eng, lambda e: e.tensor_tensor(out, in0, in1, op), reads=[in0, in1], writes=[out])

    def ts(self, out, in0, s1, s2, op0, op1=None, eng="dve", accum_out=None):
        reads = [in0]
        for s in (s1, s2):
            if s is not None and not isinstance(s, (int, float)):
                reads.append(s)
        writes = [out]
        kw = {}
        if accum_out is not None:
            kw["accum_out"] = accum_out
            writes.append(accum_out)
        if op1 is None:
            return self.add(eng, lambda e: e.tensor_scalar(out, in0, s1, None, op0, **kw), reads=reads, writes=writes)
        return self.add(eng, lambda e: e.tensor_scalar(out, in0, s1, s2, op0, op1, **kw), reads=reads, writes=writes)

    def stt(self, out, in0, scalar, in1, op0, op1, eng="dve"):
        reads = [in0, in1]
        if not isinstance(scalar, (int, float)):
            reads.append(scalar)
        return self.add(eng, lambda e: e.scalar_tensor_tensor(out, in0, scalar, in1, op0, op1),
                        reads=reads, writes=[out])

    def copy(self, out, in_, eng="dve"):
        if eng == "act":
            return self.add(eng, lambda e: e.copy(out, in_), reads=[in_], writes=[out])
        return self.add(eng, lambda e: e.tensor_copy(out, in_), reads=[in_], writes=[out])

    def reduce(self, out, in_, op, axis=AX.X, eng="dve"):
        return self.add(eng, lambda e: e.tensor_reduce(out, in_, axis, op), reads=[in_], writes=[out])

    def scan(self, out, d0, d1, initial, op0, op1):
        reads = [d0, d1]
        if not isinstance(initial, (int, float)):
            reads.append(initial)
        return self.add("dve", lambda e: e.tensor_tensor_scan(out, d0, d1, initial, op0, op1),
                        reads=reads, writes=[out])

    def recip(self, out, in_):
        return self.add("dve", lambda e: e.reciprocal(out, in_), reads=[in_], writes=[out])

    def memset(self, ap, val, eng="dve"):
        return self.add(eng, lambda e: e.memset(ap, val), reads=[], writes=[ap])

    def dma(self, out, in_, q="sp", **kw):
        return self.add(q, lambda e: e.dma_start(out=out, in_=in_, **kw), reads=[in_], writes=[out], dma=True)

    def emit(self, final_wait_dmas=()):
        nc = self.nc
        dsem = [nc.alloc_semaphore(name=f"d_{i}") for i in range(ND)]
        nsig = {}
        for e, lst in self.ins.items():
            c = 0
            for ins in lst:
                if ins.dma:
                    continue
                if ins.signal:
                    ins.sidx = c
                    c += 1
            nsig[e] = c
        nsr = {e: max(1, -(-nsig[e] // 1800)) for e in self.engs}
        self.nsr = nsr
        rings = {e: [nc.alloc_semaphore(name=f"r_{e}_{i}") for i in range(nsr[e])] for e in self.engs}
        prog = self

        def body(ename):
            def f(engine):
                for ins in prog.ins[ename]:
                    for d in ins.deps:
                        if d.dma:
                            engine.wait_ge(dsem[d.dma_id % ND], 16 * (d.dma_id // ND + 1))
                        else:
                            engine.wait_ge(rings[d.eng][d.sidx % nsr[d.eng]], d.sidx // nsr[d.eng] + 1)
                    r = ins.fn(engine)
                    if ins.dma:
                        r.then_inc(dsem[ins.dma_id % ND], 16)
                    elif ins.signal:
                        r.then_inc(rings[ename][ins.sidx % nsr[ename]], 1)
                if ename == "sp":
                    for d in final_wait_dmas:
                        engine.wait_ge(dsem[d.dma_id % ND], 16 * (d.dma_id // ND + 1))
            return f

        with nc.Block() as block:
            block.sync(body("sp"))
            block.tensor(body("pe"))
            block.vector(body("dve"))
            block.scalar(body("act"))
            block.gpsimd(body("pool"))

from concourse.bass_utils import run_bass_kernel_spmd
import math

D = 1024
T = 2048
LC = 256
TT = T + LC
NCH = TT // 128
DEPTH = 2
DFF = 2816
EPS = 1e-6
NEG = -1.0e30
SLOT_EL = 8192

TBLOCKS = [(0, 256, True), (256, 768, False), (768, 1280, False), (1280, 1792, False), (1792, 2304, False)]


def _piece_list():
    pcs = [("A1", 8 * 768), ("A2", 8 * 640), ("SM1", 8 * 16)]
    for h in range(4):
        pcs.append((f"MH{h}", 8 * 512))
    pcs.append(("SM2", 8 * 16))
    for g in range(2):
        pcs.append((f"SG{g}", 8 * 768))
    for i in range(8):
        pcs.append((f"MG{i}", 8 * 384 + 3 * 4 * 128))
    pcs.append(("WO", 8 * 1024))
    for q in range(11):
        pcs.append((f"FF{q}", 8 * 512 + 2 * 1024))
    return pcs


def _k8(w, cols):
    sub = w[:, cols]
    K, n = sub.shape
    return np.ascontiguousarray(sub.reshape(K // 128, 128, n).transpose(1, 0, 2)).reshape(128, -1)


def _host_layer_weights(l, w_in, w_att_out, w_ml_out, w_ssd_out, w_o, w_up, w_down):
    wi = w_in[l]
    ar = np.arange
    qcols = np.concatenate([np.concatenate([c * 64 + ar(64), (4 + c) * 64 + ar(64)]) for c in range(4)])
    perm64 = np.concatenate([16 + ar(16), ar(16), 48 + ar(16), 32 + ar(16)])
    qperm = (qcols // 64) * 64 + perm64[qcols % 64]
    kcols = 512 + ar(128)
    kperm = 512 + (ar(128) // 64) * 64 + perm64[ar(128) % 64]
    vcols = 640 + ar(128)
    out = []
    out.append(_k8(wi, np.concatenate([qcols, kcols, vcols])))
    out.append(_k8(wi, np.concatenate([qperm, kperm])))
    g0 = 2816
    out.append(_k8(wi, np.concatenate([g0 + ar(4), g0 + 8 + ar(4), g0 + 4 + ar(4), g0 + 12 + ar(4)])))
    for h in range(4):
        out.append(_k8(wi, np.concatenate([768 + h * 128 + ar(128), 1280 + h * 128 + ar(128),
                                           1792 + h * 128 + ar(128), 2304 + h * 128 + ar(128)])))
    z0, x0, d0, m0 = 2832, 3344, 4368, 4384
    out.append(_k8(wi, d0 + ar(16)))
    for g in range(2):
        out.append(_k8(wi, np.concatenate([x0 + g * 256 + ar(256), x0 + 512 + g * 128 + ar(128),
                                           x0 + 768 + g * 128 + ar(128), z0 + g * 256 + ar(256)])))
    arow = np.concatenate([np.concatenate([c * 64 + ar(64), (4 + c) * 64 + ar(64)]) for c in range(4)])
    for i in range(8):
        cc = i * 128 + ar(128)
        a = _k8(wi, np.concatenate([m0 + cc, m0 + 1024 + cc, m0 + 2048 + cc]))
        b0 = _k8(w_att_out[l][arow], cc)
        b1 = _k8(w_ml_out[l], cc)
        b2 = _k8(w_ssd_out[l], cc)
        out.append(np.concatenate([a, b0, b1, b2], axis=1))
    out.append(_k8(w_o[l], ar(1024)))
    for q in range(11):
        j0 = 2 * q * 128
        a = _k8(w_up[l], np.concatenate([j0 + ar(256), DFF + j0 + ar(256)]))
        b = _k8(w_down[l][j0:j0 + 256], ar(1024))
        out.append(np.concatenate([a, b], axis=1))
    cat = np.concatenate(out, axis=1)
    return np.ascontiguousarray(cat, dtype=np.float32)


def _host_consts():
    ar = np.arange
    t = ar(T)
    inv = (10000.0 ** (-ar(16, dtype=np.float32) / 16)).astype(np.float32)
    row = (t // 64).astype(np.float32)
    col = (t % 64).astype(np.float32)
    ang_r = row[None, :] * inv[:, None]
    ang_c = col[None, :] * inv[:, None]
    cos64 = np.concatenate([np.cos(ang_r), np.cos(ang_r), np.cos(ang_c), np.cos(ang_c)], 0)
    sin64 = np.concatenate([-np.sin(ang_r), np.sin(ang_r), -np.sin(ang_c), np.sin(ang_c)], 0)
    cosT = np.concatenate([cos64, cos64], 0).astype(np.float32)
    sinT = np.concatenate([sin64, sin64], 0).astype(np.float32)
    q = ar(128)[:, None]
    k = ar(128)[None, :]
    maskA = np.zeros((128, 384), np.float32)
    maskA[:, 0:128] = np.where(k >= q, 0.0, -1.0e5)
    maskA[:, 256:384] = np.where(k <= q, 0.0, -1.0e5)
    s = ar(128)[:, None]
    tt = ar(128)[None, :]
    maskF = np.where(s <= tt, 0.0, NEG).astype(np.float32)
    maskB = np.where(s >= tt, 0.0, NEG).astype(np.float32)
    sel = np.zeros((8, 8, 128), np.float32)
    for r in range(8):
        sel[r, r, :] = 1.0
    ident = np.eye(128, dtype=np.float32)
    return dict(cosT=cosT, sinT=sinT, maskA=maskA, maskF=maskF, maskB=maskB, sel=sel.reshape(8, 1024), ident=ident)


class Arena:
    def __init__(self, nc, nbytes):
        self.t = nc.alloc_sbuf_tensor("arena", [128, nbytes // 2], BF16)
        self.top = 0
        self.cap = nbytes
        self.peak = 0

    def alloc(self, free, dtype, parts=128):
        if isinstance(free, int):
            free = (free,)
        n = 1
        for f in free:
            n *= f
        es = 4 if dtype == F32 else 2
        off = (self.top + 63) // 64 * 64
        self.top = off + n * es
        self.peak = max(self.peak, self.top)
        assert self.top <= self.cap, f"arena overflow {self.top} > {self.cap}"
        v = self.t[0:parts, off // 2:(off + n * es) // 2]
        if dtype == F32:
            v = v.bitcast(F32)
        if len(free) == 2:
            v = v.rearrange("p (a b) -> p a b", a=free[0])
        elif len(free) == 3:
            v = v.rearrange("p (a b c) -> p a b c", a=free[0], b=free[1])
        return v

    def mark(self):
        return self.top

    def release(self, m):
        self.top = m


def build(NB=2, L=DEPTH, dbg=False, stop_after=None):
    nc = bass.Bass("TRN2", target_bir_lowering=False)
    P = Prog(nc)

    def din(name, shape, dt=F32):
        P.skip_dram.add(name)
        return nc.dram_tensor(name, list(shape), dt, kind="ExternalInput").ap()

    def dscr(name, shape, dt):
        return nc.dram_tensor(name, list(shape), dt, kind="ExternalOutput" if dbg else "Internal").ap()

    pcs = _piece_list()
    poff = {}
    o = 0
    for nm, ne in pcs:
        poff[nm] = (o, ne)
        o += ne
    WTOT = o

    xT_d = din("xT", [NB, D, T])
    ctxT_d = din("ctxT", [NB, D, LC])
    cT_d = din("cT", [128, 24])
    wmod_d = din("wmod", [L, 128, 8 * 6144])
    bmod_d = din("bmod", [L, 128, 48])
    gn_d = din("gn", [128, (2 * L + 1) * 8])
    wl_d = [din(f"wl{l}", [128, WTOT]) for l in range(L)]
    cos_d = din("cosT", [128, T]); sin_d = din("sinT", [128, T])
    maskA_d = din("maskA", [128, 384]); maskF_d = din("maskF", [128, 128]); maskB_d = din("maskB", [128, 128])
    sel_d = din("sel", [8, 1024]); ident_d = din("ident", [128, 128])
    sink_d = din("sinkb", [128, L * 8])
    mlb_d = din("mlb", [8, L * 2])
    mlg_d = din("mlgain", [128, L * 512])
    cw_d = din("convw", [128, L * 8 * 6])
    sdt_d = din("ssddt", [8, L * 4])
    sdk_d = din("ssdskip", [128, L * 512])
    sng_d = din("ssdng", [128, L * 512])
    outT_d = nc.dram_tensor("outT", [NB, D, T], F32, kind="ExternalOutput").ap()

    brs_d = [dscr(f"dr_br{i}", [512, TT], BF16) for i in range(3)]
    yt_d = dscr("dr_yt", [D, TT], BF16)
    xs_d = dscr("dr_xs", [D, TT], F32)
    hdbg_d = dscr("dr_h", [D, TT], BF16) if dbg else None

    AR = Arena(nc, 204 * 1024)
    banks = [nc.alloc_psum_tensor(f"ps{i}", [128, 512], F32) for i in range(8)]
    st = {"rr": 0, "acc": 0, "kq": 0}

    def rr():
        b = banks[st["rr"] % 3]
        st["rr"] += 1
        return b

    def kqb():
        b = banks[3 + st["kq"] % 3]
        st["kq"] += 1
        return b

    def accb():
        b = banks[6 + st["acc"] % 2]
        st["acc"] += 1
        return b

    IDF = AR.alloc(128, F32); P.dma(IDF, ident_d)
    IDB = AR.alloc(128, BF16); P.dma(IDB, ident_d, q="pool")
    ONESB = AR.alloc(128, BF16); P.memset(ONESB, 1.0)
    COS = AR.alloc(T, BF16); P.dma(COS.rearrange("p (a b) -> p a b", b=512), cos_d.rearrange("p (a b) -> p a b", b=512), q="pool")
    SIN = AR.alloc(T, BF16); P.dma(SIN.rearrange("p (a b) -> p a b", b=512), sin_d.rearrange("p (a b) -> p a b", b=512), q="pool")
    MASKA = AR.alloc(384, F32); P.dma(MASKA, maskA_d)
    MASKF = AR.alloc(128, F32); P.dma(MASKF, maskF_d)
    MASKB = AR.alloc(128, F32); P.dma(MASKB, maskB_d)
    SEL = AR.alloc((8, 128), F32, parts=8); P.dma(SEL, sel_d.rearrange("p (a b) -> p a b", a=8))
    SINKB = AR.alloc(L * 8, F32); P.dma(SINKB, sink_d)
    MLB = AR.alloc(L * 2, F32, parts=8); P.dma(MLB, mlb_d)
    NMLB = AR.alloc(L * 2, F32, parts=8); P.ts(NMLB, MLB, -1.0, None, ALU.mult)
    MLG = AR.alloc(L * 512, F32); P.dma(MLG, mlg_d)
    CW = AR.alloc(L * 48, F32); P.dma(CW, cw_d)
    SDT = AR.alloc(L * 4, F32, parts=8); P.dma(SDT, sdt_d)
    SA = AR.alloc(L * 4, F32, parts=8)
    P.act(SA, SDT, AF.Exp)
    P.ts(SA, SA, -1.0, None, ALU.mult)
    SDK = AR.alloc(L * 512, BF16); P.dma(SDK, sdk_d, q="pool")
    SNG = AR.alloc(L * 512, F32); P.dma(SNG, sng_d)
    GN = AR.alloc((2 * L + 1) * 8, F32); P.dma(GN, gn_d)
    EPST = AR.alloc(1, F32); P.memset(EPST, EPS)
    ONEF = AR.alloc(TT, F32, parts=8); P.memset(ONEF, 1.0)
    MOD = [AR.alloc((48, 3), F32) for _ in range(L)]
    GS = [AR.alloc((2, 8, 3), F32) for _ in range(L)]
    H = AR.alloc((8, TT), BF16)
    SLOTS = [AR.alloc(SLOT_EL, BF16) for _ in range(2)]

    wq = []
    for b in range(NB):
        for l in range(L):
            for nm, ne in pcs:
                wq.append((l, nm))
    wst = {"issued": 0, "cur": -1}

    def issue_next():
        i = wst["issued"]
        if i >= len(wq):
            return
        l, nm = wq[i]
        o, ne = poff[nm]
        CH = 512 if ne % 512 == 0 else 128
        P.dma(SLOTS[i % 2][:, 0:ne].rearrange("p (a b) -> p a b", b=CH),
              wl_d[l][:, o:o + ne].rearrange("p (a b) -> p a b", b=CH), q="pool")
        wst["issued"] += 1

    def next_piece(l, nm):
        wst["cur"] += 1
        i = wst["cur"]
        assert wq[i] == (l, nm), (wq[i], l, nm)
        while wst["issued"] <= i:
            issue_next()
        return SLOTS[i % 2]

    def prefetch():
        if wst["issued"] <= wst["cur"] + 1:
            issue_next()

    def wview(slot, off, kc, n):
        return slot[:, off:off + kc * n].rearrange("p (a b) -> p a b", a=kc)

    CT = AR.alloc(24, F32); P.dma(CT, cT_d)
    SC = AR.alloc(24, F32); P.act(SC, CT, AF.Silu)
    BM = AR.alloc(L * 48, F32)
    for l in range(L):
        P.dma(BM[:, l * 48:(l + 1) * 48], bmod_d[l])
    for l in range(L):
        for pc in range(12):
            WM = SLOTS[pc % 2].bitcast(F32).rearrange("p (a b) -> p a b", a=8)
            P.dma(WM, wmod_d[l].rearrange("p (a b) -> p a b", a=8)[:, :, pc * 512:(pc + 1) * 512])
            for cc in range(4):
                ps = rr()
                for kc in range(8):
                    P.mm(ps[:, 0:3], WM[:, kc, cc * 128:(cc + 1) * 128], SC[:, kc * 3:(kc + 1) * 3],
                         start=(kc == 0), stop=(kc == 7))
                col = pc * 4 + cc
                P.ts(MOD[l][:, col, :], ps[:, 0:3], BM[:, l * 48 + col:l * 48 + col + 1], None, ALU.add)
        for n in range(2):
            for kc in range(8):
                P.ts(GS[l][:, n, kc, :], MOD[l][:, (1 + 3 * n) * 8 + kc, :], 1.0,
                     GN[:, (2 * l + n) * 8 + kc:(2 * l + n) * 8 + kc + 1], ALU.add, ALU.mult)

    def modv(l, which, kc, who):
        return MOD[l][:, which * 8 + kc, who:who + 1]

    def norm_to(Xv, l, n, b, dst, gvec=None, blocks=TBLOCKS, dst_f32=False, toff=0):
        for (t0, t1, isctx) in blocks:
            who = 2 if isctx else b
            nt = t1 - t0
            m = AR.mark()
            SQ = AR.alloc((8, nt), BF16)
            for kc in range(8):
                P.act(SQ[:, kc, :], Xv[:, kc, t0:t1], AF.Square)
            ps = rr()
            for kc in range(8):
                P.mm(ps[:, 0:nt], ONESB, SQ[:, kc, :], start=(kc == 0), stop=(kc == 7))
            RS = AR.alloc(nt, F32)
            P.act(RS, ps[:, 0:nt], AF.Sqrt, bias=EPST, scale=1.0 / D)
            P.recip(RS, RS)
            TMP = AR.alloc((2, nt), F32)
            for kc in range(8):
                tv = TMP[:, kc % 2, :]
                P.tt(tv, Xv[:, kc, t0:t1], RS, ALU.mult)
                if gvec is None:
                    P.act(dst[:, kc, t0 - toff:t1 - toff], tv, AF.Identity, scale=GS[l][:, n, kc, who:who + 1],
                          bias=modv(l, 3 * n, kc, who))
                else:
                    P.act(dst[:, kc, t0 - toff:t1 - toff], tv, AF.Identity, scale=gvec[:, kc:kc + 1])
            AR.release(m)

    def proj_fm(W, c0, ncols, t0, t1):
        ps = rr()
        for kc in range(8):
            P.mm(ps[0:ncols, 0:t1 - t0], W[:, kc, c0:c0 + ncols], H[:, kc, t0:t1], start=(kc == 0), stop=(kc == 7))
        return ps[0:ncols, 0:t1 - t0]

    def proj_tm(W, c0, ncols, c):
        ps = rr()
        for kc in range(8):
            P.mm(ps[:, 0:ncols], H[:, kc, c * 128:(c + 1) * 128], W[:, kc, c0:c0 + ncols],
                 start=(kc == 0), stop=(kc == 7))
        return ps[:, 0:ncols]

    def rows_scan(dst, src, op0, op1, d0, reverse):
        if not reverse:
            P.scan(dst[:, 0:LC], d0[:, 0:LC], src[:, 0:LC], 0.0, op0, op1)
            P.scan(dst[:, LC:TT], d0[:, LC:TT], src[:, LC:TT], dst[:, LC - 1:LC], op0, op1)
        else:
            P.scan(dst[:, 0:LC][:, ::-1], d0[:, 0:LC][:, ::-1], src[:, 0:LC][:, ::-1], 0.0, op0, op1)
            P.scan(dst[:, LC:TT][:, ::-1], d0[:, LC:TT][:, ::-1], src[:, LC:TT][:, ::-1], dst[:, 0:1], op0, op1)

    def rows_to_cols(COLS, ci, R):
        for c0 in range(0, NCH, 6):
            ps = rr()
            for c in range(c0, c0 + 6):
                P.transpose(ps[:, (c - c0) * 8:(c - c0) * 8 + 8], R[:, c * 128:(c + 1) * 128], IDF[0:8, 0:8])
            P.copy(COLS[:, c0:c0 + 6, ci, :], ps[:, 0:48].rearrange("p (a b) -> p a b", a=6))

    def contribs(t):
        out = []
        for s in range(0, t):
            out.append((s, 0, False))
        out.append((t, 0, True))
        if t >= 2:
            for s in (0, 1):
                out.append((s, 1, False))
            for s in range(t + 1, NCH):
                out.append((s, 1, False))
        else:
            for s in range(t + 1, 2):
                out.append((s, 1, False))
        out.append((t, 1, True))
        return out

    MASKD = [MASKF, MASKB]

    def build_bcast(rowtile, r, dst):
        for bi, (t0, t1, isctx) in enumerate(TBLOCKS):
            ps = rr()
            P.mm(ps[:, 0:t1 - t0], SEL[:, r, :], rowtile[:, t0:t1])
            P.copy(dst[:, t0:t1], ps[:, 0:t1 - t0], eng=("act" if bi % 2 == 0 else "dve"))

    def run_pipelined(items, stage1, stage2, fin):
        LA = 2
        cidx = [k for k, it in enumerate(items) if it[0] == "c"]
        nxt = {cidx[j]: (cidx[j + LA] if j + LA < len(cidx) else None) for j in range(len(cidx))}
        hnd = {}
        for j in range(min(LA, len(cidx))):
            hnd[cidx[j]] = stage1(items[cidx[j]])
        for k, it in enumerate(items):
            if it[0] == "c":
                n = nxt[k]
                if n is not None:
                    hnd[n] = stage1(items[n])
                stage2(it, hnd.pop(k))
            else:
                fin(it)

    def decay_T(rowtile, r, t, s, masked, d, biascol):
        ps = rr()
        if masked:
            P.mm(ps[:, 0:128], IDF, MASKD[d], start=True, stop=False)
        P.mm(ps[:, 0:128], SEL[:, r, :], rowtile[:, t * 128:(t + 1) * 128], start=(not masked), stop=True)
        return ps

    for b in range(NB):
        for l in range(L):
            last = (l == L - 1)
            need_ctx = not last
            blocks = TBLOCKS if need_ctx else TBLOCKS[1:]
            m_layer = AR.mark()
            if l == 0:
                X = AR.alloc((8, TT), F32)
                P.dma(X[:, :, 0:LC], ctxT_d[b].rearrange("(a p) t -> p a t", p=128))
                P.dma(X[:, :, LC:TT], xT_d[b].rearrange("(a p) t -> p a t", p=128))
            norm_to(X, l, 0, b, H)
            if l > 0:
                P.dma(xs_d.rearrange("(a p) t -> p a t", p=128), X)
                AR.release(m_x)
                m_layer = m_x
            if dbg and l == 0 and b == 0:
                P.dma(hdbg_d.rearrange("(a p) t -> p a t", p=128), H)
            AR.release(m_layer)
            if stop_after == "norm1":
                break

            m0 = AR.mark()
            W1 = next_piece(l, "A1"); prefetch()
            W1v = wview(W1, 0, 8, 768)
            QT = AR.alloc((4, TT), BF16)
            KT = AR.alloc(TT, BF16)
            VP = [AR.alloc((NCH, 128), BF16) for _ in range(2)]
            P.memset(VP[0], 0.0); P.memset(VP[1], 0.0)
            W2 = next_piece(l, "A2")
            W2v = wview(W2, 0, 8, 640)
            if stop_after == "attv":
                for c in range(NCH):
                    ps = proj_tm(W1v, 640, 128, c)
                    P.copy(VP[0][:, c, 0:64], ps[:, 0:64], eng="act")
                    P.copy(VP[1][:, c, 64:128], ps[:, 64:128], eng="act")
                P.dma(brs_d[0][0:128, :], VP[0].rearrange("p a b -> p (a b)"))
                break
            if stop_after == "attr":
                ps = proj_fm(W1v, 0, 128, 256, 768)
                ps2 = proj_fm(W2v, 0, 128, 256, 768)
                T1 = AR.alloc(512, F32); T2 = AR.alloc(512, F32)
                P.tt(T1, ps, COS[:, 0:512], ALU.mult)
                P.tt(T2, ps2, SIN[:, 0:512], ALU.mult)
                P.tt(QT[:, 0, 256:768], T1, T2, ALU.add)
                ps = proj_fm(W1v, 0, 128, 0, 256)
                P.copy(QT[:, 0, 0:256], ps, eng="act")
                P.dma(brs_d[0].rearrange("(a p) t -> p a t", p=128), QT)
                break
            if stop_after == "attw":
                ps = proj_fm(W1v, 0, 128, 256, 768)
                P.copy(QT[:, 0, 256:768], ps, eng="act")
                ps = proj_tm(W1v, 640, 128, 3)
                P.copy(VP[0][:, 3, 0:64], ps[:, 0:64], eng="act")
                P.dma(brs_d[0].rearrange("(a p) t -> p a t", p=128), QT)
                break
            for (t0, t1, isctx) in TBLOCKS:
                nt = t1 - t0
                for c in range(5):
                    if isctx and c < 4 and not need_ctx:
                        continue
                    dst = QT[:, c, t0:t1] if c < 4 else KT[:, t0:t1]
                    ps = proj_fm(W1v, c * 128, 128, t0, t1)
                    if isctx:
                        P.copy(dst, ps, eng="act")
                    else:
                        ps2 = proj_fm(W2v, c * 128, 128, t0, t1)
                        m = AR.mark()
                        T1 = AR.alloc(nt, F32); T2 = AR.alloc(nt, F32)
                        P.tt(T1, ps, COS[:, t0 - LC:t1 - LC], ALU.mult)
                        P.tt(T2, ps2, SIN[:, t0 - LC:t1 - LC], ALU.mult)
                        P.tt(dst, T1, T2, ALU.add)
                        AR.release(m)
            for c in range(NCH):
                ps = proj_tm(W1v, 640, 128, c)
                P.copy(VP[0][:, c, 0:64], ps[:, 0:64], eng="act")
                P.copy(VP[1][:, c, 64:128], ps[:, 64:128], eng="act")
            prefetch()
            ATs = AR.alloc((4, TT), BF16)
            if not need_ctx:
                P.memset(ATs[:, :, 0:LC], 0.0)
            if stop_after == "attproj":
                P.dma(brs_d[0].rearrange("(a p) t -> p a t", p=128), QT)
                break
            qblocks = list(range(0 if need_ctx else 2, NCH))
            if stop_after == "att1":
                qblocks = qblocks[:1]
            for qb in qblocks:
                lat = qb >= 2
                if lat:
                    loc = [k for k in (qb - 1, qb, qb + 1) if 2 <= k < NCH]
                else:
                    loc = []
                nl = len(loc) * 128
                keych = [0, 1] + loc
                mo = 0 if (not lat or qb - 1 >= 2) else 128
                for c in range(4):
                    acc = accb()
                    for half in range(2):
                        head = c + 4 * half
                        psl = slice(64 * half, 64 * half + 64)
                        lhsT = QT[psl, c, qb * 128:(qb + 1) * 128]
                        m = AR.mark()
                        SM = AR.alloc(640, F32)
                        psB = rr()
                        P.mm(psB[:, 0:256], lhsT, KT[psl, 0:256])
                        P.copy(SM[:, 0:256], psB[:, 0:256], eng="act")
                        if lat:
                            psA = rr()
                            P.mm(psA[:, 0:nl], lhsT, KT[psl, loc[0] * 128:(loc[-1] + 1) * 128])
                            P.tt(SM[:, 256:256 + nl], psA[:, 0:nl], MASKA[:, mo:mo + nl], ALU.add)
                        ntot = 256 + nl
                        sk = SINKB[:, l * 8 + head:l * 8 + head + 1]
                        ST4 = AR.alloc(8, F32)
                        P.reduce(ST4[:, 0:1], SM[:, 0:ntot], ALU.max)
                        P.ts(ST4[:, 1:2], ST4[:, 0:1], 0.125, sk, ALU.mult, ALU.max)
                        P.ts(ST4[:, 2:3], ST4[:, 1:2], -1.0, None, ALU.mult)
                        PB = AR.alloc(640, BF16)
                        P.act(PB[:, 0:ntot], SM[:, 0:ntot], AF.Exp, bias=ST4[:, 2:3], scale=0.125, accum_out=ST4[:, 3:4])
                        P.act(ST4[:, 4:5], ST4[:, 2:3], AF.Exp, bias=sk)
                        P.tt(ST4[:, 5:6], ST4[:, 3:4], ST4[:, 4:5], ALU.add)
                        P.recip(ST4[:, 6:7], ST4[:, 5:6])
                        PN = AR.alloc(640, BF16)
                        P.ts(PN[:, 0:ntot], PB[:, 0:ntot], ST4[:, 6:7], None, ALU.mult)
                        psT = rr().bitcast(BF16)
                        nk = len(keych)
                        for j in range(nk):
                            P.transpose(psT[:, j * 128:(j + 1) * 128], PN[:, j * 128:(j + 1) * 128], IDB)
                        PTs = AR.alloc(640, BF16)
                        P.copy(PTs[:, 0:nk * 128], psT[:, 0:nk * 128], eng="act")
                        for j, kc_ in enumerate(keych):
                            P.mm(acc[:, 0:128], VP[half][:, kc_, :], PTs[:, j * 128:(j + 1) * 128],
                                 start=(half == 0 and j == 0), stop=(half == 1 and j == nk - 1))
                        AR.release(m)
                    P.copy(ATs[:, c, qb * 128:(qb + 1) * 128], acc[:, 0:128])
            P.dma(brs_d[0].rearrange("(a p) t -> p a t", p=128), ATs)
            AR.release(m0)
            if stop_after in ("att", "att1"):
                break

            m0 = AR.mark()
            WS = next_piece(l, "SM1"); prefetch()
            WSv = wview(WS, 0, 8, 16)
            NMR = [AR.alloc(TT, F32, parts=8) for _ in range(2)]
            COLS = AR.alloc((NCH, 4, 8), F32)
            MTs = AR.alloc((4, TT), BF16)
            m_rows = AR.mark()
            LI = AR.alloc(TT, F32, parts=8); LF = AR.alloc(TT, F32, parts=8)
            for (t0, t1, isctx) in TBLOCKS:
                ps = proj_fm(WSv, 0, 8, t0, t1)
                P.act(LI[:, t0:t1], ps, AF.Identity, bias=MLB[:, 2 * l:2 * l + 1])
                ps = proj_fm(WSv, 8, 8, t0, t1)
                P.act(LF[:, t0:t1], ps, AF.Exp, bias=NMLB[:, 2 * l + 1:2 * l + 2], scale=-1.0)
            P.act(LF, LF, AF.Ln, bias=1.0)
            P.ts(LF, LF, -1.0, None, ALU.mult)
            m1 = AR.mark()
            for d in range(2):
                T1 = AR.alloc(TT, F32, parts=8); T4 = AR.alloc(TT, F32, parts=8)
                T2 = NMR[d]
                rows_scan(T1, LF, ALU.mult, ALU.add, ONEF, d == 1)
                pass
                P.tt(T1, LI, T1, ALU.subtract)
                rows_scan(T2, T1, ALU.max, ALU.max, T1, d == 1)
                P.tt(T4, LI, T1, ALU.subtract)
                P.tt(T4, T4, T2, ALU.add)
                P.act(T4, T4, AF.Exp, scale=-1.0)
                rows_to_cols(COLS, d * 2 + 0, T1)
                rows_to_cols(COLS, d * 2 + 1, T4)
                P.ts(T2, T2, -1.0, None, ALU.mult)
                AR.release(m1)
            AR.release(m_rows)
            if stop_after == "mlrows":
                P.dma(brs_d[1][0:128, 0:NCH * 32], COLS.rearrange("p a b c -> p (a b c)").bitcast(BF16)[:, 0:NCH * 32])
                P.dma(brs_d[1][128:136, :], NMR[0].bitcast(BF16)[:, 0:TT])
                break
            if not need_ctx:
                P.memset(MTs[:, :, 0:LC], 0.0)
            for h in range(4):
                WH = next_piece(l, f"MH{h}"); prefetch()
                WHv = wview(WH, 0, 8, 512)
                m2 = AR.mark()
                QTh = AR.alloc(TT, BF16); KTh = AR.alloc(TT, BF16)
                VA = AR.alloc((NCH, 132), BF16); OS = AR.alloc((NCH, 128), BF16)
                P.memset(VA[:, :, 128:129], 1.0)
                for (t0, t1, isctx) in TBLOCKS:
                    ps = proj_fm(WHv, 0, 128, t0, t1)
                    P.copy(QTh[:, t0:t1], ps, eng="act")
                    ps = proj_fm(WHv, 128, 128, t0, t1)
                    P.ts(KTh[:, t0:t1], ps, 128.0 ** -0.5, None, ALU.mult)
                for c in range(NCH):
                    ps = proj_tm(WHv, 256, 256, c)
                    P.copy(VA[:, c, 0:128], ps[:, 0:128])
                    P.act(OS[:, c, :], ps[:, 128:256], AF.Sigmoid)
                ring = [(AR.alloc(128, F32), AR.alloc(128, BF16)) for _ in range(6)]
                mring = [AR.alloc(128, F32) for _ in range(2)]
                BC = [AR.alloc(TT, F32) for _ in range(2)]
                for d in range(2):
                    build_bcast(NMR[d], d * 4 + h, BC[d])
                mst = {"rc": 0, "accs": None, "mc": 0}
                items = []
                for t in (range(NCH) if need_ctx else range(2, NCH)):
                    cl = contribs(t)
                    cnt = [sum(1 for x in cl if x[1] == d) for d in range(2)]
                    seen = [0, 0]
                    for ci_, (s_, d, masked) in enumerate(cl):
                        items.append(("c", t, s_, d, masked, seen[d] == 0, seen[d] == cnt[d] - 1, ci_ == 0))
                        seen[d] += 1
                    items.append(("fin", t))

                def ml_s1(it):
                    _, t, s_, d, masked, first, lastc, newt = it
                    pk = kqb()
                    P.mm(pk[:, 0:128], KTh[:, s_ * 128:(s_ + 1) * 128], QTh[:, t * 128:(t + 1) * 128])
                    return pk, None

                def ml_s2(it, hh):
                    _, t, s_, d, masked, first, lastc, newt = it
                    pk, psD = hh
                    if newt:
                        bk = accb()
                        mst["accs"] = [bk[:, 0:256], bk[:, 256:512]]
                    r = d * 4 + h
                    DT_, ST_ = ring[mst["rc"] % 6]
                    mst["rc"] += 1
                    src = BC[d][:, t * 128:(t + 1) * 128]
                    if masked:
                        tm = mring[mst["mc"] % 2]
                        mst["mc"] += 1
                        P.tt(tm, src, MASKD[d], ALU.add)
                        src = tm
                    P.act(DT_, src, AF.Exp, bias=COLS[:, s_, d * 2, r:r + 1])
                    P.tt(ST_, pk[:, 0:128], DT_, ALU.mult)
                    P.mm(mst["accs"][d][:, 0:129], ST_, VA[:, s_, 0:129], start=first, stop=lastc)

                def ml_fin(it):
                    t = it[1]
                    accs = mst["accs"]
                    m3 = AR.mark()
                    HD = []
                    for d in range(2):
                        r = d * 4 + h
                        A_ = AR.alloc(132, F32)
                        P.copy(A_[:, 0:129], accs[d][:, 0:129], eng="act")
                        S4 = AR.alloc(4, F32)
                        P.stt(S4[:, 0:1], A_[:, 128:129], -1.0, A_[:, 128:129], ALU.mult, ALU.max)
                        P.tt(S4[:, 1:2], S4[:, 0:1], COLS[:, t, d * 2 + 1, r:r + 1], ALU.max)
                        P.recip(S4[:, 2:3], S4[:, 1:2])
                        Hd = AR.alloc(128, F32)
                        P.ts(Hd, A_[:, 0:128], S4[:, 2:3], None, ALU.mult)
                        HD.append(Hd)
                    HS = AR.alloc(128, F32)
                    P.tt(HS, HD[0], HD[1], ALU.add)
                    S4 = AR.alloc(4, F32)
                    JK = AR.alloc(128, F32)
                    P.act(JK, HS, AF.Square, accum_out=S4[:, 0:1])
                    P.act(S4[:, 1:2], S4[:, 0:1], AF.Ln, bias=EPST, scale=1.0 / 128)
                    P.act(S4[:, 2:3], S4[:, 1:2], AF.Exp, scale=-0.5)
                    O1 = AR.alloc(128, F32)
                    P.stt(O1, HS, S4[:, 2:3], MLG[:, l * 512 + h * 128:l * 512 + (h + 1) * 128], ALU.mult, ALU.mult)
                    O2 = AR.alloc(128, BF16)
                    P.tt(O2, O1, OS[:, t, :], ALU.mult)
                    pT = rr().bitcast(BF16)
                    P.transpose(pT[:, 0:128], O2, IDB)
                    P.copy(MTs[:, h, t * 128:(t + 1) * 128], pT[:, 0:128], eng="act")
                    AR.release(m3)

                run_pipelined(items, ml_s1, ml_s2, ml_fin)
                AR.release(m2)
                if stop_after == "mlh1":
                    break
            P.dma(brs_d[1].rearrange("(a p) t -> p a t", p=128), MTs)
            AR.release(m0)
            if stop_after in ("ml", "mlh1"):
                break

            m0 = AR.mark()
            WS = next_piece(l, "SM2"); prefetch()
            WSv = wview(WS, 0, 8, 16)
            ACS = [AR.alloc(TT, F32, parts=8) for _ in range(2)]
            COLS = AR.alloc((NCH, 4, 8), F32)
            m1 = AR.mark()
            for d in range(2):
                DTr = AR.alloc(TT, F32, parts=8); T1 = AR.alloc(TT, F32, parts=8)
                for (t0, t1, isctx) in TBLOCKS:
                    ps = proj_fm(WSv, 8 * d, 8, t0, t1)
                    P.act(DTr[:, t0:t1], ps, AF.Exp, bias=SDT[:, 4 * l + d:4 * l + d + 1])
                P.act(DTr, DTr, AF.Ln, bias=1.0)
                P.ts(T1, DTr, SA[:, 4 * l + 2 + d:4 * l + 3 + d], None, ALU.mult)
                rows_scan(ACS[d], T1, ALU.mult, ALU.add, ONEF, d == 1)
                P.ts(T1, ACS[d], -1.0, None, ALU.mult)
                rows_to_cols(COLS, d * 2 + 0, T1)
                rows_to_cols(COLS, d * 2 + 1, DTr)
                AR.release(m1)
            YG = AR.alloc((NCH, 512), BF16)
            SSQ = AR.alloc((NCH, 8), F32)
            trange = list(range(NCH) if need_ctx else range(2, NCH))
            for g in range(2):
                WG = next_piece(l, f"SG{g}"); prefetch()
                WGv = wview(WG, 0, 8, 768)
                m2 = AR.mark()
                XTOK = AR.alloc((NCH, 256), BF16)
                BTg = AR.alloc(TT, BF16); CTg = AR.alloc(TT, BF16)
                ZS = AR.alloc((NCH, 256), BF16)
                m3 = AR.mark()
                PAD = AR.alloc(2312, F32); ACC = AR.alloc(2308, F32); XF = AR.alloc(TT, BF16)
                P.memset(PAD[:, 0:2], 0.0); P.memset(PAD[:, 258:262], 0.0); P.memset(PAD[:, 2310:2312], 0.0)
                for ch in range(4):
                    cidx = [g * 2, g * 2 + 1, 4 + g, 6 + g][ch]
                    cw = CW[:, (l * 8 + cidx) * 6:(l * 8 + cidx) * 6 + 6]
                    for (t0, t1, isctx) in TBLOCKS:
                        ps = proj_fm(WGv, ch * 128, 128, t0, t1)
                        o0 = 2 + t0 if isctx else 262 + (t0 - LC)
                        P.copy(PAD[:, o0:o0 + (t1 - t0)], ps, eng="act")
                    P.ts(ACC, PAD[:, 0:2308], cw[:, 0:1], None, ALU.mult)
                    for j in range(1, 5):
                        P.stt(ACC, PAD[:, j:j + 2308], cw[:, j:j + 1], ACC, ALU.mult, ALU.add)
                    dst = XF if ch < 2 else (BTg if ch == 2 else CTg)
                    P.act(dst[:, 0:LC], ACC[:, 0:LC], AF.Silu, bias=cw[:, 5:6])
                    P.act(dst[:, LC:TT], ACC[:, 260:2308], AF.Silu, bias=cw[:, 5:6])
                    if ch < 2:
                        for c0 in range(0, NCH, 4):
                            cs = list(range(c0, min(c0 + 4, NCH)))
                            pT = rr().bitcast(BF16)
                            for c in cs:
                                P.transpose(pT[:, (c - c0) * 128:(c - c0 + 1) * 128], XF[:, c * 128:(c + 1) * 128], IDB)
                            P.copy(XTOK[:, c0:c0 + len(cs), ch * 128:(ch + 1) * 128],
                                   pT[:, 0:len(cs) * 128].rearrange("p (a b) -> p a b", a=len(cs)))
                AR.release(m3)
                BTOK = None
                for c in range(NCH):
                    ps = proj_tm(WGv, 512, 256, c)
                    P.act(ZS[:, c, :], ps, AF.Silu)
                ring = [(AR.alloc(128, F32), AR.alloc(128, BF16)) for _ in range(6)]
                mring = [AR.alloc(128, F32) for _ in range(2)]
                BC = [AR.alloc(TT, F32) for _ in range(2)]
                sst = {"rc": 0, "acc": None, "mc": 0}
                for r4 in range(4):
                    hd = g * 4 + r4
                    for d in range(2):
                        build_bcast(ACS[d], hd, BC[d])
                    items = []
                    for t in trange:
                        cl = contribs(t)
                        for ci_, (s_, d, masked) in enumerate(cl):
                            items.append(("c", t, s_, d, masked, ci_ == 0, ci_ == len(cl) - 1))
                        items.append(("fin", t))

                    def sd_s1(it, hd=hd):
                        _, t, s_, d, masked, first, lastc = it
                        pk = kqb()
                        P.mm(pk[:, 0:128], BTg[:, s_ * 128:(s_ + 1) * 128], CTg[:, t * 128:(t + 1) * 128])
                        return pk, None

                    def sd_s2(it, hh, hd=hd, r4=r4):
                        _, t, s_, d, masked, first, lastc = it
                        pk, psD = hh
                        if first:
                            sst["acc"] = accb()
                        DT_, MT_ = ring[sst["rc"] % 6]
                        sst["rc"] += 1
                        src = BC[d][:, t * 128:(t + 1) * 128]
                        if masked:
                            tm = mring[sst["mc"] % 2]
                            sst["mc"] += 1
                            P.tt(tm, src, MASKD[d], ALU.add)
                            src = tm
                        P.act(DT_, src, AF.Exp, bias=COLS[:, s_, d * 2, hd:hd + 1])
                        P.stt(MT_, pk[:, 0:128], COLS[:, s_, d * 2 + 1, hd:hd + 1], DT_, ALU.mult, ALU.mult)
                        P.mm(sst["acc"][:, 0:64], MT_, XTOK[:, s_, r4 * 64:(r4 + 1) * 64], start=first, stop=lastc)

                    def sd_fin(it, hd=hd, r4=r4):
                        t = it[1]
                        acc = sst["acc"]
                        m4 = AR.mark()
                        Y1 = AR.alloc(64, F32)
                        P.tt(Y1, XTOK[:, t, r4 * 64:(r4 + 1) * 64],
                             SDK[:, l * 512 + hd * 64:l * 512 + (hd + 1) * 64], ALU.mult)
                        P.tt(Y1, Y1, acc[:, 0:64], ALU.add)
                        P.tt(YG[:, t, hd * 64:(hd + 1) * 64], Y1, ZS[:, t, r4 * 64:(r4 + 1) * 64], ALU.mult)
                        JK = AR.alloc(64, F32)
                        P.act(JK, YG[:, t, hd * 64:(hd + 1) * 64], AF.Square, accum_out=SSQ[:, t, hd:hd + 1])
                        AR.release(m4)

                    run_pipelined(items, sd_s1, sd_s2, sd_fin)
                AR.release(m2)
            STs = AR.alloc((4, TT), BF16)
            if not need_ctx:
                P.memset(STs[:, :, 0:LC], 0.0)
            for t in trange:
                m4 = AR.mark()
                S4 = AR.alloc(4, F32)
                P.reduce(S4[:, 0:1], SSQ[:, t, :], ALU.add)
                P.act(S4[:, 1:2], S4[:, 0:1], AF.Ln, bias=EPST, scale=1.0 / 512)
                P.act(S4[:, 2:3], S4[:, 1:2], AF.Exp, scale=-0.5)
                O2 = AR.alloc(512, BF16)
                P.stt(O2, YG[:, t, :], S4[:, 2:3], SNG[:, l * 512:(l + 1) * 512], ALU.mult, ALU.mult)
                pT = rr().bitcast(BF16)
                for ch in range(4):
                    P.transpose(pT[:, ch * 128:(ch + 1) * 128], O2[:, ch * 128:(ch + 1) * 128], IDB)
                P.copy(STs[:, :, t * 128:(t + 1) * 128], pT[:, 0:512].rearrange("p (a b) -> p a b", a=4), eng="act")
                AR.release(m4)
            P.dma(brs_d[2].rearrange("(a p) t -> p a t", p=128), STs)
            AR.release(m0)
            if stop_after == "ssd":
                break

            m0 = AR.mark()
            BR = [AR.alloc((4, TT), BF16) for _ in range(3)]
            for i in range(3):
                P.dma(BR[i], brs_d[i].rearrange("(a p) t -> p a t", p=128))
            for i in range(8):
                WMp = next_piece(l, f"MG{i}"); prefetch()
                Wg = wview(WMp, 0, 8, 384)
                Wo_ = WMp[:, 3072:3072 + 1536].rearrange("p (b a c) -> p b a c", b=3, a=4)
                m_i = AR.mark()
                YTi = AR.alloc(TT, BF16)
                for (t0, t1, isctx) in blocks:
                    nt = t1 - t0
                    m1 = AR.mark()
                    YA = AR.alloc(nt, F32); TM = AR.alloc(nt, F32)
                    for br in range(3):
                        psG = proj_fm(Wg, br * 128, 128, t0, t1)
                        SG = AR.alloc(nt, F32)
                        P.act(SG, psG, AF.Sigmoid)
                        psO = rr()
                        for kc in range(4):
                            P.mm(psO[:, 0:nt], Wo_[:, br, kc, :], BR[br][:, kc, t0:t1], start=(kc == 0), stop=(kc == 3))
                        if br == 0:
                            P.tt(YA, SG, psO[:, 0:nt], ALU.mult)
                        elif br == 1:
                            P.tt(TM, SG, psO[:, 0:nt], ALU.mult)
                            P.tt(YA, YA, TM, ALU.add, eng="pool")
                        else:
                            P.tt(TM, SG, psO[:, 0:nt], ALU.mult)
                            P.tt(YTi[:, t0:t1], YA, TM, ALU.add, eng="pool")
                    AR.release(m1)
                P.dma(yt_d[i * 128:(i + 1) * 128, blocks[0][0]:TT], YTi[:, blocks[0][0]:TT])
                AR.release(m_i)
            AR.release(m0)

            m_x = AR.mark()
            X = AR.alloc((8, TT), F32)
            if l == 0:
                P.dma(X[:, :, 0:LC], ctxT_d[b].rearrange("(a p) t -> p a t", p=128))
                P.dma(X[:, :, LC:TT], xT_d[b].rearrange("(a p) t -> p a t", p=128))
            else:
                P.dma(X, xs_d.rearrange("(a p) t -> p a t", p=128))
            WOp = next_piece(l, "WO"); prefetch()
            WOv = wview(WOp, 0, 8, 1024)
            for (t0, t1, isctx) in blocks:
                nt = t1 - t0
                who = 2 if isctx else b
                m1 = AR.mark()
                Yb = AR.alloc((8, nt), BF16)
                P.dma(Yb, yt_d.rearrange("(a p) t -> p a t", p=128)[:, :, t0:t1])
                for o in range(8):
                    ps = rr()
                    for kc in range(8):
                        P.mm(ps[:, 0:nt], WOv[:, kc, o * 128:(o + 1) * 128], Yb[:, kc, :], start=(kc == 0), stop=(kc == 7))
                    P.stt(X[:, o, t0:t1], ps[:, 0:nt], modv(l, 2, o, who), X[:, o, t0:t1], ALU.mult, ALU.add)
                AR.release(m1)
            if stop_after == "wo":
                break

            norm_to(X, l, 1, b, H, blocks=blocks)
            for q in range(11):
                WF = next_piece(l, f"FF{q}"); prefetch()
                Wu = wview(WF, 0, 8, 512)
                Wd = wview(WF, 4096, 2, 1024)
                for (t0, t1, isctx) in blocks:
                    nt = t1 - t0
                    who = 2 if isctx else b
                    m1 = AR.mark()
                    A_ = AR.alloc((2, nt), BF16)
                    for jj in range(2):
                        psg = proj_fm(Wu, jj * 128, 128, t0, t1)
                        SGt = AR.alloc(nt, F32)
                        P.act(SGt, psg, AF.Silu)
                        psu = proj_fm(Wu, 256 + jj * 128, 128, t0, t1)
                        P.tt(A_[:, jj, :], SGt, psu, ALU.mult)
                    for o in range(8):
                        ps = rr()
                        for jj in range(2):
                            P.mm(ps[:, 0:nt], Wd[:, jj, o * 128:(o + 1) * 128], A_[:, jj, :], start=(jj == 0), stop=(jj == 1))
                        P.stt(X[:, o, t0:t1], ps[:, 0:nt], modv(l, 5, o, who), X[:, o, t0:t1], ALU.mult, ALU.add)
                    AR.release(m1)
            if last:
                for (t0, t1, isctx) in [(LC + 256 * i, LC + 256 * (i + 1), False) for i in range(8)]:
                    m1 = AR.mark()
                    OUT = AR.alloc((8, t1 - t0), F32)
                    norm_to(X, l, 0, b, OUT, gvec=GN[:, 2 * L * 8:(2 * L + 1) * 8], blocks=[(t0, t1, isctx)], toff=t0)
                    P.dma(outT_d[b].rearrange("(a p) t -> p a t", p=128)[:, :, t0 - LC:t1 - LC], OUT)
                    AR.release(m1)
                AR.release(m_x)
        else:
            continue
        break

    outs = [d for d in P.dmas if False]
    final = [d for d in P.dmas]
    P.emit(final_wait_dmas=final[-ND:])
    AR.P = P
    return nc, AR


_CACHE = {}


def host_prep(inputs, NB=2, L=DEPTH):
    f32 = np.float32
    g = {k: np.asarray(v, dtype=f32) for k, v in inputs.items()}
    consts = _host_consts()
    wl = [_host_layer_weights(l, g["w_in"], g["w_att_out"], g["w_ml_out"], g["w_ssd_out"], g["w_o"], g["w_up"], g["w_down"])
          for l in range(L)]
    wmod = np.stack([_k8(g["w_mod"][l], np.arange(6144)) for l in range(L)])
    bmod = np.stack([g["b_mod"][l].reshape(48, 128).T for l in range(L)])
    gn = np.concatenate([np.concatenate([g["g_norm1"][l].reshape(8, 128).T, g["g_norm2"][l].reshape(8, 128).T], 1)
                         for l in range(L)] + [g["g_final"].reshape(8, 128).T], 1)
    sinkb = np.broadcast_to(g["att_sink"][:L].reshape(1, L * 8), (128, L * 8))
    mlb = np.concatenate([np.stack([g["ml_i_bias"][l].reshape(8), g["ml_f_bias"][l].reshape(8)], 1) for l in range(L)], 1)
    mlgain = np.broadcast_to(g["ml_head_gain"][:L].reshape(1, L * 512), (128, L * 512))
    cw = []
    for l in range(L):
        for ci in range(8):
            ch = slice(ci * 128, (ci + 1) * 128)
            cw.append(np.concatenate([g["ssd_conv_w"][l][:, ch].T, g["ssd_conv_b"][l][ch][:, None]], 1))
    convw = np.concatenate(cw, 1)
    sdt = np.concatenate([np.stack([g["ssd_dt_bias"][l][0], g["ssd_dt_bias"][l][1], g["ssd_a_log"][l][0], g["ssd_a_log"][l][1]], 1)
                          for l in range(L)], 1)
    ssdskip = np.broadcast_to(np.repeat(g["ssd_d"][:L], 64, axis=1).reshape(1, L * 512), (128, L * 512))
    ssdng = np.broadcast_to(g["ssd_norm_gain"][:L].reshape(1, L * 512), (128, L * 512))
    shared = dict(wmod=wmod, bmod=bmod, gn=gn, sinkb=sinkb, mlb=mlb, mlgain=mlgain, convw=convw, ssddt=sdt,
                  ssdskip=ssdskip, ssdng=ssdng, **consts)
    for l in range(L):
        shared[f"wl{l}"] = wl[l]
    shared = {k: np.ascontiguousarray(v, dtype=f32) for k, v in shared.items()}
    ncores = g["x"].shape[0] // NB
    maps = []
    for c in range(ncores):
        bs = slice(c * NB, (c + 1) * NB)
        m = dict(shared)
        m["xT"] = np.ascontiguousarray(g["x"][bs].transpose(0, 2, 1))
        m["ctxT"] = np.ascontiguousarray(g["ctx"][bs].transpose(0, 2, 1))
        cc = np.stack([g["c"][c * NB + i] if i < NB else g["c_ctx"] for i in range(2)] + [g["c_ctx"]], 0)
        if NB == 1:
            cc = np.stack([g["c"][c], g["c"][c], g["c_ctx"]], 0)
        silu_in = cc.reshape(3, 8, 128).transpose(2, 1, 0).reshape(128, 24)
        m["cT"] = np.ascontiguousarray(silu_in)
        maps.append(m)
    return maps


def kernel(**inputs):
    NB = 2
    if "nc" not in _CACHE:
        _CACHE["nc"] = build(NB=NB)[0]
    nc = _CACHE["nc"]
    maps = host_prep(inputs, NB=NB)
    res = run_bass_kernel_spmd(nc, maps, core_ids=list(range(len(maps))))
    outs = [r["outT"] for r in res.results]
    full = np.concatenate(outs, 0)
    return np.ascontiguousarray(full.transpose(0, 2, 1)).astype(np.float32)
```

```python
import numpy as np
import concourse.bass as bass
import concourse.mybir as mybir

F32 = mybir.dt.float32
BF16 = mybir.dt.bfloat16
AF = mybir.ActivationFunctionType
ALU = mybir.AluOpType
AX = mybir.AxisListType

NS = 4
ND = 24


class Instr:
    __slots__ = ("eng", "idx", "fn", "dma", "deps", "signal", "clock", "dma_id", "sidx")

    def __init__(self, eng, idx, fn, dma):
        self.eng = eng
        self.idx = idx
        self.fn = fn
        self.dma = dma
        self.deps = []
        self.signal = False
        self.clock = None
        self.dma_id = None
        self.sidx = None


def ap_box(ap):
    t = ap.tensor
    dims = list(ap.ap)
    pstride, pn = dims[0]
    shape = list(t.shape)
    per_part = 1
    for s in shape[1:]:
        per_part *= int(s)
    off = int(ap.offset)
    if pstride == 0:
        pstride_eff = per_part
    else:
        pstride_eff = pstride
    p0 = off // per_part
    f0 = off - p0 * per_part
    lo = 0
    hi = 0
    for st, cn in dims[1:]:
        ext = (int(cn) - 1) * int(st)
        if ext >= 0:
            hi += ext
        else:
            lo += ext
    esz = mybir.dt.size(t.dtype) if hasattr(mybir.dt, "size") else None
    return (t.name, p0, p0 + int(pn), f0 + lo, f0 + hi + 1)


class Prog:
    def __init__(self, nc):
        self.nc = nc
        self.engs = {"pe": nc.tensor, "dve": nc.vector, "act": nc.scalar, "pool": nc.gpsimd, "sp": nc.sync}
        self.ins = {e: [] for e in self.engs}
        self.track = {}
        self.known = {e: {} for e in self.engs}
        self.ndma = 0
        self.dmas = []
        self.esize = {}
        self.skip_dram = set()

    def _box(self, ap):
        t = ap.tensor
        sp = str(type(t).__name__)
        if sp.startswith("DRam"):
            if t.name in self.skip_dram:
                return None
            off = int(ap.offset); lo = 0; hi = 0
            for st, cn in ap.ap:
                ext = (int(cn) - 1) * int(st)
                if ext >= 0: hi += ext
                else: lo += ext
            es = {F32: 4, BF16: 2}.get(t.dtype, 4)
            return (t.name, 0, 1, (off + lo) * es, (off + hi + 1) * es)
        name, p0, p1, f0, f1 = ap_box(ap)
        if sp.startswith("PSum"):
            return (name, 0, 128, 0, 2048)
        es = {F32: 4, BF16: 2}.get(t.dtype, None)
        if es is None:
            es = int(ap.nbytes // max(1, ap.size)) if hasattr(ap, "nbytes") else 4
        return (name, p0, p1, f0 * es, f1 * es)

    def _collect(self, ins, box, write):
        name, p0, p1, f0, f1 = box
        BIN = 4096 if not name.startswith("dr_") else (1 << 22)
        bins = range(f0 // BIN, (f1 - 1) // BIN + 1)
        tr = self.track.setdefault(name, {})
        deps = []
        seen = set()
        newrec = (ins, write, p0, p1, f0, f1)
        for bi in bins:
            recs = tr.get(bi)
            if recs is None:
                tr[bi] = [newrec]
                continue
            keep = self._scan(recs, ins, write, p0, p1, f0, f1, deps, seen, bi * BIN, (bi + 1) * BIN,
                              name_is_psum=name.startswith("ps"))
            keep.append(newrec)
            tr[bi] = keep
        return deps

    def _scan(self, recs, ins, write, p0, p1, f0, f1, deps, seen, b0, b1, name_is_psum=False):
        keep = []
        for r in recs:
            rin, rw, q0, q1, g0, g1 = r
            ov = not (q1 <= p0 or p1 <= q0 or g1 <= f0 or f1 <= g0)
            if not ov:
                keep.append(r)
                continue
            if (write or rw or (name_is_psum and rin.eng != ins.eng)) and id(rin) not in seen:
                seen.add(id(rin))
                deps.append((rin, rw))
            covered = write and p0 <= q0 and q1 <= p1 and f0 <= max(g0, b0) and min(g1, b1) <= f1
            if covered:
                continue
            if (not write) and (not rw) and rin.eng == ins.eng and not ins.dma and not rin.dma \
                    and q0 == p0 and q1 == p1 and g0 == f0 and g1 == f1:
                continue
            keep.append(r)
        return keep

    def add(self, eng, fn, reads=(), writes=(), dma=False):
        ins = Instr(eng, len(self.ins[eng]), fn, dma)
        alld = {}
        for ap in reads:
            b = self._box(ap)
            if b is None:
                continue
            for d, dw in self._collect(ins, b, False):
                alld[id(d)] = (d, True)
        for ap in writes:
            b = self._box(ap)
            if b is None:
                continue
            for d, dw in self._collect(ins, b, True):
                prev = alld.get(id(d))
                alld[id(d)] = (d, (prev[1] if prev else False) or False)
        known = self.known[eng]
        if dma:
            ins.dma_id = self.ndma
            self.ndma += 1
            self.dmas.append(ins)
            if ins.dma_id >= ND:
                d = self.dmas[ins.dma_id - ND]
                alld[id(d)] = (d, True)
        need = []
        for d, raw in sorted(alld.values(), key=lambda x: -(x[0].dma_id if x[0].dma else x[0].idx)):
            if d is ins:
                continue
            if d.dma:
                key = ("dma", d.dma_id % ND)
                val = d.dma_id // ND + 1
                if known.get(key, 0) >= val:
                    continue
                need.append(d)
            else:
                if d.eng == eng and not dma:
                    if eng == "pe":
                        continue
                    if not raw:
                        continue
                if known.get(d.eng, -1) >= d.idx:
                    continue
                need.append(d)
        for d in need:
            d.signal = True
            if d.dma:
                key = ("dma", d.dma_id % ND)
                known[key] = max(known.get(key, 0), d.dma_id // ND + 1)
            else:
                known[d.eng] = max(known.get(d.eng, -1), d.idx)
            if d.clock:
                for k, v in d.clock.items():
                    if known.get(k, -1) < v:
                        known[k] = v
        ins.deps = need
        ins.clock = dict(known)
        if not dma:
            ins.clock[eng] = max(ins.clock.get(eng, -1), ins.idx - 1)
        self.ins[eng].append(ins)
        return ins

    def mm(self, out, lhsT, rhs, start=True, stop=True, **kw):
        return self.add("pe", lambda e: e.matmul(out, lhsT, rhs, start=start, stop=stop, **kw),
                        reads=[lhsT, rhs], writes=[out])

    def transpose(self, out, in_, ident):
        return self.add("pe", lambda e: e.transpose(out, in_, ident), reads=[in_, ident], writes=[out])

    def act(self, out, in_, func, bias=None, scale=None, accum_out=None, eng="act"):
        kw = {}
        reads = [in_]
        if bias is not None:
            kw["bias"] = bias
            if not isinstance(bias, (int, float)):
                reads.append(bias)
        if scale is not None:
            kw["scale"] = scale
            if not isinstance(scale, (int, float)):
                reads.append(scale)
        writes = [out]
        if accum_out is not None:
            kw["accum_out"] = accum_out
            writes.append(accum_out)
        return self.add(eng, lambda e: e.activation(out, in_, func, **kw), reads=reads, writes=writes)

    def tt(self, out, in0, in1, op, eng="dve"):
        return self.add(eng, lambda e: e.tensor_tensor(out, in0, in1, op), reads=[in0, in1], writes=[out])

    def ts(self, out, in0, s1, s2, op0, op1=None, eng="dve", accum_out=None):
        reads = [in0]
        for s in (s1, s2):
            if s is not None and not isinstance(s, (int, float)):
                reads.append(s)
        writes = [out]
        kw = {}
        if accum_out is not None:
            kw["accum_out"] = accum_out
            writes.append(accum_out)
        if op1 is None:
            return self.add(eng, lambda e: e.tensor_scalar(out, in0, s1, None, op0, **kw), reads=reads, writes=writes)
        return self.add(eng, lambda e: e.tensor_scalar(out, in0, s1, s2, op0, op1, **kw), reads=reads, writes=writes)

    def stt(self, out, in0, scalar, in1, op0, op1, eng="dve"):
        reads = [in0, in1]
        if not isinstance(scalar, (int, float)):
            reads.append(scalar)
        return self.add(eng, lambda e: e.scalar_tensor_tensor(out, in0, scalar, in1, op0, op1),
                        reads=reads, writes=[out])

    def copy(self, out, in_, eng="dve"):
        if eng == "act":
            return self.add(eng, lambda e: e.copy(out, in_), reads=[in_], writes=[out])
        return self.add(eng, lambda e: e.tensor_copy(out, in_), reads=[in_], writes=[out])

    def reduce(self, out, in_, op, axis=AX.X, eng="dve"):
        return self.add(eng, lambda e: e.tensor_reduce(out, in_, axis, op), reads=[in_], writes=[out])

    def scan(self, out, d0, d1, initial, op0, op1):
        reads = [d0, d1]
        if not isinstance(initial, (int, float)):
            reads.append(initial)
        return self.add("dve", lambda e: e.tensor_tensor_scan(out, d0, d1, initial, op0, op1),
                        reads=reads, writes=[out])

    def recip(self, out, in_):
        return self.add("dve", lambda e: e.reciprocal(out, in_), reads=[in_], writes=[out])

    def memset(self, ap, val, eng="dve"):
        return self.add(eng, lambda e: e.memset(ap, val), reads=[], writes=[ap])

    def dma(self, out, in_, q="sp", **kw):
        return self.add(q, lambda e: e.dma_start(out=out, in_=in_, **kw), reads=[in_], writes=[out], dma=True)

    def emit(self, final_wait_dmas=()):
        nc = self.nc
        dsem = [nc.alloc_semaphore(name=f"d_{i}") for i in range(ND)]
        nsig = {}
        for e, lst in self.ins.items():
            c = 0
            for ins in lst:
                if ins.dma:
                    continue
                if ins.signal:
                    ins.sidx = c
                    c += 1
            nsig[e] = c
        nsr = {e: max(1, -(-nsig[e] // 1800)) for e in self.engs}
        self.nsr = nsr
        rings = {e: [nc.alloc_semaphore(name=f"r_{e}_{i}") for i in range(nsr[e])] for e in self.engs}
        prog = self

        def body(ename):
            def f(engine):
                for ins in prog.ins[ename]:
                    for d in ins.deps:
                        if d.dma:
                            engine.wait_ge(dsem[d.dma_id % ND], 16 * (d.dma_id // ND + 1))
                        else:
                            engine.wait_ge(rings[d.eng][d.sidx % nsr[d.eng]], d.sidx // nsr[d.eng] + 1)
                    r = ins.fn(engine)
                    if ins.dma:
                        r.then_inc(dsem[ins.dma_id % ND], 16)
                    elif ins.signal:
                        r.then_inc(rings[ename][ins.sidx % nsr[ename]], 1)
                if ename == "sp":
                    for d in final_wait_dmas:
                        engine.wait_ge(dsem[d.dma_id % ND], 16 * (d.dma_id // ND + 1))
            return f

        with nc.Block() as block:
            block.sync(body("sp"))
            block.tensor(body("pe"))
            block.vector(body("dve"))
            block.scalar(body("act"))
            block.gpsimd(body("pool"))

from concourse.bass_utils import run_bass_kernel_spmd
import math

D = 1024
T = 2048
LC = 256
TT = T + LC
NCH = TT // 128
DEPTH = 2
DFF = 2816
EPS = 1e-6
NEG = -1.0e30
SLOT_EL = 8192

TBLOCKS = [(0, 256, True), (256, 768, False), (768, 1280, False), (1280, 1792, False), (1792, 2304, False)]


def _piece_list():
    pcs = [("A1", 8 * 768), ("A2", 8 * 640), ("SM1", 8 * 16)]
    for h in range(4):
        pcs.append((f"MH{h}", 8 * 512))
    pcs.append(("SM2", 8 * 16))
    for g in range(2):
        pcs.append((f"SG{g}", 8 * 768))
    for i in range(8):
        pcs.append((f"MG{i}", 8 * 384 + 3 * 4 * 128))
    pcs.append(("WO", 8 * 1024))
    for q in range(11):
        pcs.append((f"FF{q}", 8 * 512 + 2 * 1024))
    return pcs


def _k8(w, cols):
    sub = w[:, cols]
    K, n = sub.shape
    return np.ascontiguousarray(sub.reshape(K // 128, 128, n).transpose(1, 0, 2)).reshape(128, -1)


def _host_layer_weights(l, w_in, w_att_out, w_ml_out, w_ssd_out, w_o, w_up, w_down):
    wi = w_in[l]
    ar = np.arange
    qcols = np.concatenate([np.concatenate([c * 64 + ar(64), (4 + c) * 64 + ar(64)]) for c in range(4)])
    perm64 = np.concatenate([16 + ar(16), ar(16), 48 + ar(16), 32 + ar(16)])
    qperm = (qcols // 64) * 64 + perm64[qcols % 64]
    kcols = 512 + ar(128)
    kperm = 512 + (ar(128) // 64) * 64 + perm64[ar(128) % 64]
    vcols = 640 + ar(128)
    out = []
    out.append(_k8(wi, np.concatenate([qcols, kcols, vcols])))
    out.append(_k8(wi, np.concatenate([qperm, kperm])))
    g0 = 2816
    out.append(_k8(wi, np.concatenate([g0 + ar(4), g0 + 8 + ar(4), g0 + 4 + ar(4), g0 + 12 + ar(4)])))
    for h in range(4):
        out.append(_k8(wi, np.concatenate([768 + h * 128 + ar(128), 1280 + h * 128 + ar(128),
                                           1792 + h * 128 + ar(128), 2304 + h * 128 + ar(128)])))
    z0, x0, d0, m0 = 2832, 3344, 4368, 4384
    out.append(_k8(wi, d0 + ar(16)))
    for g in range(2):
        out.append(_k8(wi, np.concatenate([x0 + g * 256 + ar(256), x0 + 512 + g * 128 + ar(128),
                                           x0 + 768 + g * 128 + ar(128), z0 + g * 256 + ar(256)])))
    arow = np.concatenate([np.concatenate([c * 64 + ar(64), (4 + c) * 64 + ar(64)]) for c in range(4)])
    for i in range(8):
        cc = i * 128 + ar(128)
        a = _k8(wi, np.concatenate([m0 + cc, m0 + 1024 + cc, m0 + 2048 + cc]))
        b0 = _k8(w_att_out[l][arow], cc)
        b1 = _k8(w_ml_out[l], cc)
        b2 = _k8(w_ssd_out[l], cc)
        out.append(np.concatenate([a, b0, b1, b2], axis=1))
    out.append(_k8(w_o[l], ar(1024)))
    for q in range(11):
        j0 = 2 * q * 128
        a = _k8(w_up[l], np.concatenate([j0 + ar(256), DFF + j0 + ar(256)]))
        b = _k8(w_down[l][j0:j0 + 256], ar(1024))
        out.append(np.concatenate([a, b], axis=1))
    cat = np.concatenate(out, axis=1)
    return np.ascontiguousarray(cat, dtype=np.float32)


def _host_consts():
    ar = np.arange
    t = ar(T)
    inv = (10000.0 ** (-ar(16, dtype=np.float32) / 16)).astype(np.float32)
    row = (t // 64).astype(np.float32)
    col = (t % 64).astype(np.float32)
    ang_r = row[None, :] * inv[:, None]
    ang_c = col[None, :] * inv[:, None]
    cos64 = np.concatenate([np.cos(ang_r), np.cos(ang_r), np.cos(ang_c), np.cos(ang_c)], 0)
    sin64 = np.concatenate([-np.sin(ang_r), np.sin(ang_r), -np.sin(ang_c), np.sin(ang_c)], 0)
    cosT = np.concatenate([cos64, cos64], 0).astype(np.float32)
    sinT = np.concatenate([sin64, sin64], 0).astype(np.float32)
    q = ar(128)[:, None]
    k = ar(128)[None, :]
    maskA = np.zeros((128, 384), np.float32)
    maskA[:, 0:128] = np.where(k >= q, 0.0, -1.0e5)
    maskA[:, 256:384] = np.where(k <= q, 0.0, -1.0e5)
    s = ar(128)[:, None]
    tt = ar(128)[None, :]
    maskF = np.where(s <= tt, 0.0, NEG).astype(np.float32)
    maskB = np.where(s >= tt, 0.0, NEG).astype(np.float32)
    sel = np.zeros((8, 8, 128), np.float32)
    for r in range(8):
        sel[r, r, :] = 1.0
    ident = np.eye(128, dtype=np.float32)
    return dict(cosT=cosT, sinT=sinT, maskA=maskA, maskF=maskF, maskB=maskB, sel=sel.reshape(8, 1024), ident=ident)


class Arena:
    def __init__(self, nc, nbytes):
        self.t = nc.alloc_sbuf_tensor("arena", [128, nbytes // 2], BF16)
        self.top = 0
        self.cap = nbytes
        self.peak = 0

    def alloc(self, free, dtype, parts=128):
        if isinstance(free, int):
            free = (free,)
        n = 1
        for f in free:
            n *= f
        es = 4 if dtype == F32 else 2
        off = (self.top + 63) // 64 * 64
        self.top = off + n * es
        self.peak = max(self.peak, self.top)
        assert self.top <= self.cap, f"arena overflow {self.top} > {self.cap}"
        v = self.t[0:parts, off // 2:(off + n * es) // 2]
        if dtype == F32:
            v = v.bitcast(F32)
        if len(free) == 2:
            v = v.rearrange("p (a b) -> p a b", a=free[0])
        elif len(free) == 3:
            v = v.rearrange("p (a b c) -> p a b c", a=free[0], b=free[1])
        return v

    def mark(self):
        return self.top

    def release(self, m):
        self.top = m


def build(NB=2, L=DEPTH, dbg=False, stop_after=None):
    nc = bass.Bass("TRN2", target_bir_lowering=False)
    P = Prog(nc)

    def din(name, shape, dt=F32):
        P.skip_dram.add(name)
        return nc.dram_tensor(name, list(shape), dt, kind="ExternalInput").ap()

    def dscr(name, shape, dt):
        return nc.dram_tensor(name, list(shape), dt, kind="ExternalOutput" if dbg else "Internal").ap()

    pcs = _piece_list()
    poff = {}
    o = 0
    for nm, ne in pcs:
        poff[nm] = (o, ne)
        o += ne
    WTOT = o

    xT_d = din("xT", [NB, D, T])
    ctxT_d = din("ctxT", [NB, D, LC])
    cT_d = din("cT", [128, 24])
    wmod_d = din("wmod", [L, 128, 8 * 6144])
    bmod_d = din("bmod", [L, 128, 48])
    gn_d = din("gn", [128, (2 * L + 1) * 8])
    wl_d = [din(f"wl{l}", [128, WTOT]) for l in range(L)]
    cos_d = din("cosT", [128, T]); sin_d = din("sinT", [128, T])
    maskA_d = din("maskA", [128, 384]); maskF_d = din("maskF", [128, 128]); maskB_d = din("maskB", [128, 128])
    sel_d = din("sel", [8, 1024]); ident_d = din("ident", [128, 128])
    sink_d = din("sinkb", [128, L * 8])
    mlb_d = din("mlb", [8, L * 2])
    mlg_d = din("mlgain", [128, L * 512])
    cw_d = din("convw", [128, L * 8 * 6])
    sdt_d = din("ssddt", [8, L * 4])
    sdk_d = din("ssdskip", [128, L * 512])
    sng_d = din("ssdng", [128, L * 512])
    outT_d = nc.dram_tensor("outT", [NB, D, T], F32, kind="ExternalOutput").ap()

    brs_d = [dscr(f"dr_br{i}", [512, TT], BF16) for i in range(3)]
    yt_d = dscr("dr_yt", [D, TT], BF16)
    xs_d = dscr("dr_xs", [D, TT], F32)
    hdbg_d = dscr("dr_h", [D, TT], BF16) if dbg else None

    AR = Arena(nc, 204 * 1024)
    banks = [nc.alloc_psum_tensor(f"ps{i}", [128, 512], F32) for i in range(8)]
    st = {"rr": 0, "acc": 0, "kq": 0, "rrn": 3}

    def rr():
        b = banks[st["rr"] % st["rrn"]]
        st["rr"] += 1
        return b

    def kqb():
        b = banks[3 + st["kq"] % 3]
        st["kq"] += 1
        return b

    def accb():
        b = banks[6 + st["acc"] % 2]
        st["acc"] += 1
        return b

    IDF = AR.alloc(128, F32); P.dma(IDF, ident_d)
    IDB = AR.alloc(128, BF16); P.dma(IDB, ident_d, q="pool")
    ONESB = AR.alloc(128, BF16); P.memset(ONESB, 1.0)
    COS = AR.alloc(T, BF16); P.dma(COS.rearrange("p (a b) -> p a b", b=512), cos_d.rearrange("p (a b) -> p a b", b=512), q="pool")
    SIN = AR.alloc(T, BF16); P.dma(SIN.rearrange("p (a b) -> p a b", b=512), sin_d.rearrange("p (a b) -> p a b", b=512), q="pool")
    MASKA = AR.alloc(384, F32); P.dma(MASKA, maskA_d)
    MASKF = AR.alloc(128, F32); P.dma(MASKF, maskF_d)
    MASKB = AR.alloc(128, F32); P.dma(MASKB, maskB_d)
    SEL = AR.alloc((8, 128), F32, parts=8); P.dma(SEL, sel_d.rearrange("p (a b) -> p a b", a=8))
    SINKB = AR.alloc(L * 8, F32); P.dma(SINKB, sink_d)
    MLB = AR.alloc(L * 2, F32, parts=8); P.dma(MLB, mlb_d)
    NMLB = AR.alloc(L * 2, F32, parts=8); P.ts(NMLB, MLB, -1.0, None, ALU.mult)
    MLG = AR.alloc(L * 512, F32); P.dma(MLG, mlg_d)
    CW = AR.alloc(L * 48, F32); P.dma(CW, cw_d)
    SDT = AR.alloc(L * 4, F32, parts=8); P.dma(SDT, sdt_d)
    SA = AR.alloc(L * 4, F32, parts=8)
    P.act(SA, SDT, AF.Exp)
    P.ts(SA, SA, -1.0, None, ALU.mult)
    SDK = AR.alloc(L * 512, BF16); P.dma(SDK, sdk_d, q="pool")
    SNG = AR.alloc(L * 512, F32); P.dma(SNG, sng_d)
    GN = AR.alloc((2 * L + 1) * 8, F32); P.dma(GN, gn_d)
    EPST = AR.alloc(1, F32); P.memset(EPST, EPS)
    ONEF = AR.alloc(TT, F32, parts=8); P.memset(ONEF, 1.0)
    MOD = [AR.alloc((48, 3), F32) for _ in range(L)]
    GS = [AR.alloc((2, 8, 3), F32) for _ in range(L)]
    H = AR.alloc((8, TT), BF16)
    SLOTS = [AR.alloc(SLOT_EL, BF16) for _ in range(2)]

    wq = []
    for b in range(NB):
        for l in range(L):
            for nm, ne in pcs:
                wq.append((l, nm))
    wst = {"issued": 0, "cur": -1}

    def issue_next():
        i = wst["issued"]
        if i >= len(wq):
            return
        l, nm = wq[i]
        o, ne = poff[nm]
        CH = 512 if ne % 512 == 0 else 128
        P.dma(SLOTS[i % 2][:, 0:ne].rearrange("p (a b) -> p a b", b=CH),
              wl_d[l][:, o:o + ne].rearrange("p (a b) -> p a b", b=CH), q="pool")
        wst["issued"] += 1

    def next_piece(l, nm):
        wst["cur"] += 1
        i = wst["cur"]
        assert wq[i] == (l, nm), (wq[i], l, nm)
        while wst["issued"] <= i:
            issue_next()
        return SLOTS[i % 2]

    def prefetch():
        if wst["issued"] <= wst["cur"] + 1:
            issue_next()

    def wview(slot, off, kc, n):
        return slot[:, off:off + kc * n].rearrange("p (a b) -> p a b", a=kc)

    CT = AR.alloc(24, F32); P.dma(CT, cT_d)
    SC = AR.alloc(24, F32); P.act(SC, CT, AF.Silu)
    BM = AR.alloc(L * 48, F32)
    for l in range(L):
        P.dma(BM[:, l * 48:(l + 1) * 48], bmod_d[l])
    for l in range(L):
        for pc in range(12):
            WM = SLOTS[pc % 2].bitcast(F32).rearrange("p (a b) -> p a b", a=8)
            P.dma(WM, wmod_d[l].rearrange("p (a b) -> p a b", a=8)[:, :, pc * 512:(pc + 1) * 512])
            for cc in range(4):
                ps = rr()
                for kc in range(8):
                    P.mm(ps[:, 0:3], WM[:, kc, cc * 128:(cc + 1) * 128], SC[:, kc * 3:(kc + 1) * 3],
                         start=(kc == 0), stop=(kc == 7))
                col = pc * 4 + cc
                P.ts(MOD[l][:, col, :], ps[:, 0:3], BM[:, l * 48 + col:l * 48 + col + 1], None, ALU.add)
        for n in range(2):
            for kc in range(8):
                P.ts(GS[l][:, n, kc, :], MOD[l][:, (1 + 3 * n) * 8 + kc, :], 1.0,
                     GN[:, (2 * l + n) * 8 + kc:(2 * l + n) * 8 + kc + 1], ALU.add, ALU.mult)

    def modv(l, which, kc, who):
        return MOD[l][:, which * 8 + kc, who:who + 1]

    def norm_to(Xv, l, n, b, dst, gvec=None, blocks=TBLOCKS, dst_f32=False, toff=0):
        for (t0, t1, isctx) in blocks:
            who = 2 if isctx else b
            nt = t1 - t0
            m = AR.mark()
            SQ = AR.alloc((8, nt), BF16)
            for kc in range(8):
                P.act(SQ[:, kc, :], Xv[:, kc, t0:t1], AF.Square)
            ps = rr()
            for kc in range(8):
                P.mm(ps[:, 0:nt], ONESB, SQ[:, kc, :], start=(kc == 0), stop=(kc == 7))
            RS = AR.alloc(nt, F32)
            P.act(RS, ps[:, 0:nt], AF.Sqrt, bias=EPST, scale=1.0 / D)
            P.recip(RS, RS)
            TMP = AR.alloc((2, nt), F32)
            for kc in range(8):
                tv = TMP[:, kc % 2, :]
                P.tt(tv, Xv[:, kc, t0:t1], RS, ALU.mult)
                if gvec is None:
                    P.act(dst[:, kc, t0 - toff:t1 - toff], tv, AF.Identity, scale=GS[l][:, n, kc, who:who + 1],
                          bias=modv(l, 3 * n, kc, who))
                else:
                    P.act(dst[:, kc, t0 - toff:t1 - toff], tv, AF.Identity, scale=gvec[:, kc:kc + 1])
            AR.release(m)

    def proj_fm(W, c0, ncols, t0, t1):
        ps = rr()
        for kc in range(8):
            P.mm(ps[0:ncols, 0:t1 - t0], W[:, kc, c0:c0 + ncols], H[:, kc, t0:t1], start=(kc == 0), stop=(kc == 7))
        return ps[0:ncols, 0:t1 - t0]

    def proj_tm(W, c0, ncols, c):
        ps = rr()
        for kc in range(8):
            P.mm(ps[:, 0:ncols], H[:, kc, c * 128:(c + 1) * 128], W[:, kc, c0:c0 + ncols],
                 start=(kc == 0), stop=(kc == 7))
        return ps[:, 0:ncols]

    def rows_scan(dst, src, op0, op1, d0, reverse):
        if not reverse:
            P.scan(dst[:, 0:LC], d0[:, 0:LC], src[:, 0:LC], 0.0, op0, op1)
            P.scan(dst[:, LC:TT], d0[:, LC:TT], src[:, LC:TT], dst[:, LC - 1:LC], op0, op1)
        else:
            P.scan(dst[:, 0:LC][:, ::-1], d0[:, 0:LC][:, ::-1], src[:, 0:LC][:, ::-1], 0.0, op0, op1)
            P.scan(dst[:, LC:TT][:, ::-1], d0[:, LC:TT][:, ::-1], src[:, LC:TT][:, ::-1], dst[:, 0:1], op0, op1)

    def rows_to_cols(COLS, ci, R):
        for c0 in range(0, NCH, 6):
            ps = rr()
            for c in range(c0, c0 + 6):
                P.transpose(ps[:, (c - c0) * 8:(c - c0) * 8 + 8], R[:, c * 128:(c + 1) * 128], IDF[0:8, 0:8])
            P.copy(COLS[:, c0:c0 + 6, ci, :], ps[:, 0:48].rearrange("p (a b) -> p a b", a=6))

    def contribs(t):
        out = []
        for s in range(0, t):
            out.append((s, 0, False))
        out.append((t, 0, True))
        if t >= 2:
            for s in (0, 1):
                out.append((s, 1, False))
            for s in range(t + 1, NCH):
                out.append((s, 1, False))
        else:
            for s in range(t + 1, 2):
                out.append((s, 1, False))
        out.append((t, 1, True))
        return out

    MASKD = [MASKF, MASKB]

    def build_bcast(rowtile, r, dst):
        for bi, (t0, t1, isctx) in enumerate(TBLOCKS):
            ps = rr()
            P.mm(ps[:, 0:t1 - t0], SEL[:, r, :], rowtile[:, t0:t1])
            P.copy(dst[:, t0:t1], ps[:, 0:t1 - t0], eng=("act" if bi % 2 == 0 else "dve"))

    def run_pipelined(items, stage1, stage2, fin):
        LA = 2
        cidx = [k for k, it in enumerate(items) if it[0] == "c"]
        nxt = {cidx[j]: (cidx[j + LA] if j + LA < len(cidx) else None) for j in range(len(cidx))}
        hnd = {}
        for j in range(min(LA, len(cidx))):
            hnd[cidx[j]] = stage1(items[cidx[j]])
        for k, it in enumerate(items):
            if it[0] == "c":
                n = nxt[k]
                if n is not None:
                    hnd[n] = stage1(items[n])
                stage2(it, hnd.pop(k))
            else:
                fin(it)

    def decay_T(rowtile, r, t, s, masked, d, biascol):
        ps = rr()
        if masked:
            P.mm(ps[:, 0:128], IDF, MASKD[d], start=True, stop=False)
        P.mm(ps[:, 0:128], SEL[:, r, :], rowtile[:, t * 128:(t + 1) * 128], start=(not masked), stop=True)
        return ps

    for b in range(NB):
        for l in range(L):
            last = (l == L - 1)
            need_ctx = not last
            blocks = TBLOCKS if need_ctx else TBLOCKS[1:]
            m_layer = AR.mark()
            if l == 0:
                X = AR.alloc((8, TT), F32)
                P.dma(X[:, :, 0:LC], ctxT_d[b].rearrange("(a p) t -> p a t", p=128))
                P.dma(X[:, :, LC:TT], xT_d[b].rearrange("(a p) t -> p a t", p=128))
            norm_to(X, l, 0, b, H)
            if l > 0:
                P.dma(xs_d.rearrange("(a p) t -> p a t", p=128), X)
                AR.release(m_x)
                m_layer = m_x
            if dbg and l == 0 and b == 0:
                P.dma(hdbg_d.rearrange("(a p) t -> p a t", p=128), H)
            AR.release(m_layer)
            if stop_after == "norm1":
                break

            m0 = AR.mark()
            W1 = next_piece(l, "A1"); prefetch()
            W1v = wview(W1, 0, 8, 768)
            QT = AR.alloc((4, TT), BF16)
            KT = AR.alloc(TT, BF16)
            VP = [AR.alloc((NCH, 128), BF16) for _ in range(2)]
            P.memset(VP[0], 0.0); P.memset(VP[1], 0.0)
            W2 = next_piece(l, "A2")
            W2v = wview(W2, 0, 8, 640)
            if stop_after == "attv":
                for c in range(NCH):
                    ps = proj_tm(W1v, 640, 128, c)
                    P.copy(VP[0][:, c, 0:64], ps[:, 0:64], eng="act")
                    P.copy(VP[1][:, c, 64:128], ps[:, 64:128], eng="act")
                P.dma(brs_d[0][0:128, :], VP[0].rearrange("p a b -> p (a b)"))
                break
            if stop_after == "attr":
                ps = proj_fm(W1v, 0, 128, 256, 768)
                ps2 = proj_fm(W2v, 0, 128, 256, 768)
                T1 = AR.alloc(512, F32); T2 = AR.alloc(512, F32)
                P.tt(T1, ps, COS[:, 0:512], ALU.mult)
                P.tt(T2, ps2, SIN[:, 0:512], ALU.mult)
                P.tt(QT[:, 0, 256:768], T1, T2, ALU.add)
                ps = proj_fm(W1v, 0, 128, 0, 256)
                P.copy(QT[:, 0, 0:256], ps, eng="act")
                P.dma(brs_d[0].rearrange("(a p) t -> p a t", p=128), QT)
                break
            if stop_after == "attw":
                ps = proj_fm(W1v, 0, 128, 256, 768)
                P.copy(QT[:, 0, 256:768], ps, eng="act")
                ps = proj_tm(W1v, 640, 128, 3)
                P.copy(VP[0][:, 3, 0:64], ps[:, 0:64], eng="act")
                P.dma(brs_d[0].rearrange("(a p) t -> p a t", p=128), QT)
                break
            for (t0, t1, isctx) in TBLOCKS:
                nt = t1 - t0
                for c in range(5):
                    if isctx and c < 4 and not need_ctx:
                        continue
                    dst = QT[:, c, t0:t1] if c < 4 else KT[:, t0:t1]
                    ps = proj_fm(W1v, c * 128, 128, t0, t1)
                    if isctx:
                        P.copy(dst, ps, eng="act")
                    else:
                        ps2 = proj_fm(W2v, c * 128, 128, t0, t1)
                        m = AR.mark()
                        T1 = AR.alloc(nt, F32); T2 = AR.alloc(nt, F32)
                        P.tt(T1, ps, COS[:, t0 - LC:t1 - LC], ALU.mult)
                        P.tt(T2, ps2, SIN[:, t0 - LC:t1 - LC], ALU.mult)
                        P.tt(dst, T1, T2, ALU.add)
                        AR.release(m)
            for c in range(NCH):
                ps = proj_tm(W1v, 640, 128, c)
                P.copy(VP[0][:, c, 0:64], ps[:, 0:64], eng="act")
                P.copy(VP[1][:, c, 64:128], ps[:, 64:128], eng="act")
            prefetch()
            ATs = AR.alloc((4, TT), BF16)
            if not need_ctx:
                P.memset(ATs[:, :, 0:LC], 0.0)
            if stop_after == "attproj":
                P.dma(brs_d[0].rearrange("(a p) t -> p a t", p=128), QT)
                break
            qblocks = list(range(0 if need_ctx else 2, NCH))
            if stop_after == "att1":
                qblocks = qblocks[:1]
            st["rrn"] = 6
            NSET = 3
            bufs = [dict(SM=AR.alloc(640, F32), ST4=AR.alloc(8, F32), PB=AR.alloc(640, BF16),
                         PN=AR.alloc(640, BF16), PTs=AR.alloc(640, BF16)) for _ in range(NSET)]
            its = []
            for qb in qblocks:
                lat = qb >= 2
                loc = [k for k in (qb - 1, qb, qb + 1) if 2 <= k < NCH] if lat else []
                for c in range(4):
                    for half in range(2):
                        its.append(dict(qb=qb, lat=lat, loc=loc, c=c, half=half, nl=len(loc) * 128,
                                        keych=[0, 1] + loc, mo=(0 if (not lat or qb - 1 >= 2) else 128)))
            ast = {"acc": None}

            def at_A(it):
                psl = slice(64 * it["half"], 64 * it["half"] + 64)
                lhsT = QT[psl, it["c"], it["qb"] * 128:(it["qb"] + 1) * 128]
                psB = rr()
                P.mm(psB[:, 0:256], lhsT, KT[psl, 0:256])
                psA = None
                if it["lat"]:
                    psA = rr()
                    P.mm(psA[:, 0:it["nl"]], lhsT, KT[psl, it["loc"][0] * 128:(it["loc"][-1] + 1) * 128])
                return psB, psA

            def at_BC(it, hnd, bf):
                psB, psA = hnd
                SM, ST4, PB, PN, PTs = bf["SM"], bf["ST4"], bf["PB"], bf["PN"], bf["PTs"]
                nl, mo, half, c, qb = it["nl"], it["mo"], it["half"], it["c"], it["qb"]
                head = c + 4 * half
                if half == 0:
                    ast["acc"] = accb()
                acc = ast["acc"]
                P.copy(SM[:, 0:256], psB[:, 0:256], eng="act")
                if it["lat"]:
                    P.tt(SM[:, 256:256 + nl], psA[:, 0:nl], MASKA[:, mo:mo + nl], ALU.add)
                ntot = 256 + nl
                sk = SINKB[:, l * 8 + head:l * 8 + head + 1]
                P.reduce(ST4[:, 0:1], SM[:, 0:ntot], ALU.max)
                P.ts(ST4[:, 1:2], ST4[:, 0:1], 0.125, sk, ALU.mult, ALU.max)
                P.ts(ST4[:, 2:3], ST4[:, 1:2], -1.0, None, ALU.mult)
                P.act(PB[:, 0:ntot], SM[:, 0:ntot], AF.Exp, bias=ST4[:, 2:3], scale=0.125, accum_out=ST4[:, 3:4])
                P.act(ST4[:, 4:5], ST4[:, 2:3], AF.Exp, bias=sk)
                P.tt(ST4[:, 5:6], ST4[:, 3:4], ST4[:, 4:5], ALU.add)
                P.recip(ST4[:, 6:7], ST4[:, 5:6])
                P.ts(PN[:, 0:ntot], PB[:, 0:ntot], ST4[:, 6:7], None, ALU.mult)
                psT = rr().bitcast(BF16)
                keych = it["keych"]
                nk = len(keych)
                for j in range(nk):
                    P.transpose(psT[:, j * 128:(j + 1) * 128], PN[:, j * 128:(j + 1) * 128], IDB)
                P.copy(PTs[:, 0:nk * 128], psT[:, 0:nk * 128], eng="act")
                for j, kc_ in enumerate(keych):
                    P.mm(acc[:, 0:128], VP[half][:, kc_, :], PTs[:, j * 128:(j + 1) * 128],
                         start=(half == 0 and j == 0), stop=(half == 1 and j == nk - 1))
                if half == 1:
                    P.copy(ATs[:, c, qb * 128:(qb + 1) * 128], acc[:, 0:128])

            hnd = {}
            if its:
                hnd[0] = at_A(its[0])
            for j, it in enumerate(its):
                if j + 1 < len(its):
                    hnd[j + 1] = at_A(its[j + 1])
                at_BC(it, hnd.pop(j), bufs[j % NSET])
            st["rrn"] = 3
            P.dma(brs_d[0].rearrange("(a p) t -> p a t", p=128), ATs)
            AR.release(m0)
            if stop_after in ("att", "att1"):
                break

            m0 = AR.mark()
            WS = next_piece(l, "SM1"); prefetch()
            WSv = wview(WS, 0, 8, 16)
            NMR = [AR.alloc(TT, F32, parts=8) for _ in range(2)]
            COLS = AR.alloc((NCH, 4, 8), F32)
            MTs = AR.alloc((4, TT), BF16)
            m_rows = AR.mark()
            LI = AR.alloc(TT, F32, parts=8); LF = AR.alloc(TT, F32, parts=8)
            for (t0, t1, isctx) in TBLOCKS:
                ps = proj_fm(WSv, 0, 8, t0, t1)
                P.act(LI[:, t0:t1], ps, AF.Identity, bias=MLB[:, 2 * l:2 * l + 1])
                ps = proj_fm(WSv, 8, 8, t0, t1)
                P.act(LF[:, t0:t1], ps, AF.Exp, bias=NMLB[:, 2 * l + 1:2 * l + 2], scale=-1.0)
            P.act(LF, LF, AF.Ln, bias=1.0)
            P.ts(LF, LF, -1.0, None, ALU.mult)
            m1 = AR.mark()
            for d in range(2):
                T1 = AR.alloc(TT, F32, parts=8); T4 = AR.alloc(TT, F32, parts=8)
                T2 = NMR[d]
                rows_scan(T1, LF, ALU.mult, ALU.add, ONEF, d == 1)
                pass
                P.tt(T1, LI, T1, ALU.subtract)
                rows_scan(T2, T1, ALU.max, ALU.max, T1, d == 1)
                P.tt(T4, LI, T1, ALU.subtract)
                P.tt(T4, T4, T2, ALU.add)
                P.act(T4, T4, AF.Exp, scale=-1.0)
                rows_to_cols(COLS, d * 2 + 0, T1)
                rows_to_cols(COLS, d * 2 + 1, T4)
                P.ts(T2, T2, -1.0, None, ALU.mult)
                AR.release(m1)
            AR.release(m_rows)
            if stop_after == "mlrows":
                P.dma(brs_d[1][0:128, 0:NCH * 32], COLS.rearrange("p a b c -> p (a b c)").bitcast(BF16)[:, 0:NCH * 32])
                P.dma(brs_d[1][128:136, :], NMR[0].bitcast(BF16)[:, 0:TT])
                break
            if not need_ctx:
                P.memset(MTs[:, :, 0:LC], 0.0)
            for h in range(4):
                WH = next_piece(l, f"MH{h}"); prefetch()
                WHv = wview(WH, 0, 8, 512)
                m2 = AR.mark()
                QTh = AR.alloc(TT, BF16); KTh = AR.alloc(TT, BF16)
                VA = AR.alloc((NCH, 132), BF16); OS = AR.alloc((NCH, 128), BF16)
                P.memset(VA[:, :, 128:129], 1.0)
                for (t0, t1, isctx) in TBLOCKS:
                    ps = proj_fm(WHv, 0, 128, t0, t1)
                    P.copy(QTh[:, t0:t1], ps, eng="act")
                    ps = proj_fm(WHv, 128, 128, t0, t1)
                    P.ts(KTh[:, t0:t1], ps, 128.0 ** -0.5, None, ALU.mult)
                for c in range(NCH):
                    ps = proj_tm(WHv, 256, 256, c)
                    P.copy(VA[:, c, 0:128], ps[:, 0:128])
                    P.act(OS[:, c, :], ps[:, 128:256], AF.Sigmoid)
                ring = [(AR.alloc(128, F32), AR.alloc(128, BF16)) for _ in range(6)]
                mring = [AR.alloc(128, F32) for _ in range(2)]
                BC = [AR.alloc(TT, F32) for _ in range(2)]
                for d in range(2):
                    build_bcast(NMR[d], d * 4 + h, BC[d])
                mst = {"rc": 0, "accs": None, "mc": 0}
                items = []
                for t in (range(NCH) if need_ctx else range(2, NCH)):
                    cl = contribs(t)
                    cnt = [sum(1 for x in cl if x[1] == d) for d in range(2)]
                    seen = [0, 0]
                    for ci_, (s_, d, masked) in enumerate(cl):
                        items.append(("c", t, s_, d, masked, seen[d] == 0, seen[d] == cnt[d] - 1, ci_ == 0))
                        seen[d] += 1
                    items.append(("fin", t))

                def ml_s1(it):
                    _, t, s_, d, masked, first, lastc, newt = it
                    pk = kqb()
                    P.mm(pk[:, 0:128], KTh[:, s_ * 128:(s_ + 1) * 128], QTh[:, t * 128:(t + 1) * 128])
                    return pk, None

                def ml_s2(it, hh):
                    _, t, s_, d, masked, first, lastc, newt = it
                    pk, psD = hh
                    if newt:
                        bk = accb()
                        mst["accs"] = [bk[:, 0:256], bk[:, 256:512]]
                    r = d * 4 + h
                    DT_, ST_ = ring[mst["rc"] % 6]
                    mst["rc"] += 1
                    src = BC[d][:, t * 128:(t + 1) * 128]
                    if masked:
                        tm = mring[mst["mc"] % 2]
                        mst["mc"] += 1
                        P.tt(tm, src, MASKD[d], ALU.add)
                        src = tm
                    P.act(DT_, src, AF.Exp, bias=COLS[:, s_, d * 2, r:r + 1])
                    P.tt(ST_, pk[:, 0:128], DT_, ALU.mult)
                    P.mm(mst["accs"][d][:, 0:129], ST_, VA[:, s_, 0:129], start=first, stop=lastc)

                def ml_fin(it):
                    t = it[1]
                    accs = mst["accs"]
                    m3 = AR.mark()
                    HD = []
                    for d in range(2):
                        r = d * 4 + h
                        A_ = AR.alloc(132, F32)
                        P.copy(A_[:, 0:129], accs[d][:, 0:129], eng="act")
                        S4 = AR.alloc(4, F32)
                        P.stt(S4[:, 0:1], A_[:, 128:129], -1.0, A_[:, 128:129], ALU.mult, ALU.max)
                        P.tt(S4[:, 1:2], S4[:, 0:1], COLS[:, t, d * 2 + 1, r:r + 1], ALU.max)
                        P.recip(S4[:, 2:3], S4[:, 1:2])
                        Hd = AR.alloc(128, F32)
                        P.ts(Hd, A_[:, 0:128], S4[:, 2:3], None, ALU.mult)
                        HD.append(Hd)
                    HS = AR.alloc(128, F32)
                    P.tt(HS, HD[0], HD[1], ALU.add)
                    S4 = AR.alloc(4, F32)
                    JK = AR.alloc(128, F32)
                    P.act(JK, HS, AF.Square, accum_out=S4[:, 0:1])
                    P.act(S4[:, 1:2], S4[:, 0:1], AF.Ln, bias=EPST, scale=1.0 / 128)
                    P.act(S4[:, 2:3], S4[:, 1:2], AF.Exp, scale=-0.5)
                    O1 = AR.alloc(128, F32)
                    P.stt(O1, HS, S4[:, 2:3], MLG[:, l * 512 + h * 128:l * 512 + (h + 1) * 128], ALU.mult, ALU.mult)
                    O2 = AR.alloc(128, BF16)
                    P.tt(O2, O1, OS[:, t, :], ALU.mult)
                    pT = rr().bitcast(BF16)
                    P.transpose(pT[:, 0:128], O2, IDB)
                    P.copy(MTs[:, h, t * 128:(t + 1) * 128], pT[:, 0:128], eng="act")
                    AR.release(m3)

                run_pipelined(items, ml_s1, ml_s2, ml_fin)
                AR.release(m2)
                if stop_after == "mlh1":
                    break
            P.dma(brs_d[1].rearrange("(a p) t -> p a t", p=128), MTs)
            AR.release(m0)
            if stop_after in ("ml", "mlh1"):
                break

            m0 = AR.mark()
            WS = next_piece(l, "SM2"); prefetch()
            WSv = wview(WS, 0, 8, 16)
            ACS = [AR.alloc(TT, F32, parts=8) for _ in range(2)]
            COLS = AR.alloc((NCH, 4, 8), F32)
            m1 = AR.mark()
            for d in range(2):
                DTr = AR.alloc(TT, F32, parts=8); T1 = AR.alloc(TT, F32, parts=8)
                for (t0, t1, isctx) in TBLOCKS:
                    ps = proj_fm(WSv, 8 * d, 8, t0, t1)
                    P.act(DTr[:, t0:t1], ps, AF.Exp, bias=SDT[:, 4 * l + d:4 * l + d + 1])
                P.act(DTr, DTr, AF.Ln, bias=1.0)
                P.ts(T1, DTr, SA[:, 4 * l + 2 + d:4 * l + 3 + d], None, ALU.mult)
                rows_scan(ACS[d], T1, ALU.mult, ALU.add, ONEF, d == 1)
                P.ts(T1, ACS[d], -1.0, None, ALU.mult)
                rows_to_cols(COLS, d * 2 + 0, T1)
                rows_to_cols(COLS, d * 2 + 1, DTr)
                AR.release(m1)
            YG = AR.alloc((NCH, 512), BF16)
            SSQ = AR.alloc((NCH, 8), F32)
            trange = list(range(NCH) if need_ctx else range(2, NCH))
            for g in range(2):
                WG = next_piece(l, f"SG{g}"); prefetch()
                WGv = wview(WG, 0, 8, 768)
                m2 = AR.mark()
                XTOK = AR.alloc((NCH, 256), BF16)
                BTg = AR.alloc(TT, BF16); CTg = AR.alloc(TT, BF16)
                ZS = AR.alloc((NCH, 256), BF16)
                m3 = AR.mark()
                PAD = AR.alloc(2312, F32); ACC = AR.alloc(2308, F32); XF = AR.alloc(TT, BF16)
                P.memset(PAD[:, 0:2], 0.0); P.memset(PAD[:, 258:262], 0.0); P.memset(PAD[:, 2310:2312], 0.0)
                for ch in range(4):
                    cidx = [g * 2, g * 2 + 1, 4 + g, 6 + g][ch]
                    cw = CW[:, (l * 8 + cidx) * 6:(l * 8 + cidx) * 6 + 6]
                    for (t0, t1, isctx) in TBLOCKS:
                        ps = proj_fm(WGv, ch * 128, 128, t0, t1)
                        o0 = 2 + t0 if isctx else 262 + (t0 - LC)
                        P.copy(PAD[:, o0:o0 + (t1 - t0)], ps, eng="act")
                    P.ts(ACC, PAD[:, 0:2308], cw[:, 0:1], None, ALU.mult)
                    for j in range(1, 5):
                        P.stt(ACC, PAD[:, j:j + 2308], cw[:, j:j + 1], ACC, ALU.mult, ALU.add)
                    dst = XF if ch < 2 else (BTg if ch == 2 else CTg)
                    P.act(dst[:, 0:LC], ACC[:, 0:LC], AF.Silu, bias=cw[:, 5:6])
                    P.act(dst[:, LC:TT], ACC[:, 260:2308], AF.Silu, bias=cw[:, 5:6])
                    if ch < 2:
                        for c0 in range(0, NCH, 4):
                            cs = list(range(c0, min(c0 + 4, NCH)))
                            pT = rr().bitcast(BF16)
                            for c in cs:
                                P.transpose(pT[:, (c - c0) * 128:(c - c0 + 1) * 128], XF[:, c * 128:(c + 1) * 128], IDB)
                            P.copy(XTOK[:, c0:c0 + len(cs), ch * 128:(ch + 1) * 128],
                                   pT[:, 0:len(cs) * 128].rearrange("p (a b) -> p a b", a=len(cs)))
                AR.release(m3)
                BTOK = None
                for c in range(NCH):
                    ps = proj_tm(WGv, 512, 256, c)
                    P.act(ZS[:, c, :], ps, AF.Silu)
                ring = [(AR.alloc(128, F32), AR.alloc(128, BF16)) for _ in range(6)]
                mring = [AR.alloc(128, F32) for _ in range(2)]
                BC = [AR.alloc(TT, F32) for _ in range(2)]
                sst = {"rc": 0, "acc": None, "mc": 0}
                for r4 in range(4):
                    hd = g * 4 + r4
                    for d in range(2):
                        build_bcast(ACS[d], hd, BC[d])
                    items = []
                    for t in trange:
                        cl = contribs(t)
                        for ci_, (s_, d, masked) in enumerate(cl):
                            items.append(("c", t, s_, d, masked, ci_ == 0, ci_ == len(cl) - 1))
                        items.append(("fin", t))

                    def sd_s1(it, hd=hd):
                        _, t, s_, d, masked, first, lastc = it
                        pk = kqb()
                        P.mm(pk[:, 0:128], BTg[:, s_ * 128:(s_ + 1) * 128], CTg[:, t * 128:(t + 1) * 128])
                        return pk, None

                    def sd_s2(it, hh, hd=hd, r4=r4):
                        _, t, s_, d, masked, first, lastc = it
                        pk, psD = hh
                        if first:
                            sst["acc"] = accb()
                        DT_, MT_ = ring[sst["rc"] % 6]
                        sst["rc"] += 1
                        src = BC[d][:, t * 128:(t + 1) * 128]
                        if masked:
                            tm = mring[sst["mc"] % 2]
                            sst["mc"] += 1
                            P.tt(tm, src, MASKD[d], ALU.add)
                            src = tm
                        P.act(DT_, src, AF.Exp, bias=COLS[:, s_, d * 2, hd:hd + 1])
                        P.stt(MT_, pk[:, 0:128], COLS[:, s_, d * 2 + 1, hd:hd + 1], DT_, ALU.mult, ALU.mult)
                        P.mm(sst["acc"][:, 0:64], MT_, XTOK[:, s_, r4 * 64:(r4 + 1) * 64], start=first, stop=lastc)

                    def sd_fin(it, hd=hd, r4=r4):
                        t = it[1]
                        acc = sst["acc"]
                        m4 = AR.mark()
                        Y1 = AR.alloc(64, F32)
                        P.tt(Y1, XTOK[:, t, r4 * 64:(r4 + 1) * 64],
                             SDK[:, l * 512 + hd * 64:l * 512 + (hd + 1) * 64], ALU.mult)
                        P.tt(Y1, Y1, acc[:, 0:64], ALU.add)
                        P.tt(YG[:, t, hd * 64:(hd + 1) * 64], Y1, ZS[:, t, r4 * 64:(r4 + 1) * 64], ALU.mult)
                        JK = AR.alloc(64, F32)
                        P.act(JK, YG[:, t, hd * 64:(hd + 1) * 64], AF.Square, accum_out=SSQ[:, t, hd:hd + 1])
                        AR.release(m4)

                    run_pipelined(items, sd_s1, sd_s2, sd_fin)
                AR.release(m2)
            STs = AR.alloc((4, TT), BF16)
            if not need_ctx:
                P.memset(STs[:, :, 0:LC], 0.0)
            for t in trange:
                m4 = AR.mark()
                S4 = AR.alloc(4, F32)
                P.reduce(S4[:, 0:1], SSQ[:, t, :], ALU.add)
                P.act(S4[:, 1:2], S4[:, 0:1], AF.Ln, bias=EPST, scale=1.0 / 512)
                P.act(S4[:, 2:3], S4[:, 1:2], AF.Exp, scale=-0.5)
                O2 = AR.alloc(512, BF16)
                P.stt(O2, YG[:, t, :], S4[:, 2:3], SNG[:, l * 512:(l + 1) * 512], ALU.mult, ALU.mult)
                pT = rr().bitcast(BF16)
                for ch in range(4):
                    P.transpose(pT[:, ch * 128:(ch + 1) * 128], O2[:, ch * 128:(ch + 1) * 128], IDB)
                P.copy(STs[:, :, t * 128:(t + 1) * 128], pT[:, 0:512].rearrange("p (a b) -> p a b", a=4), eng="act")
                AR.release(m4)
            P.dma(brs_d[2].rearrange("(a p) t -> p a t", p=128), STs)
            AR.release(m0)
            if stop_after == "ssd":
                break

            m0 = AR.mark()
            BR = [AR.alloc((4, TT), BF16) for _ in range(3)]
            for i in range(3):
                P.dma(BR[i], brs_d[i].rearrange("(a p) t -> p a t", p=128))
            for i in range(8):
                WMp = next_piece(l, f"MG{i}"); prefetch()
                Wg = wview(WMp, 0, 8, 384)
                Wo_ = WMp[:, 3072:3072 + 1536].rearrange("p (b a c) -> p b a c", b=3, a=4)
                m_i = AR.mark()
                YTi = AR.alloc(TT, BF16)
                for (t0, t1, isctx) in blocks:
                    nt = t1 - t0
                    m1 = AR.mark()
                    YA = AR.alloc(nt, F32); TM = AR.alloc(nt, F32)
                    for br in range(3):
                        psG = proj_fm(Wg, br * 128, 128, t0, t1)
                        SG = AR.alloc(nt, F32)
                        P.act(SG, psG, AF.Sigmoid)
                        psO = rr()
                        for kc in range(4):
                            P.mm(psO[:, 0:nt], Wo_[:, br, kc, :], BR[br][:, kc, t0:t1], start=(kc == 0), stop=(kc == 3))
                        if br == 0:
                            P.tt(YA, SG, psO[:, 0:nt], ALU.mult)
                        elif br == 1:
                            P.tt(TM, SG, psO[:, 0:nt], ALU.mult)
                            P.tt(YA, YA, TM, ALU.add, eng="pool")
                        else:
                            P.tt(TM, SG, psO[:, 0:nt], ALU.mult)
                            P.tt(YTi[:, t0:t1], YA, TM, ALU.add, eng="pool")
                    AR.release(m1)
                P.dma(yt_d[i * 128:(i + 1) * 128, blocks[0][0]:TT], YTi[:, blocks[0][0]:TT])
                AR.release(m_i)
            AR.release(m0)

            m_x = AR.mark()
            X = AR.alloc((8, TT), F32)
            if l == 0:
                P.dma(X[:, :, 0:LC], ctxT_d[b].rearrange("(a p) t -> p a t", p=128))
                P.dma(X[:, :, LC:TT], xT_d[b].rearrange("(a p) t -> p a t", p=128))
            else:
                P.dma(X, xs_d.rearrange("(a p) t -> p a t", p=128))
            WOp = next_piece(l, "WO"); prefetch()
            WOv = wview(WOp, 0, 8, 1024)
            for (t0, t1, isctx) in blocks:
                nt = t1 - t0
                who = 2 if isctx else b
                m1 = AR.mark()
                Yb = AR.alloc((8, nt), BF16)
                P.dma(Yb, yt_d.rearrange("(a p) t -> p a t", p=128)[:, :, t0:t1])
                for o in range(8):
                    ps = rr()
                    for kc in range(8):
                        P.mm(ps[:, 0:nt], WOv[:, kc, o * 128:(o + 1) * 128], Yb[:, kc, :], start=(kc == 0), stop=(kc == 7))
                    P.stt(X[:, o, t0:t1], ps[:, 0:nt], modv(l, 2, o, who), X[:, o, t0:t1], ALU.mult, ALU.add)
                AR.release(m1)
            if stop_after == "wo":
                break

            norm_to(X, l, 1, b, H, blocks=blocks)
            for q in range(11):
                WF = next_piece(l, f"FF{q}"); prefetch()
                Wu = wview(WF, 0, 8, 512)
                Wd = wview(WF, 4096, 2, 1024)
                for (t0, t1, isctx) in blocks:
                    nt = t1 - t0
                    who = 2 if isctx else b
                    m1 = AR.mark()
                    A_ = AR.alloc((2, nt), BF16)
                    for jj in range(2):
                        psg = proj_fm(Wu, jj * 128, 128, t0, t1)
                        SGt = AR.alloc(nt, F32)
                        P.act(SGt, psg, AF.Silu)
                        psu = proj_fm(Wu, 256 + jj * 128, 128, t0, t1)
                        P.tt(A_[:, jj, :], SGt, psu, ALU.mult)
                    for o in range(8):
                        ps = rr()
                        for jj in range(2):
                            P.mm(ps[:, 0:nt], Wd[:, jj, o * 128:(o + 1) * 128], A_[:, jj, :], start=(jj == 0), stop=(jj == 1))
                        P.stt(X[:, o, t0:t1], ps[:, 0:nt], modv(l, 5, o, who), X[:, o, t0:t1], ALU.mult, ALU.add)
                    AR.release(m1)
            if last:
                for (t0, t1, isctx) in [(LC + 256 * i, LC + 256 * (i + 1), False) for i in range(8)]:
                    m1 = AR.mark()
                    OUT = AR.alloc((8, t1 - t0), F32)
                    norm_to(X, l, 0, b, OUT, gvec=GN[:, 2 * L * 8:(2 * L + 1) * 8], blocks=[(t0, t1, isctx)], toff=t0)
                    P.dma(outT_d[b].rearrange("(a p) t -> p a t", p=128)[:, :, t0 - LC:t1 - LC], OUT)
                    AR.release(m1)
                AR.release(m_x)
        else:
            continue
        break

    outs = [d for d in P.dmas if False]
    final = [d for d in P.dmas]
    P.emit(final_wait_dmas=final[-ND:])
    AR.P = P
    return nc, AR


_CACHE = {}


def host_prep(inputs, NB=2, L=DEPTH):
    f32 = np.float32
    g = {k: np.asarray(v, dtype=f32) for k, v in inputs.items()}
    consts = _host_consts()
    wl = [_host_layer_weights(l, g["w_in"], g["w_att_out"], g["w_ml_out"], g["w_ssd_out"], g["w_o"], g["w_up"], g["w_down"])
          for l in range(L)]
    wmod = np.stack([_k8(g["w_mod"][l], np.arange(6144)) for l in range(L)])
    bmod = np.stack([g["b_mod"][l].reshape(48, 128).T for l in range(L)])
    gn = np.concatenate([np.concatenate([g["g_norm1"][l].reshape(8, 128).T, g["g_norm2"][l].reshape(8, 128).T], 1)
                         for l in range(L)] + [g["g_final"].reshape(8, 128).T], 1)
    sinkb = np.broadcast_to(g["att_sink"][:L].reshape(1, L * 8), (128, L * 8))
    mlb = np.concatenate([np.stack([g["ml_i_bias"][l].reshape(8), g["ml_f_bias"][l].reshape(8)], 1) for l in range(L)], 1)
    mlgain = np.broadcast_to(g["ml_head_gain"][:L].reshape(1, L * 512), (128, L * 512))
    cw = []
    for l in range(L):
        for ci in range(8):
            ch = slice(ci * 128, (ci + 1) * 128)
            cw.append(np.concatenate([g["ssd_conv_w"][l][:, ch].T, g["ssd_conv_b"][l][ch][:, None]], 1))
    convw = np.concatenate(cw, 1)
    sdt = np.concatenate([np.stack([g["ssd_dt_bias"][l][0], g["ssd_dt_bias"][l][1], g["ssd_a_log"][l][0], g["ssd_a_log"][l][1]], 1)
                          for l in range(L)], 1)
    ssdskip = np.broadcast_to(np.repeat(g["ssd_d"][:L], 64, axis=1).reshape(1, L * 512), (128, L * 512))
    ssdng = np.broadcast_to(g["ssd_norm_gain"][:L].reshape(1, L * 512), (128, L * 512))
    shared = dict(wmod=wmod, bmod=bmod, gn=gn, sinkb=sinkb, mlb=mlb, mlgain=mlgain, convw=convw, ssddt=sdt,
                  ssdskip=ssdskip, ssdng=ssdng, **consts)
    for l in range(L):
        shared[f"wl{l}"] = wl[l]
    shared = {k: np.ascontiguousarray(v, dtype=f32) for k, v in shared.items()}
    ncores = g["x"].shape[0] // NB
    maps = []
    for c in range(ncores):
        bs = slice(c * NB, (c + 1) * NB)
        m = dict(shared)
        m["xT"] = np.ascontiguousarray(g["x"][bs].transpose(0, 2, 1))
        m["ctxT"] = np.ascontiguousarray(g["ctx"][bs].transpose(0, 2, 1))
        cc = np.stack([g["c"][c * NB + i] if i < NB else g["c_ctx"] for i in range(2)] + [g["c_ctx"]], 0)
        if NB == 1:
            cc = np.stack([g["c"][c], g["c"][c], g["c_ctx"]], 0)
        silu_in = cc.reshape(3, 8, 128).transpose(2, 1, 0).reshape(128, 24)
        m["cT"] = np.ascontiguousarray(silu_in)
        maps.append(m)
    return maps


def kernel(**inputs):
    NB = 2
    if "nc" not in _CACHE:
        _CACHE["nc"] = build(NB=NB)[0]
    nc = _CACHE["nc"]
    maps = host_prep(inputs, NB=NB)
    res = run_bass_kernel_spmd(nc, maps, core_ids=list(range(len(maps))))
    outs = [r["outT"] for r in res.results]
    full = np.concatenate(outs, 0)
    return np.ascontiguousarray(full.transpose(0, 2, 1)).astype(np.float32)
```

```python
import numpy as np
import concourse.bass as bass
import concourse.mybir as mybir

F32 = mybir.dt.float32
BF16 = mybir.dt.bfloat16
AF = mybir.ActivationFunctionType
ALU = mybir.AluOpType
AX = mybir.AxisListType

NS = 4
ND = 24


class Instr:
    __slots__ = ("eng", "idx", "fn", "dma", "deps", "signal", "clock", "dma_id", "sidx")

    def __init__(self, eng, idx, fn, dma):
        self.eng = eng
        self.idx = idx
        self.fn = fn
        self.dma = dma
        self.deps = []
        self.signal = False
        self.clock = None
        self.dma_id = None
        self.sidx = None


def ap_box(ap):
    t = ap.tensor
    dims = list(ap.ap)
    pstride, pn = dims[0]
    shape = list(t.shape)
    per_part = 1
    for s in shape[1:]:
        per_part *= int(s)
    off = int(ap.offset)
    if pstride == 0:
        pstride_eff = per_part
    else:
        pstride_eff = pstride
    p0 = off // per_part
    f0 = off - p0 * per_part
    lo = 0
    hi = 0
    for st, cn in dims[1:]:
        ext = (int(cn) - 1) * int(st)
        if ext >= 0:
            hi += ext
        else:
            lo += ext
    esz = mybir.dt.size(t.dtype) if hasattr(mybir.dt, "size") else None
    return (t.name, p0, p0 + int(pn), f0 + lo, f0 + hi + 1)


class Prog:
    def __init__(self, nc):
        self.nc = nc
        self.engs = {"pe": nc.tensor, "dve": nc.vector, "act": nc.scalar, "pool": nc.gpsimd, "sp": nc.sync}
        self.ins = {e: [] for e in self.engs}
        self.track = {}
        self.known = {e: {} for e in self.engs}
        self.ndma = 0
        self.dmas = []
        self.esize = {}
        self.skip_dram = set()

    def _box(self, ap):
        t = ap.tensor
        sp = str(type(t).__name__)
        if sp.startswith("DRam"):
            if t.name in self.skip_dram:
                return None
            off = int(ap.offset); lo = 0; hi = 0
            for st, cn in ap.ap:
                ext = (int(cn) - 1) * int(st)
                if ext >= 0: hi += ext
                else: lo += ext
            es = {F32: 4, BF16: 2}.get(t.dtype, 4)
            return (t.name, 0, 1, (off + lo) * es, (off + hi + 1) * es)
        name, p0, p1, f0, f1 = ap_box(ap)
        if sp.startswith("PSum"):
            return (name, 0, 128, 0, 2048)
        es = {F32: 4, BF16: 2}.get(t.dtype, None)
        if es is None:
            es = int(ap.nbytes // max(1, ap.size)) if hasattr(ap, "nbytes") else 4
        return (name, p0, p1, f0 * es, f1 * es)

    def _collect(self, ins, box, write):
        name, p0, p1, f0, f1 = box
        BIN = 4096 if not name.startswith("dr_") else (1 << 22)
        bins = range(f0 // BIN, (f1 - 1) // BIN + 1)
        tr = self.track.setdefault(name, {})
        deps = []
        seen = set()
        newrec = (ins, write, p0, p1, f0, f1)
        for bi in bins:
            recs = tr.get(bi)
            if recs is None:
                tr[bi] = [newrec]
                continue
            keep = self._scan(recs, ins, write, p0, p1, f0, f1, deps, seen, bi * BIN, (bi + 1) * BIN,
                              name_is_psum=name.startswith("ps"))
            keep.append(newrec)
            tr[bi] = keep
        return deps

    def _scan(self, recs, ins, write, p0, p1, f0, f1, deps, seen, b0, b1, name_is_psum=False):
        keep = []
        for r in recs:
            rin, rw, q0, q1, g0, g1 = r
            ov = not (q1 <= p0 or p1 <= q0 or g1 <= f0 or f1 <= g0)
            if not ov:
                keep.append(r)
                continue
            if (write or rw or (name_is_psum and rin.eng != ins.eng)) and id(rin) not in seen:
                seen.add(id(rin))
                deps.append((rin, rw))
            covered = write and p0 <= q0 and q1 <= p1 and f0 <= max(g0, b0) and min(g1, b1) <= f1
            if covered:
                continue
            if (not write) and (not rw) and rin.eng == ins.eng and not ins.dma and not rin.dma \
                    and q0 == p0 and q1 == p1 and g0 == f0 and g1 == f1:
                continue
            keep.append(r)
        return keep

    def add(self, eng, fn, reads=(), writes=(), dma=False):
        ins = Instr(eng, len(self.ins[eng]), fn, dma)
        alld = {}
        for ap in reads:
            b = self._box(ap)
            if b is None:
                continue
            for d, dw in self._collect(ins, b, False):
                alld[id(d)] = (d, True)
        for ap in writes:
            b = self._box(ap)
            if b is None:
                continue
            for d, dw in self._collect(ins, b, True):
                prev = alld.get(id(d))
                alld[id(d)] = (d, (prev[1] if prev else False) or False)
        known = self.known[eng]
        if dma:
            ins.dma_id = self.ndma
            self.ndma += 1
            self.dmas.append(ins)
            if ins.dma_id >= ND:
                d = self.dmas[ins.dma_id - ND]
                alld[id(d)] = (d, True)
        need = []
        for d, raw in sorted(alld.values(), key=lambda x: -(x[0].dma_id if x[0].dma else x[0].idx)):
            if d is ins:
                continue
            if d.dma:
                key = ("dma", d.dma_id % ND)
                val = d.dma_id // ND + 1
                if known.get(key, 0) >= val:
                    continue
                need.append(d)
            else:
                if d.eng == eng and not dma:
                    if eng == "pe":
                        continue
                    if not raw:
                        continue
                if known.get(d.eng, -1) >= d.idx:
                    continue
                need.append(d)
        for d in need:
            d.signal = True
            if d.dma:
                key = ("dma", d.dma_id % ND)
                known[key] = max(known.get(key, 0), d.dma_id // ND + 1)
            else:
                known[d.eng] = max(known.get(d.eng, -1), d.idx)
            if d.clock:
                for k, v in d.clock.items():
                    if known.get(k, -1) < v:
                        known[k] = v
        ins.deps = need
        ins.clock = dict(known)
        if not dma:
            ins.clock[eng] = max(ins.clock.get(eng, -1), ins.idx - 1)
        self.ins[eng].append(ins)
        return ins

    def mm(self, out, lhsT, rhs, start=True, stop=True, **kw):
        return self.add("pe", lambda e: e.matmul(out, lhsT, rhs, start=start, stop=stop, **kw),
                        reads=[lhsT, rhs], writes=[out])

    def transpose(self, out, in_, ident):
        return self.add("pe", lambda e: e.transpose(out, in_, ident), reads=[in_, ident], writes=[out])

    def act(self, out, in_, func, bias=None, scale=None, accum_out=None, eng="act"):
        kw = {}
        reads = [in_]
        if bias is not None:
            kw["bias"] = bias
            if not isinstance(bias, (int, float)):
                reads.append(bias)
        if scale is not None:
            kw["scale"] = scale
            if not isinstance(scale, (int, float)):
                reads.append(scale)
        writes = [out]
        if accum_out is not None:
            kw["accum_out"] = accum_out
            writes.append(accum_out)
        return self.add(eng, lambda e: e.activation(out, in_, func, **kw), reads=reads, writes=writes)

    def tt(self, out, in0, in1, op, eng="dve"):
        return self.add(eng, lambda e: e.tensor_tensor(out, in0, in1, op), reads=[in0, in1], writes=[out])

    def ts(self, out, in0, s1, s2, op0, op1=None, eng="dve", accum_out=None):
        reads = [in0]
        for s in (s1, s2):
            if s is not None and not isinstance(s, (int, float)):
                reads.append(s)
        writes = [out]
        kw = {}
        if accum_out is not None:
            kw["accum_out"] = accum_out
            writes.append(accum_out)
        if op1 is None:
            return self.add(eng, lambda e: e.tensor_scalar(out, in0, s1, None, op0, **kw), reads=reads, writes=writes)
        return self.add(eng, lambda e: e.tensor_scalar(out, in0, s1, s2, op0, op1, **kw), reads=reads, writes=writes)

    def stt(self, out, in0, scalar, in1, op0, op1, eng="dve"):
        reads = [in0, in1]
        if not isinstance(scalar, (int, float)):
            reads.append(scalar)
        return self.add(eng, lambda e: e.scalar_tensor_tensor(out, in0, scalar, in1, op0, op1),
                        reads=reads, writes=[out])

    def copy(self, out, in_, eng="dve"):
        if eng == "act":
            return self.add(eng, lambda e: e.copy(out, in_), reads=[in_], writes=[out])
        return self.add(eng, lambda e: e.tensor_copy(out, in_), reads=[in_], writes=[out])

    def reduce(self, out, in_, op, axis=AX.X, eng="dve"):
        return self.add(eng, lambda e: e.tensor_reduce(out, in_, axis, op), reads=[in_], writes=[out])

    def scan(self, out, d0, d1, initial, op0, op1):
        reads = [d0, d1]
        if not isinstance(initial, (int, float)):
            reads.append(initial)
        return self.add("dve", lambda e: e.tensor_tensor_scan(out, d0, d1, initial, op0, op1),
                        reads=reads, writes=[out])

    def recip(self, out, in_):
        return self.add("dve", lambda e: e.reciprocal(out, in_), reads=[in_], writes=[out])

    def memset(self, ap, val, eng="dve"):
        return self.add(eng, lambda e: e.memset(ap, val), reads=[], writes=[ap])

    def dma(self, out, in_, q="sp", **kw):
        return self.add(q, lambda e: e.dma_start(out=out, in_=in_, **kw), reads=[in_], writes=[out], dma=True)

    def emit(self, final_wait_dmas=()):
        nc = self.nc
        dsem = [nc.alloc_semaphore(name=f"d_{i}") for i in range(ND)]
        nsig = {}
        for e, lst in self.ins.items():
            c = 0
            for ins in lst:
                if ins.dma:
                    continue
                if ins.signal:
                    ins.sidx = c
                    c += 1
            nsig[e] = c
        nsr = {e: max(1, -(-nsig[e] // 1800)) for e in self.engs}
        self.nsr = nsr
        rings = {e: [nc.alloc_semaphore(name=f"r_{e}_{i}") for i in range(nsr[e])] for e in self.engs}
        prog = self

        def body(ename):
            def f(engine):
                for ins in prog.ins[ename]:
                    for d in ins.deps:
                        if d.dma:
                            engine.wait_ge(dsem[d.dma_id % ND], 16 * (d.dma_id // ND + 1))
                        else:
                            engine.wait_ge(rings[d.eng][d.sidx % nsr[d.eng]], d.sidx // nsr[d.eng] + 1)
                    r = ins.fn(engine)
                    if ins.dma:
                        r.then_inc(dsem[ins.dma_id % ND], 16)
                    elif ins.signal:
                        r.then_inc(rings[ename][ins.sidx % nsr[ename]], 1)
                if ename == "sp":
                    for d in final_wait_dmas:
                        engine.wait_ge(dsem[d.dma_id % ND], 16 * (d.dma_id // ND + 1))
            return f

        with nc.Block() as block:
            block.sync(body("sp"))
            block.tensor(body("pe"))
            block.vector(body("dve"))
            block.scalar(body("act"))
            block.gpsimd(body("pool"))

from concourse.bass_utils import run_bass_kernel_spmd
import math

D = 1024
T = 2048
LC = 256
TT = T + LC
NCH = TT // 128
DEPTH = 2
DFF = 2816
EPS = 1e-6
NEG = -1.0e30
SLOT_EL = 8192

TBLOCKS = [(0, 256, True), (256, 768, False), (768, 1280, False), (1280, 1792, False), (1792, 2304, False)]


def _piece_list():
    pcs = [("A1", 8 * 768), ("A2", 8 * 640), ("SM1", 8 * 16)]
    for h in range(4):
        pcs.append((f"MH{h}", 8 * 512))
    pcs.append(("SM2", 8 * 16))
    for g in range(2):
        pcs.append((f"SG{g}", 8 * 768))
    for i in range(8):
        pcs.append((f"MG{i}", 8 * 384 + 3 * 4 * 128))
    pcs.append(("WO", 8 * 1024))
    for q in range(11):
        pcs.append((f"FF{q}", 8 * 512 + 2 * 1024))
    return pcs


def _k8(w, cols):
    sub = w[:, cols]
    K, n = sub.shape
    return np.ascontiguousarray(sub.reshape(K // 128, 128, n).transpose(1, 0, 2)).reshape(128, -1)


def _host_layer_weights(l, w_in, w_att_out, w_ml_out, w_ssd_out, w_o, w_up, w_down):
    wi = w_in[l]
    ar = np.arange
    qcols = np.concatenate([np.concatenate([c * 64 + ar(64), (4 + c) * 64 + ar(64)]) for c in range(4)])
    perm64 = np.concatenate([16 + ar(16), ar(16), 48 + ar(16), 32 + ar(16)])
    qperm = (qcols // 64) * 64 + perm64[qcols % 64]
    kcols = 512 + ar(128)
    kperm = 512 + (ar(128) // 64) * 64 + perm64[ar(128) % 64]
    vcols = 640 + ar(128)
    out = []
    out.append(_k8(wi, np.concatenate([qcols, kcols, vcols])))
    out.append(_k8(wi, np.concatenate([qperm, kperm])))
    g0 = 2816
    out.append(_k8(wi, np.concatenate([g0 + ar(4), g0 + 8 + ar(4), g0 + 4 + ar(4), g0 + 12 + ar(4)])))
    for h in range(4):
        out.append(_k8(wi, np.concatenate([768 + h * 128 + ar(128), 1280 + h * 128 + ar(128),
                                           1792 + h * 128 + ar(128), 2304 + h * 128 + ar(128)])))
    z0, x0, d0, m0 = 2832, 3344, 4368, 4384
    out.append(_k8(wi, d0 + ar(16)))
    for g in range(2):
        out.append(_k8(wi, np.concatenate([x0 + g * 256 + ar(256), x0 + 512 + g * 128 + ar(128),
                                           x0 + 768 + g * 128 + ar(128), z0 + g * 256 + ar(256)])))
    arow = np.concatenate([np.concatenate([c * 64 + ar(64), (4 + c) * 64 + ar(64)]) for c in range(4)])
    for i in range(8):
        cc = i * 128 + ar(128)
        a = _k8(wi, np.concatenate([m0 + cc, m0 + 1024 + cc, m0 + 2048 + cc]))
        b0 = _k8(w_att_out[l][arow], cc)
        b1 = _k8(w_ml_out[l], cc)
        b2 = _k8(w_ssd_out[l], cc)
        out.append(np.concatenate([a, b0, b1, b2], axis=1))
    out.append(_k8(w_o[l], ar(1024)))
    for q in range(11):
        j0 = 2 * q * 128
        a = _k8(w_up[l], np.concatenate([j0 + ar(256), DFF + j0 + ar(256)]))
        b = _k8(w_down[l][j0:j0 + 256], ar(1024))
        out.append(np.concatenate([a, b], axis=1))
    cat = np.concatenate(out, axis=1)
    return np.ascontiguousarray(cat, dtype=np.float32)


def _host_consts():
    ar = np.arange
    t = ar(T)
    inv = (10000.0 ** (-ar(16, dtype=np.float32) / 16)).astype(np.float32)
    row = (t // 64).astype(np.float32)
    col = (t % 64).astype(np.float32)
    ang_r = row[None, :] * inv[:, None]
    ang_c = col[None, :] * inv[:, None]
    cos64 = np.concatenate([np.cos(ang_r), np.cos(ang_r), np.cos(ang_c), np.cos(ang_c)], 0)
    sin64 = np.concatenate([-np.sin(ang_r), np.sin(ang_r), -np.sin(ang_c), np.sin(ang_c)], 0)
    cosT = np.concatenate([cos64, cos64], 0).astype(np.float32)
    sinT = np.concatenate([sin64, sin64], 0).astype(np.float32)
    q = ar(128)[:, None]
    k = ar(128)[None, :]
    maskA = np.zeros((128, 384), np.float32)
    maskA[:, 0:128] = np.where(k >= q, 0.0, -1.0e5)
    maskA[:, 256:384] = np.where(k <= q, 0.0, -1.0e5)
    s = ar(128)[:, None]
    tt = ar(128)[None, :]
    maskF = np.where(s <= tt, 0.0, NEG).astype(np.float32)
    maskB = np.where(s >= tt, 0.0, NEG).astype(np.float32)
    sel = np.zeros((8, 8, 128), np.float32)
    for r in range(8):
        sel[r, r, :] = 1.0
    ident = np.eye(128, dtype=np.float32)
    return dict(cosT=cosT, sinT=sinT, maskA=maskA, maskF=maskF, maskB=maskB, sel=sel.reshape(8, 1024), ident=ident)


class Arena:
    def __init__(self, nc, nbytes):
        self.t = nc.alloc_sbuf_tensor("arena", [128, nbytes // 2], BF16)
        self.top = 0
        self.cap = nbytes
        self.peak = 0

    def alloc(self, free, dtype, parts=128):
        if isinstance(free, int):
            free = (free,)
        n = 1
        for f in free:
            n *= f
        es = 4 if dtype == F32 else 2
        off = (self.top + 63) // 64 * 64
        self.top = off + n * es
        self.peak = max(self.peak, self.top)
        assert self.top <= self.cap, f"arena overflow {self.top} > {self.cap}"
        v = self.t[0:parts, off // 2:(off + n * es) // 2]
        if dtype == F32:
            v = v.bitcast(F32)
        if len(free) == 2:
            v = v.rearrange("p (a b) -> p a b", a=free[0])
        elif len(free) == 3:
            v = v.rearrange("p (a b c) -> p a b c", a=free[0], b=free[1])
        return v

    def mark(self):
        return self.top

    def release(self, m):
        self.top = m


def build(NB=2, L=DEPTH, dbg=False, stop_after=None):
    nc = bass.Bass("TRN2", target_bir_lowering=False)
    P = Prog(nc)

    def din(name, shape, dt=F32):
        P.skip_dram.add(name)
        return nc.dram_tensor(name, list(shape), dt, kind="ExternalInput").ap()

    def dscr(name, shape, dt):
        return nc.dram_tensor(name, list(shape), dt, kind="ExternalOutput" if dbg else "Internal").ap()

    pcs = _piece_list()
    poff = {}
    o = 0
    for nm, ne in pcs:
        poff[nm] = (o, ne)
        o += ne
    WTOT = o

    xT_d = din("xT", [NB, D, T])
    ctxT_d = din("ctxT", [NB, D, LC])
    cT_d = din("cT", [128, 24])
    wmod_d = din("wmod", [L, 128, 8 * 6144])
    bmod_d = din("bmod", [L, 128, 48])
    gn_d = din("gn", [128, (2 * L + 1) * 8])
    wl_d = [din(f"wl{l}", [128, WTOT]) for l in range(L)]
    cos_d = din("cosT", [128, T]); sin_d = din("sinT", [128, T])
    maskA_d = din("maskA", [128, 384]); maskF_d = din("maskF", [128, 128]); maskB_d = din("maskB", [128, 128])
    sel_d = din("sel", [8, 1024]); ident_d = din("ident", [128, 128])
    sink_d = din("sinkb", [128, L * 8])
    mlb_d = din("mlb", [8, L * 2])
    mlg_d = din("mlgain", [128, L * 512])
    cw_d = din("convw", [128, L * 8 * 6])
    sdt_d = din("ssddt", [8, L * 4])
    sdk_d = din("ssdskip", [128, L * 512])
    sng_d = din("ssdng", [128, L * 512])
    outT_d = nc.dram_tensor("outT", [NB, D, T], F32, kind="ExternalOutput").ap()

    brs_d = [dscr(f"dr_br{i}", [512, TT], BF16) for i in range(3)]
    yt_d = dscr("dr_yt", [D, TT], BF16)
    xs_d = dscr("dr_xs", [D, TT], F32)
    hdbg_d = dscr("dr_h", [D, TT], BF16) if dbg else None

    AR = Arena(nc, 204 * 1024)
    banks = [nc.alloc_psum_tensor(f"ps{i}", [128, 512], F32) for i in range(8)]
    st = {"rr": 0, "acc": 0, "kq": 0, "rrn": 3}

    def rr():
        b = banks[st["rr"] % st["rrn"]]
        st["rr"] += 1
        return b

    def kqb():
        b = banks[2 + st["kq"] % 4]
        st["kq"] += 1
        return b

    def accb():
        b = banks[6 + st["acc"] % 2]
        st["acc"] += 1
        return b

    IDF = AR.alloc(128, F32); P.dma(IDF, ident_d)
    IDB = AR.alloc(128, BF16); P.dma(IDB, ident_d, q="pool")
    ONESB = AR.alloc(128, BF16); P.memset(ONESB, 1.0)
    COS = AR.alloc(T, BF16); P.dma(COS.rearrange("p (a b) -> p a b", b=512), cos_d.rearrange("p (a b) -> p a b", b=512), q="pool")
    SIN = AR.alloc(T, BF16); P.dma(SIN.rearrange("p (a b) -> p a b", b=512), sin_d.rearrange("p (a b) -> p a b", b=512), q="pool")
    MASKA = AR.alloc(384, F32); P.dma(MASKA, maskA_d)
    MASKF = AR.alloc(128, F32); P.dma(MASKF, maskF_d)
    MASKB = AR.alloc(128, F32); P.dma(MASKB, maskB_d)
    SEL = AR.alloc((8, 128), F32, parts=8); P.dma(SEL, sel_d.rearrange("p (a b) -> p a b", a=8))
    SINKB = AR.alloc(L * 8, F32); P.dma(SINKB, sink_d)
    MLB = AR.alloc(L * 2, F32, parts=8); P.dma(MLB, mlb_d)
    NMLB = AR.alloc(L * 2, F32, parts=8); P.ts(NMLB, MLB, -1.0, None, ALU.mult)
    MLG = AR.alloc(L * 512, F32); P.dma(MLG, mlg_d)
    CW = AR.alloc(L * 48, F32); P.dma(CW, cw_d)
    SDT = AR.alloc(L * 4, F32, parts=8); P.dma(SDT, sdt_d)
    SA = AR.alloc(L * 4, F32, parts=8)
    P.act(SA, SDT, AF.Exp)
    P.ts(SA, SA, -1.0, None, ALU.mult)
    SDK = AR.alloc(L * 512, BF16); P.dma(SDK, sdk_d, q="pool")
    SNG = AR.alloc(L * 512, F32); P.dma(SNG, sng_d)
    GN = AR.alloc((2 * L + 1) * 8, F32); P.dma(GN, gn_d)
    EPST = AR.alloc(1, F32); P.memset(EPST, EPS)
    ONEF = AR.alloc(TT, F32, parts=8); P.memset(ONEF, 1.0)
    MOD = [AR.alloc((48, 3), F32) for _ in range(L)]
    GS = [AR.alloc((2, 8, 3), F32) for _ in range(L)]
    H = AR.alloc((8, TT), BF16)
    SLOTS = [AR.alloc(SLOT_EL, BF16) for _ in range(2)]

    wq = []
    for b in range(NB):
        for l in range(L):
            for nm, ne in pcs:
                wq.append((l, nm))
    wst = {"issued": 0, "cur": -1}

    def issue_next():
        i = wst["issued"]
        if i >= len(wq):
            return
        l, nm = wq[i]
        o, ne = poff[nm]
        CH = 512 if ne % 512 == 0 else 128
        P.dma(SLOTS[i % 2][:, 0:ne].rearrange("p (a b) -> p a b", b=CH),
              wl_d[l][:, o:o + ne].rearrange("p (a b) -> p a b", b=CH), q="pool")
        wst["issued"] += 1

    def next_piece(l, nm):
        wst["cur"] += 1
        i = wst["cur"]
        assert wq[i] == (l, nm), (wq[i], l, nm)
        while wst["issued"] <= i:
            issue_next()
        return SLOTS[i % 2]

    def prefetch():
        if wst["issued"] <= wst["cur"] + 1:
            issue_next()

    def wview(slot, off, kc, n):
        return slot[:, off:off + kc * n].rearrange("p (a b) -> p a b", a=kc)

    CT = AR.alloc(24, F32); P.dma(CT, cT_d)
    SC = AR.alloc(24, F32); P.act(SC, CT, AF.Silu)
    BM = AR.alloc(L * 48, F32)
    for l in range(L):
        P.dma(BM[:, l * 48:(l + 1) * 48], bmod_d[l])
    for l in range(L):
        for pc in range(12):
            WM = SLOTS[pc % 2].bitcast(F32).rearrange("p (a b) -> p a b", a=8)
            P.dma(WM, wmod_d[l].rearrange("p (a b) -> p a b", a=8)[:, :, pc * 512:(pc + 1) * 512])
            for cc in range(4):
                ps = rr()
                for kc in range(8):
                    P.mm(ps[:, 0:3], WM[:, kc, cc * 128:(cc + 1) * 128], SC[:, kc * 3:(kc + 1) * 3],
                         start=(kc == 0), stop=(kc == 7))
                col = pc * 4 + cc
                P.ts(MOD[l][:, col, :], ps[:, 0:3], BM[:, l * 48 + col:l * 48 + col + 1], None, ALU.add)
        for n in range(2):
            for kc in range(8):
                P.ts(GS[l][:, n, kc, :], MOD[l][:, (1 + 3 * n) * 8 + kc, :], 1.0,
                     GN[:, (2 * l + n) * 8 + kc:(2 * l + n) * 8 + kc + 1], ALU.add, ALU.mult)

    def modv(l, which, kc, who):
        return MOD[l][:, which * 8 + kc, who:who + 1]

    def norm_to(Xv, l, n, b, dst, gvec=None, blocks=TBLOCKS, dst_f32=False, toff=0):
        for (t0, t1, isctx) in blocks:
            who = 2 if isctx else b
            nt = t1 - t0
            m = AR.mark()
            SQ = AR.alloc((8, nt), BF16)
            for kc in range(8):
                P.act(SQ[:, kc, :], Xv[:, kc, t0:t1], AF.Square)
            ps = rr()
            for kc in range(8):
                P.mm(ps[:, 0:nt], ONESB, SQ[:, kc, :], start=(kc == 0), stop=(kc == 7))
            RS = AR.alloc(nt, F32)
            P.act(RS, ps[:, 0:nt], AF.Sqrt, bias=EPST, scale=1.0 / D)
            P.recip(RS, RS)
            TMP = AR.alloc((2, nt), F32)
            for kc in range(8):
                tv = TMP[:, kc % 2, :]
                P.tt(tv, Xv[:, kc, t0:t1], RS, ALU.mult)
                if gvec is None:
                    P.act(dst[:, kc, t0 - toff:t1 - toff], tv, AF.Identity, scale=GS[l][:, n, kc, who:who + 1],
                          bias=modv(l, 3 * n, kc, who))
                else:
                    P.act(dst[:, kc, t0 - toff:t1 - toff], tv, AF.Identity, scale=gvec[:, kc:kc + 1])
            AR.release(m)

    def proj_fm(W, c0, ncols, t0, t1):
        ps = rr()
        for kc in range(8):
            P.mm(ps[0:ncols, 0:t1 - t0], W[:, kc, c0:c0 + ncols], H[:, kc, t0:t1], start=(kc == 0), stop=(kc == 7))
        return ps[0:ncols, 0:t1 - t0]

    def proj_tm(W, c0, ncols, c):
        ps = rr()
        for kc in range(8):
            P.mm(ps[:, 0:ncols], H[:, kc, c * 128:(c + 1) * 128], W[:, kc, c0:c0 + ncols],
                 start=(kc == 0), stop=(kc == 7))
        return ps[:, 0:ncols]

    def rows_scan(dst, src, op0, op1, d0, reverse):
        if not reverse:
            P.scan(dst[:, 0:LC], d0[:, 0:LC], src[:, 0:LC], 0.0, op0, op1)
            P.scan(dst[:, LC:TT], d0[:, LC:TT], src[:, LC:TT], dst[:, LC - 1:LC], op0, op1)
        else:
            P.scan(dst[:, 0:LC][:, ::-1], d0[:, 0:LC][:, ::-1], src[:, 0:LC][:, ::-1], 0.0, op0, op1)
            P.scan(dst[:, LC:TT][:, ::-1], d0[:, LC:TT][:, ::-1], src[:, LC:TT][:, ::-1], dst[:, 0:1], op0, op1)

    def rows_to_cols(COLS, ci, R):
        for c0 in range(0, NCH, 6):
            ps = rr()
            for c in range(c0, c0 + 6):
                P.transpose(ps[:, (c - c0) * 8:(c - c0) * 8 + 8], R[:, c * 128:(c + 1) * 128], IDF[0:8, 0:8])
            P.copy(COLS[:, c0:c0 + 6, ci, :], ps[:, 0:48].rearrange("p (a b) -> p a b", a=6))

    def contribs(t):
        out = []
        for s in range(0, t):
            out.append((s, 0, False))
        out.append((t, 0, True))
        if t >= 2:
            for s in (0, 1):
                out.append((s, 1, False))
            for s in range(t + 1, NCH):
                out.append((s, 1, False))
        else:
            for s in range(t + 1, 2):
                out.append((s, 1, False))
        out.append((t, 1, True))
        return out

    MASKD = [MASKF, MASKB]

    def build_bcast(rowtile, r, dst):
        for bi, (t0, t1, isctx) in enumerate(TBLOCKS):
            ps = rr()
            P.mm(ps[:, 0:t1 - t0], SEL[:, r, :], rowtile[:, t0:t1])
            P.copy(dst[:, t0:t1], ps[:, 0:t1 - t0], eng=("act" if bi % 2 == 0 else "dve"))

    def run_pipelined(items, stage1, stage2, fin):
        LA = 3
        cidx = [k for k, it in enumerate(items) if it[0] == "c"]
        nxt = {cidx[j]: (cidx[j + LA] if j + LA < len(cidx) else None) for j in range(len(cidx))}
        hnd = {}
        for j in range(min(LA, len(cidx))):
            hnd[cidx[j]] = stage1(items[cidx[j]])
        for k, it in enumerate(items):
            if it[0] == "c":
                n = nxt[k]
                if n is not None:
                    hnd[n] = stage1(items[n])
                stage2(it, hnd.pop(k))
            else:
                fin(it)

    def decay_T(rowtile, r, t, s, masked, d, biascol):
        ps = rr()
        if masked:
            P.mm(ps[:, 0:128], IDF, MASKD[d], start=True, stop=False)
        P.mm(ps[:, 0:128], SEL[:, r, :], rowtile[:, t * 128:(t + 1) * 128], start=(not masked), stop=True)
        return ps

    for b in range(NB):
        for l in range(L):
            last = (l == L - 1)
            need_ctx = not last
            blocks = TBLOCKS if need_ctx else TBLOCKS[1:]
            m_layer = AR.mark()
            if l == 0:
                X = AR.alloc((8, TT), F32)
                P.dma(X[:, :, 0:LC], ctxT_d[b].rearrange("(a p) t -> p a t", p=128))
                P.dma(X[:, :, LC:TT], xT_d[b].rearrange("(a p) t -> p a t", p=128))
            norm_to(X, l, 0, b, H)
            if l > 0:
                P.dma(xs_d.rearrange("(a p) t -> p a t", p=128), X)
                AR.release(m_x)
                m_layer = m_x
            if dbg and l == 0 and b == 0:
                P.dma(hdbg_d.rearrange("(a p) t -> p a t", p=128), H)
            AR.release(m_layer)
            if stop_after == "norm1":
                break

            m0 = AR.mark()
            W1 = next_piece(l, "A1"); prefetch()
            W1v = wview(W1, 0, 8, 768)
            QT = AR.alloc((4, TT), BF16)
            KT = AR.alloc(TT, BF16)
            VP = [AR.alloc((NCH, 128), BF16) for _ in range(2)]
            P.memset(VP[0], 0.0); P.memset(VP[1], 0.0)
            W2 = next_piece(l, "A2")
            W2v = wview(W2, 0, 8, 640)
            if stop_after == "attv":
                for c in range(NCH):
                    ps = proj_tm(W1v, 640, 128, c)
                    P.copy(VP[0][:, c, 0:64], ps[:, 0:64], eng="act")
                    P.copy(VP[1][:, c, 64:128], ps[:, 64:128], eng="act")
                P.dma(brs_d[0][0:128, :], VP[0].rearrange("p a b -> p (a b)"))
                break
            if stop_after == "attr":
                ps = proj_fm(W1v, 0, 128, 256, 768)
                ps2 = proj_fm(W2v, 0, 128, 256, 768)
                T1 = AR.alloc(512, F32); T2 = AR.alloc(512, F32)
                P.tt(T1, ps, COS[:, 0:512], ALU.mult)
                P.tt(T2, ps2, SIN[:, 0:512], ALU.mult)
                P.tt(QT[:, 0, 256:768], T1, T2, ALU.add)
                ps = proj_fm(W1v, 0, 128, 0, 256)
                P.copy(QT[:, 0, 0:256], ps, eng="act")
                P.dma(brs_d[0].rearrange("(a p) t -> p a t", p=128), QT)
                break
            if stop_after == "attw":
                ps = proj_fm(W1v, 0, 128, 256, 768)
                P.copy(QT[:, 0, 256:768], ps, eng="act")
                ps = proj_tm(W1v, 640, 128, 3)
                P.copy(VP[0][:, 3, 0:64], ps[:, 0:64], eng="act")
                P.dma(brs_d[0].rearrange("(a p) t -> p a t", p=128), QT)
                break
            for (t0, t1, isctx) in TBLOCKS:
                nt = t1 - t0
                for c in range(5):
                    if isctx and c < 4 and not need_ctx:
                        continue
                    dst = QT[:, c, t0:t1] if c < 4 else KT[:, t0:t1]
                    ps = proj_fm(W1v, c * 128, 128, t0, t1)
                    if isctx:
                        P.copy(dst, ps, eng="act")
                    else:
                        ps2 = proj_fm(W2v, c * 128, 128, t0, t1)
                        m = AR.mark()
                        T1 = AR.alloc(nt, F32); T2 = AR.alloc(nt, F32)
                        P.tt(T1, ps, COS[:, t0 - LC:t1 - LC], ALU.mult)
                        P.tt(T2, ps2, SIN[:, t0 - LC:t1 - LC], ALU.mult)
                        P.tt(dst, T1, T2, ALU.add)
                        AR.release(m)
            for c in range(NCH):
                ps = proj_tm(W1v, 640, 128, c)
                P.copy(VP[0][:, c, 0:64], ps[:, 0:64], eng="act")
                P.copy(VP[1][:, c, 64:128], ps[:, 64:128], eng="act")
            prefetch()
            ATs = AR.alloc((4, TT), BF16)
            if not need_ctx:
                P.memset(ATs[:, :, 0:LC], 0.0)
            if stop_after == "attproj":
                P.dma(brs_d[0].rearrange("(a p) t -> p a t", p=128), QT)
                break
            qblocks = list(range(0 if need_ctx else 2, NCH))
            if stop_after == "att1":
                qblocks = qblocks[:1]
            st["rrn"] = 6
            NSET = 3
            bufs = [dict(SM=AR.alloc(640, F32), ST4=AR.alloc(8, F32), PB=AR.alloc(640, BF16),
                         PN=AR.alloc(640, BF16), PTs=AR.alloc(640, BF16)) for _ in range(NSET)]
            its = []
            for qb in qblocks:
                lat = qb >= 2
                loc = [k for k in (qb - 1, qb, qb + 1) if 2 <= k < NCH] if lat else []
                for c in range(4):
                    for half in range(2):
                        its.append(dict(qb=qb, lat=lat, loc=loc, c=c, half=half, nl=len(loc) * 128,
                                        keych=[0, 1] + loc, mo=(0 if (not lat or qb - 1 >= 2) else 128)))
            ast = {"acc": None}

            def at_A(it):
                psl = slice(64 * it["half"], 64 * it["half"] + 64)
                lhsT = QT[psl, it["c"], it["qb"] * 128:(it["qb"] + 1) * 128]
                psB = rr()
                P.mm(psB[:, 0:256], lhsT, KT[psl, 0:256])
                psA = None
                if it["lat"]:
                    psA = rr()
                    P.mm(psA[:, 0:it["nl"]], lhsT, KT[psl, it["loc"][0] * 128:(it["loc"][-1] + 1) * 128])
                return psB, psA

            def at_BC(it, hnd, bf):
                psB, psA = hnd
                SM, ST4, PB, PN, PTs = bf["SM"], bf["ST4"], bf["PB"], bf["PN"], bf["PTs"]
                nl, mo, half, c, qb = it["nl"], it["mo"], it["half"], it["c"], it["qb"]
                head = c + 4 * half
                if half == 0:
                    ast["acc"] = accb()
                acc = ast["acc"]
                P.copy(SM[:, 0:256], psB[:, 0:256], eng="act")
                if it["lat"]:
                    P.tt(SM[:, 256:256 + nl], psA[:, 0:nl], MASKA[:, mo:mo + nl], ALU.add)
                ntot = 256 + nl
                sk = SINKB[:, l * 8 + head:l * 8 + head + 1]
                P.reduce(ST4[:, 0:1], SM[:, 0:ntot], ALU.max)
                P.ts(ST4[:, 1:2], ST4[:, 0:1], 0.125, sk, ALU.mult, ALU.max)
                P.ts(ST4[:, 2:3], ST4[:, 1:2], -1.0, None, ALU.mult)
                P.act(PB[:, 0:ntot], SM[:, 0:ntot], AF.Exp, bias=ST4[:, 2:3], scale=0.125, accum_out=ST4[:, 3:4])
                P.act(ST4[:, 4:5], ST4[:, 2:3], AF.Exp, bias=sk)
                P.tt(ST4[:, 5:6], ST4[:, 3:4], ST4[:, 4:5], ALU.add)
                P.recip(ST4[:, 6:7], ST4[:, 5:6])
                P.ts(PN[:, 0:ntot], PB[:, 0:ntot], ST4[:, 6:7], None, ALU.mult)
                psT = rr().bitcast(BF16)
                keych = it["keych"]
                nk = len(keych)
                for j in range(nk):
                    P.transpose(psT[:, j * 128:(j + 1) * 128], PN[:, j * 128:(j + 1) * 128], IDB)
                P.copy(PTs[:, 0:nk * 128], psT[:, 0:nk * 128], eng="act")
                for j, kc_ in enumerate(keych):
                    P.mm(acc[:, 0:128], VP[half][:, kc_, :], PTs[:, j * 128:(j + 1) * 128],
                         start=(half == 0 and j == 0), stop=(half == 1 and j == nk - 1))
                if half == 1:
                    P.copy(ATs[:, c, qb * 128:(qb + 1) * 128], acc[:, 0:128])

            hnd = {}
            if its:
                hnd[0] = at_A(its[0])
            for j, it in enumerate(its):
                if j + 1 < len(its):
                    hnd[j + 1] = at_A(its[j + 1])
                at_BC(it, hnd.pop(j), bufs[j % NSET])
            st["rrn"] = 3
            P.dma(brs_d[0].rearrange("(a p) t -> p a t", p=128), ATs)
            AR.release(m0)
            if stop_after in ("att", "att1"):
                break

            st["rrn"] = 2
            m0 = AR.mark()
            WS = next_piece(l, "SM1"); prefetch()
            WSv = wview(WS, 0, 8, 16)
            NMR = [AR.alloc(TT, F32, parts=8) for _ in range(2)]
            COLS = AR.alloc((NCH, 4, 8), F32)
            MTs = AR.alloc((4, TT), BF16)
            m_rows = AR.mark()
            LI = AR.alloc(TT, F32, parts=8); LF = AR.alloc(TT, F32, parts=8)
            for (t0, t1, isctx) in TBLOCKS:
                ps = proj_fm(WSv, 0, 8, t0, t1)
                P.act(LI[:, t0:t1], ps, AF.Identity, bias=MLB[:, 2 * l:2 * l + 1])
                ps = proj_fm(WSv, 8, 8, t0, t1)
                P.act(LF[:, t0:t1], ps, AF.Exp, bias=NMLB[:, 2 * l + 1:2 * l + 2], scale=-1.0)
            P.act(LF, LF, AF.Ln, bias=1.0)
            P.ts(LF, LF, -1.0, None, ALU.mult)
            m1 = AR.mark()
            for d in range(2):
                T1 = AR.alloc(TT, F32, parts=8); T4 = AR.alloc(TT, F32, parts=8)
                T2 = NMR[d]
                rows_scan(T1, LF, ALU.mult, ALU.add, ONEF, d == 1)
                pass
                P.tt(T1, LI, T1, ALU.subtract)
                rows_scan(T2, T1, ALU.max, ALU.max, T1, d == 1)
                P.tt(T4, LI, T1, ALU.subtract)
                P.tt(T4, T4, T2, ALU.add)
                P.act(T4, T4, AF.Exp, scale=-1.0)
                rows_to_cols(COLS, d * 2 + 0, T1)
                rows_to_cols(COLS, d * 2 + 1, T4)
                P.ts(T2, T2, -1.0, None, ALU.mult)
                AR.release(m1)
            AR.release(m_rows)
            if stop_after == "mlrows":
                P.dma(brs_d[1][0:128, 0:NCH * 32], COLS.rearrange("p a b c -> p (a b c)").bitcast(BF16)[:, 0:NCH * 32])
                P.dma(brs_d[1][128:136, :], NMR[0].bitcast(BF16)[:, 0:TT])
                break
            if not need_ctx:
                P.memset(MTs[:, :, 0:LC], 0.0)
            for h in range(4):
                WH = next_piece(l, f"MH{h}"); prefetch()
                WHv = wview(WH, 0, 8, 512)
                m2 = AR.mark()
                QTh = AR.alloc(TT, BF16); KTh = AR.alloc(TT, BF16)
                VA = AR.alloc((NCH, 132), BF16); OS = AR.alloc((NCH, 128), BF16)
                P.memset(VA[:, :, 128:129], 1.0)
                for (t0, t1, isctx) in TBLOCKS:
                    ps = proj_fm(WHv, 0, 128, t0, t1)
                    P.copy(QTh[:, t0:t1], ps, eng="act")
                    ps = proj_fm(WHv, 128, 128, t0, t1)
                    P.ts(KTh[:, t0:t1], ps, 128.0 ** -0.5, None, ALU.mult)
                for c in range(NCH):
                    ps = proj_tm(WHv, 256, 256, c)
                    P.copy(VA[:, c, 0:128], ps[:, 0:128])
                    P.act(OS[:, c, :], ps[:, 128:256], AF.Sigmoid)
                ring = [(AR.alloc(128, F32), AR.alloc(128, BF16)) for _ in range(6)]
                mring = [AR.alloc(128, F32) for _ in range(2)]
                BC = [AR.alloc(TT, F32) for _ in range(2)]
                for d in range(2):
                    build_bcast(NMR[d], d * 4 + h, BC[d])
                mst = {"rc": 0, "accs": None, "mc": 0}
                items = []
                for t in (range(NCH) if need_ctx else range(2, NCH)):
                    cl = contribs(t)
                    cnt = [sum(1 for x in cl if x[1] == d) for d in range(2)]
                    seen = [0, 0]
                    for ci_, (s_, d, masked) in enumerate(cl):
                        items.append(("c", t, s_, d, masked, seen[d] == 0, seen[d] == cnt[d] - 1, ci_ == 0))
                        seen[d] += 1
                    items.append(("fin", t))

                def ml_s1(it):
                    _, t, s_, d, masked, first, lastc, newt = it
                    pk = kqb()
                    P.mm(pk[:, 0:128], KTh[:, s_ * 128:(s_ + 1) * 128], QTh[:, t * 128:(t + 1) * 128])
                    return pk, None

                def ml_s2(it, hh):
                    _, t, s_, d, masked, first, lastc, newt = it
                    pk, psD = hh
                    if newt:
                        bk = accb()
                        mst["accs"] = [bk[:, 0:256], bk[:, 256:512]]
                    r = d * 4 + h
                    DT_, ST_ = ring[mst["rc"] % 6]
                    mst["rc"] += 1
                    src = BC[d][:, t * 128:(t + 1) * 128]
                    if masked:
                        tm = mring[mst["mc"] % 2]
                        mst["mc"] += 1
                        P.tt(tm, src, MASKD[d], ALU.add)
                        src = tm
                    P.act(DT_, src, AF.Exp, bias=COLS[:, s_, d * 2, r:r + 1])
                    P.tt(ST_, pk[:, 0:128], DT_, ALU.mult)
                    P.mm(mst["accs"][d][:, 0:129], ST_, VA[:, s_, 0:129], start=first, stop=lastc)

                def ml_fin(it):
                    t = it[1]
                    accs = mst["accs"]
                    m3 = AR.mark()
                    HD = []
                    for d in range(2):
                        r = d * 4 + h
                        A_ = AR.alloc(132, F32)
                        P.copy(A_[:, 0:129], accs[d][:, 0:129], eng="act")
                        S4 = AR.alloc(4, F32)
                        P.stt(S4[:, 0:1], A_[:, 128:129], -1.0, A_[:, 128:129], ALU.mult, ALU.max)
                        P.tt(S4[:, 1:2], S4[:, 0:1], COLS[:, t, d * 2 + 1, r:r + 1], ALU.max)
                        P.recip(S4[:, 2:3], S4[:, 1:2])
                        Hd = AR.alloc(128, F32)
                        P.ts(Hd, A_[:, 0:128], S4[:, 2:3], None, ALU.mult)
                        HD.append(Hd)
                    HS = AR.alloc(128, F32)
                    P.tt(HS, HD[0], HD[1], ALU.add)
                    S4 = AR.alloc(4, F32)
                    JK = AR.alloc(128, F32)
                    P.act(JK, HS, AF.Square, accum_out=S4[:, 0:1])
                    P.act(S4[:, 1:2], S4[:, 0:1], AF.Ln, bias=EPST, scale=1.0 / 128)
                    P.act(S4[:, 2:3], S4[:, 1:2], AF.Exp, scale=-0.5)
                    O1 = AR.alloc(128, F32)
                    P.stt(O1, HS, S4[:, 2:3], MLG[:, l * 512 + h * 128:l * 512 + (h + 1) * 128], ALU.mult, ALU.mult)
                    O2 = AR.alloc(128, BF16)
                    P.tt(O2, O1, OS[:, t, :], ALU.mult)
                    pT = rr().bitcast(BF16)
                    P.transpose(pT[:, 0:128], O2, IDB)
                    P.copy(MTs[:, h, t * 128:(t + 1) * 128], pT[:, 0:128], eng="act")
                    AR.release(m3)

                run_pipelined(items, ml_s1, ml_s2, ml_fin)
                AR.release(m2)
                if stop_after == "mlh1":
                    break
            P.dma(brs_d[1].rearrange("(a p) t -> p a t", p=128), MTs)
            AR.release(m0)
            if stop_after in ("ml", "mlh1"):
                break

            m0 = AR.mark()
            WS = next_piece(l, "SM2"); prefetch()
            WSv = wview(WS, 0, 8, 16)
            ACS = [AR.alloc(TT, F32, parts=8) for _ in range(2)]
            COLS = AR.alloc((NCH, 4, 8), F32)
            m1 = AR.mark()
            for d in range(2):
                DTr = AR.alloc(TT, F32, parts=8); T1 = AR.alloc(TT, F32, parts=8)
                for (t0, t1, isctx) in TBLOCKS:
                    ps = proj_fm(WSv, 8 * d, 8, t0, t1)
                    P.act(DTr[:, t0:t1], ps, AF.Exp, bias=SDT[:, 4 * l + d:4 * l + d + 1])
                P.act(DTr, DTr, AF.Ln, bias=1.0)
                P.ts(T1, DTr, SA[:, 4 * l + 2 + d:4 * l + 3 + d], None, ALU.mult)
                rows_scan(ACS[d], T1, ALU.mult, ALU.add, ONEF, d == 1)
                P.ts(T1, ACS[d], -1.0, None, ALU.mult)
                rows_to_cols(COLS, d * 2 + 0, T1)
                rows_to_cols(COLS, d * 2 + 1, DTr)
                AR.release(m1)
            YG = AR.alloc((NCH, 512), BF16)
            SSQ = AR.alloc((NCH, 8), F32)
            trange = list(range(NCH) if need_ctx else range(2, NCH))
            for g in range(2):
                WG = next_piece(l, f"SG{g}"); prefetch()
                WGv = wview(WG, 0, 8, 768)
                m2 = AR.mark()
                XTOK = AR.alloc((NCH, 256), BF16)
                BTg = AR.alloc(TT, BF16); CTg = AR.alloc(TT, BF16)
                ZS = AR.alloc((NCH, 256), BF16)
                m3 = AR.mark()
                PAD = AR.alloc(2312, F32); ACC = AR.alloc(2308, F32); XF = AR.alloc(TT, BF16)
                P.memset(PAD[:, 0:2], 0.0); P.memset(PAD[:, 258:262], 0.0); P.memset(PAD[:, 2310:2312], 0.0)
                for ch in range(4):
                    cidx = [g * 2, g * 2 + 1, 4 + g, 6 + g][ch]
                    cw = CW[:, (l * 8 + cidx) * 6:(l * 8 + cidx) * 6 + 6]
                    for (t0, t1, isctx) in TBLOCKS:
                        ps = proj_fm(WGv, ch * 128, 128, t0, t1)
                        o0 = 2 + t0 if isctx else 262 + (t0 - LC)
                        P.copy(PAD[:, o0:o0 + (t1 - t0)], ps, eng="act")
                    P.ts(ACC, PAD[:, 0:2308], cw[:, 0:1], None, ALU.mult)
                    for j in range(1, 5):
                        P.stt(ACC, PAD[:, j:j + 2308], cw[:, j:j + 1], ACC, ALU.mult, ALU.add)
                    dst = XF if ch < 2 else (BTg if ch == 2 else CTg)
                    P.act(dst[:, 0:LC], ACC[:, 0:LC], AF.Silu, bias=cw[:, 5:6])
                    P.act(dst[:, LC:TT], ACC[:, 260:2308], AF.Silu, bias=cw[:, 5:6])
                    if ch < 2:
                        for c0 in range(0, NCH, 4):
                            cs = list(range(c0, min(c0 + 4, NCH)))
                            pT = rr().bitcast(BF16)
                            for c in cs:
                                P.transpose(pT[:, (c - c0) * 128:(c - c0 + 1) * 128], XF[:, c * 128:(c + 1) * 128], IDB)
                            P.copy(XTOK[:, c0:c0 + len(cs), ch * 128:(ch + 1) * 128],
                                   pT[:, 0:len(cs) * 128].rearrange("p (a b) -> p a b", a=len(cs)))
                AR.release(m3)
                BTOK = None
                for c in range(NCH):
                    ps = proj_tm(WGv, 512, 256, c)
                    P.act(ZS[:, c, :], ps, AF.Silu)
                ring = [(AR.alloc(128, F32), AR.alloc(128, BF16)) for _ in range(6)]
                mring = [AR.alloc(128, F32) for _ in range(2)]
                BC = [AR.alloc(TT, F32) for _ in range(2)]
                sst = {"rc": 0, "acc": None, "mc": 0}
                for r4 in range(4):
                    hd = g * 4 + r4
                    for d in range(2):
                        build_bcast(ACS[d], hd, BC[d])
                    items = []
                    for t in trange:
                        cl = contribs(t)
                        for ci_, (s_, d, masked) in enumerate(cl):
                            items.append(("c", t, s_, d, masked, ci_ == 0, ci_ == len(cl) - 1))
                        items.append(("fin", t))

                    def sd_s1(it, hd=hd):
                        _, t, s_, d, masked, first, lastc = it
                        pk = kqb()
                        P.mm(pk[:, 0:128], BTg[:, s_ * 128:(s_ + 1) * 128], CTg[:, t * 128:(t + 1) * 128])
                        return pk, None

                    def sd_s2(it, hh, hd=hd, r4=r4):
                        _, t, s_, d, masked, first, lastc = it
                        pk, psD = hh
                        if first:
                            sst["acc"] = accb()
                        DT_, MT_ = ring[sst["rc"] % 6]
                        sst["rc"] += 1
                        src = BC[d][:, t * 128:(t + 1) * 128]
                        if masked:
                            tm = mring[sst["mc"] % 2]
                            sst["mc"] += 1
                            P.tt(tm, src, MASKD[d], ALU.add)
                            src = tm
                        P.act(DT_, src, AF.Exp, bias=COLS[:, s_, d * 2, hd:hd + 1])
                        P.stt(MT_, pk[:, 0:128], COLS[:, s_, d * 2 + 1, hd:hd + 1], DT_, ALU.mult, ALU.mult)
                        P.mm(sst["acc"][:, 0:64], MT_, XTOK[:, s_, r4 * 64:(r4 + 1) * 64], start=first, stop=lastc)

                    def sd_fin(it, hd=hd, r4=r4):
                        t = it[1]
                        acc = sst["acc"]
                        m4 = AR.mark()
                        Y1 = AR.alloc(64, F32)
                        P.tt(Y1, XTOK[:, t, r4 * 64:(r4 + 1) * 64],
                             SDK[:, l * 512 + hd * 64:l * 512 + (hd + 1) * 64], ALU.mult)
                        P.tt(Y1, Y1, acc[:, 0:64], ALU.add)
                        P.tt(YG[:, t, hd * 64:(hd + 1) * 64], Y1, ZS[:, t, r4 * 64:(r4 + 1) * 64], ALU.mult)
                        JK = AR.alloc(64, F32)
                        P.act(JK, YG[:, t, hd * 64:(hd + 1) * 64], AF.Square, accum_out=SSQ[:, t, hd:hd + 1])
                        AR.release(m4)

                    run_pipelined(items, sd_s1, sd_s2, sd_fin)
                AR.release(m2)
            STs = AR.alloc((4, TT), BF16)
            if not need_ctx:
                P.memset(STs[:, :, 0:LC], 0.0)
            for t in trange:
                m4 = AR.mark()
                S4 = AR.alloc(4, F32)
                P.reduce(S4[:, 0:1], SSQ[:, t, :], ALU.add)
                P.act(S4[:, 1:2], S4[:, 0:1], AF.Ln, bias=EPST, scale=1.0 / 512)
                P.act(S4[:, 2:3], S4[:, 1:2], AF.Exp, scale=-0.5)
                O2 = AR.alloc(512, BF16)
                P.stt(O2, YG[:, t, :], S4[:, 2:3], SNG[:, l * 512:(l + 1) * 512], ALU.mult, ALU.mult)
                pT = rr().bitcast(BF16)
                for ch in range(4):
                    P.transpose(pT[:, ch * 128:(ch + 1) * 128], O2[:, ch * 128:(ch + 1) * 128], IDB)
                P.copy(STs[:, :, t * 128:(t + 1) * 128], pT[:, 0:512].rearrange("p (a b) -> p a b", a=4), eng="act")
                AR.release(m4)
            P.dma(brs_d[2].rearrange("(a p) t -> p a t", p=128), STs)
            AR.release(m0)
            if stop_after == "ssd":
                break

            st["rrn"] = 3
            m0 = AR.mark()
            BR = [AR.alloc((4, TT), BF16) for _ in range(3)]
            for i in range(3):
                P.dma(BR[i], brs_d[i].rearrange("(a p) t -> p a t", p=128))
            for i in range(8):
                WMp = next_piece(l, f"MG{i}"); prefetch()
                Wg = wview(WMp, 0, 8, 384)
                Wo_ = WMp[:, 3072:3072 + 1536].rearrange("p (b a c) -> p b a c", b=3, a=4)
                m_i = AR.mark()
                YTi = AR.alloc(TT, BF16)
                for (t0, t1, isctx) in blocks:
                    nt = t1 - t0
                    m1 = AR.mark()
                    YA = AR.alloc(nt, F32); TM = AR.alloc(nt, F32)
                    for br in range(3):
                        psG = proj_fm(Wg, br * 128, 128, t0, t1)
                        SG = AR.alloc(nt, F32)
                        P.act(SG, psG, AF.Sigmoid)
                        psO = rr()
                        for kc in range(4):
                            P.mm(psO[:, 0:nt], Wo_[:, br, kc, :], BR[br][:, kc, t0:t1], start=(kc == 0), stop=(kc == 3))
                        if br == 0:
                            P.tt(YA, SG, psO[:, 0:nt], ALU.mult)
                        elif br == 1:
                            P.tt(TM, SG, psO[:, 0:nt], ALU.mult)
                            P.tt(YA, YA, TM, ALU.add, eng="pool")
                        else:
                            P.tt(TM, SG, psO[:, 0:nt], ALU.mult)
                            P.tt(YTi[:, t0:t1], YA, TM, ALU.add, eng="pool")
                    AR.release(m1)
                P.dma(yt_d[i * 128:(i + 1) * 128, blocks[0][0]:TT], YTi[:, blocks[0][0]:TT])
                AR.release(m_i)
            AR.release(m0)

            m_x = AR.mark()
            X = AR.alloc((8, TT), F32)
            if l == 0:
                P.dma(X[:, :, 0:LC], ctxT_d[b].rearrange("(a p) t -> p a t", p=128))
                P.dma(X[:, :, LC:TT], xT_d[b].rearrange("(a p) t -> p a t", p=128))
            else:
                P.dma(X, xs_d.rearrange("(a p) t -> p a t", p=128))
            WOp = next_piece(l, "WO"); prefetch()
            WOv = wview(WOp, 0, 8, 1024)
            for (t0, t1, isctx) in blocks:
                nt = t1 - t0
                who = 2 if isctx else b
                m1 = AR.mark()
                Yb = AR.alloc((8, nt), BF16)
                P.dma(Yb, yt_d.rearrange("(a p) t -> p a t", p=128)[:, :, t0:t1])
                for o in range(8):
                    ps = rr()
                    for kc in range(8):
                        P.mm(ps[:, 0:nt], WOv[:, kc, o * 128:(o + 1) * 128], Yb[:, kc, :], start=(kc == 0), stop=(kc == 7))
                    P.stt(X[:, o, t0:t1], ps[:, 0:nt], modv(l, 2, o, who), X[:, o, t0:t1], ALU.mult, ALU.add)
                AR.release(m1)
            if stop_after == "wo":
                break

            norm_to(X, l, 1, b, H, blocks=blocks)
            for q in range(11):
                WF = next_piece(l, f"FF{q}"); prefetch()
                Wu = wview(WF, 0, 8, 512)
                Wd = wview(WF, 4096, 2, 1024)
                for (t0, t1, isctx) in blocks:
                    nt = t1 - t0
                    who = 2 if isctx else b
                    m1 = AR.mark()
                    A_ = AR.alloc((2, nt), BF16)
                    for jj in range(2):
                        psg = proj_fm(Wu, jj * 128, 128, t0, t1)
                        SGt = AR.alloc(nt, F32)
                        P.act(SGt, psg, AF.Silu)
                        psu = proj_fm(Wu, 256 + jj * 128, 128, t0, t1)
                        P.tt(A_[:, jj, :], SGt, psu, ALU.mult)
                    for o in range(8):
                        ps = rr()
                        for jj in range(2):
                            P.mm(ps[:, 0:nt], Wd[:, jj, o * 128:(o + 1) * 128], A_[:, jj, :], start=(jj == 0), stop=(jj == 1))
                        P.stt(X[:, o, t0:t1], ps[:, 0:nt], modv(l, 5, o, who), X[:, o, t0:t1], ALU.mult, ALU.add)
                    AR.release(m1)
            if last:
                for (t0, t1, isctx) in [(LC + 256 * i, LC + 256 * (i + 1), False) for i in range(8)]:
                    m1 = AR.mark()
                    OUT = AR.alloc((8, t1 - t0), F32)
                    norm_to(X, l, 0, b, OUT, gvec=GN[:, 2 * L * 8:(2 * L + 1) * 8], blocks=[(t0, t1, isctx)], toff=t0)
                    P.dma(outT_d[b].rearrange("(a p) t -> p a t", p=128)[:, :, t0 - LC:t1 - LC], OUT)
                    AR.release(m1)
                AR.release(m_x)
        else:
            continue
        break

    outs = [d for d in P.dmas if False]
    final = [d for d in P.dmas]
    P.emit(final_wait_dmas=final[-ND:])
    AR.P = P
    return nc, AR


_CACHE = {}


def host_prep(inputs, NB=2, L=DEPTH):
    f32 = np.float32
    g = {k: np.asarray(v, dtype=f32) for k, v in inputs.items()}
    consts = _host_consts()
    wl = [_host_layer_weights(l, g["w_in"], g["w_att_out"], g["w_ml_out"], g["w_ssd_out"], g["w_o"], g["w_up"], g["w_down"])
          for l in range(L)]
    wmod = np.stack([_k8(g["w_mod"][l], np.arange(6144)) for l in range(L)])
    bmod = np.stack([g["b_mod"][l].reshape(48, 128).T for l in range(L)])
    gn = np.concatenate([np.concatenate([g["g_norm1"][l].reshape(8, 128).T, g["g_norm2"][l].reshape(8, 128).T], 1)
                         for l in range(L)] + [g["g_final"].reshape(8, 128).T], 1)
    sinkb = np.broadcast_to(g["att_sink"][:L].reshape(1, L * 8), (128, L * 8))
    mlb = np.concatenate([np.stack([g["ml_i_bias"][l].reshape(8), g["ml_f_bias"][l].reshape(8)], 1) for l in range(L)], 1)
    mlgain = np.broadcast_to(g["ml_head_gain"][:L].reshape(1, L * 512), (128, L * 512))
    cw = []
    for l in range(L):
        for ci in range(8):
            ch = slice(ci * 128, (ci + 1) * 128)
            cw.append(np.concatenate([g["ssd_conv_w"][l][:, ch].T, g["ssd_conv_b"][l][ch][:, None]], 1))
    convw = np.concatenate(cw, 1)
    sdt = np.concatenate([np.stack([g["ssd_dt_bias"][l][0], g["ssd_dt_bias"][l][1], g["ssd_a_log"][l][0], g["ssd_a_log"][l][1]], 1)
                          for l in range(L)], 1)
    ssdskip = np.broadcast_to(np.repeat(g["ssd_d"][:L], 64, axis=1).reshape(1, L * 512), (128, L * 512))
    ssdng = np.broadcast_to(g["ssd_norm_gain"][:L].reshape(1, L * 512), (128, L * 512))
    shared = dict(wmod=wmod, bmod=bmod, gn=gn, sinkb=sinkb, mlb=mlb, mlgain=mlgain, convw=convw, ssddt=sdt,
                  ssdskip=ssdskip, ssdng=ssdng, **consts)
    for l in range(L):
        shared[f"wl{l}"] = wl[l]
    shared = {k: np.ascontiguousarray(v, dtype=f32) for k, v in shared.items()}
    ncores = g["x"].shape[0] // NB
    maps = []
    for c in range(ncores):
        bs = slice(c * NB, (c + 1) * NB)
        m = dict(shared)
        m["xT"] = np.ascontiguousarray(g["x"][bs].transpose(0, 2, 1))
        m["ctxT"] = np.ascontiguousarray(g["ctx"][bs].transpose(0, 2, 1))
        cc = np.stack([g["c"][c * NB + i] if i < NB else g["c_ctx"] for i in range(2)] + [g["c_ctx"]], 0)
        if NB == 1:
            cc = np.stack([g["c"][c], g["c"][c], g["c_ctx"]], 0)
        silu_in = cc.reshape(3, 8, 128).transpose(2, 1, 0).reshape(128, 24)
        m["cT"] = np.ascontiguousarray(silu_in)
        maps.append(m)
    return maps


def kernel(**inputs):
    NB = 2
    if "nc" not in _CACHE:
        _CACHE["nc"] = build(NB=NB)[0]
    nc = _CACHE["nc"]
    maps = host_prep(inputs, NB=NB)
    res = run_bass_kernel_spmd(nc, maps, core_ids=list(range(len(maps))))
    outs = [r["outT"] for r in res.results]
    full = np.concatenate(outs, 0)
    return np.ascontiguousarray(full.transpose(0, 2, 1)).astype(np.float32)
```

```python
import numpy as np
import concourse.bass as bass
import concourse.mybir as mybir

F32 = mybir.dt.float32
BF16 = mybir.dt.bfloat16
AF = mybir.ActivationFunctionType
ALU = mybir.AluOpType
AX = mybir.AxisListType

NS = 4
ND = 24


class Instr:
    __slots__ = ("eng", "idx", "fn", "dma", "deps", "signal", "clock", "dma_id", "sidx", "dsem", "dval")

    def __init__(self, eng, idx, fn, dma):
        self.eng = eng
        self.idx = idx
        self.fn = fn
        self.dma = dma
        self.deps = []
        self.signal = False
        self.clock = None
        self.dma_id = None
        self.sidx = None


def ap_box(ap):
    t = ap.tensor
    dims = list(ap.ap)
    pstride, pn = dims[0]
    shape = list(t.shape)
    per_part = 1
    for s in shape[1:]:
        per_part *= int(s)
    off = int(ap.offset)
    if pstride == 0:
        pstride_eff = per_part
    else:
        pstride_eff = pstride
    p0 = off // per_part
    f0 = off - p0 * per_part
    lo = 0
    hi = 0
    for st, cn in dims[1:]:
        ext = (int(cn) - 1) * int(st)
        if ext >= 0:
            hi += ext
        else:
            lo += ext
    esz = mybir.dt.size(t.dtype) if hasattr(mybir.dt, "size") else None
    return (t.name, p0, p0 + int(pn), f0 + lo, f0 + hi + 1)


class Prog:
    def __init__(self, nc):
        self.nc = nc
        self.engs = {"pe": nc.tensor, "dve": nc.vector, "act": nc.scalar, "pool": nc.gpsimd, "sp": nc.sync}
        self.ins = {e: [] for e in self.engs}
        self.track = {}
        self.known = {e: {} for e in self.engs}
        self.ndma = 0
        self.dmas = []
        self.esize = {}
        self.skip_dram = set()
        self.dma_pools = {}

    def _box(self, ap):
        t = ap.tensor
        sp = str(type(t).__name__)
        if sp.startswith("DRam"):
            if t.name in self.skip_dram:
                return None
            off = int(ap.offset); lo = 0; hi = 0
            for st, cn in ap.ap:
                ext = (int(cn) - 1) * int(st)
                if ext >= 0: hi += ext
                else: lo += ext
            es = {F32: 4, BF16: 2}.get(t.dtype, 4)
            return (t.name, 0, 1, (off + lo) * es, (off + hi + 1) * es)
        name, p0, p1, f0, f1 = ap_box(ap)
        if sp.startswith("PSum"):
            return (name, 0, 128, 0, 2048)
        es = {F32: 4, BF16: 2}.get(t.dtype, None)
        if es is None:
            es = int(ap.nbytes // max(1, ap.size)) if hasattr(ap, "nbytes") else 4
        return (name, p0, p1, f0 * es, f1 * es)

    def _collect(self, ins, box, write):
        name, p0, p1, f0, f1 = box
        BIN = 4096 if not name.startswith("dr_") else (1 << 22)
        bins = range(f0 // BIN, (f1 - 1) // BIN + 1)
        tr = self.track.setdefault(name, {})
        deps = []
        seen = set()
        newrec = (ins, write, p0, p1, f0, f1)
        for bi in bins:
            recs = tr.get(bi)
            if recs is None:
                tr[bi] = [newrec]
                continue
            keep = self._scan(recs, ins, write, p0, p1, f0, f1, deps, seen, bi * BIN, (bi + 1) * BIN,
                              name_is_psum=name.startswith("ps"))
            keep.append(newrec)
            tr[bi] = keep
        return deps

    def _scan(self, recs, ins, write, p0, p1, f0, f1, deps, seen, b0, b1, name_is_psum=False):
        keep = []
        for r in recs:
            rin, rw, q0, q1, g0, g1 = r
            ov = not (q1 <= p0 or p1 <= q0 or g1 <= f0 or f1 <= g0)
            if not ov:
                keep.append(r)
                continue
            if (write or rw or (name_is_psum and rin.eng != ins.eng)) and id(rin) not in seen:
                seen.add(id(rin))
                deps.append((rin, rw))
            covered = write and p0 <= q0 and q1 <= p1 and f0 <= max(g0, b0) and min(g1, b1) <= f1
            if covered:
                continue
            if (not write) and (not rw) and rin.eng == ins.eng and not ins.dma and not rin.dma \
                    and q0 == p0 and q1 == p1 and g0 == f0 and g1 == f1:
                continue
            keep.append(r)
        return keep

    def add(self, eng, fn, reads=(), writes=(), dma=False):
        ins = Instr(eng, len(self.ins[eng]), fn, dma)
        alld = {}
        for ap in reads:
            b = self._box(ap)
            if b is None:
                continue
            for d, dw in self._collect(ins, b, False):
                alld[id(d)] = (d, True)
        for ap in writes:
            b = self._box(ap)
            if b is None:
                continue
            for d, dw in self._collect(ins, b, True):
                prev = alld.get(id(d))
                alld[id(d)] = (d, (prev[1] if prev else False) or False)
        known = self.known[eng]
        if dma:
            base, size = (16, 8) if eng == "pool" else (0, 16)
            lst = self.dma_pools.setdefault(base, [])
            ins.dma_id = self.ndma
            self.ndma += 1
            self.dmas.append(ins)
            ins.dsem = base + len(lst) % size
            ins.dval = len(lst) // size + 1
            if len(lst) >= size:
                d = lst[len(lst) - size]
                alld[id(d)] = (d, True)
            lst.append(ins)
        need = []
        for d, raw in sorted(alld.values(), key=lambda x: -(x[0].dma_id if x[0].dma else x[0].idx)):
            if d is ins:
                continue
            if d.dma:
                key = ("dma", d.dsem)
                val = d.dval
                if known.get(key, 0) >= val:
                    continue
                need.append(d)
            else:
                if d.eng == eng and not dma:
                    if eng == "pe":
                        continue
                    if not raw:
                        continue
                if known.get(d.eng, -1) >= d.idx:
                    continue
                need.append(d)
        for d in need:
            d.signal = True
            if d.dma:
                key = ("dma", d.dsem)
                known[key] = max(known.get(key, 0), d.dval)
            else:
                known[d.eng] = max(known.get(d.eng, -1), d.idx)
            if d.clock:
                for k, v in d.clock.items():
                    if known.get(k, -1) < v:
                        known[k] = v
        ins.deps = need
        ins.clock = dict(known)
        if not dma:
            ins.clock[eng] = max(ins.clock.get(eng, -1), ins.idx - 1)
        self.ins[eng].append(ins)
        return ins

    def mm(self, out, lhsT, rhs, start=True, stop=True, **kw):
        return self.add("pe", lambda e: e.matmul(out, lhsT, rhs, start=start, stop=stop, **kw),
                        reads=[lhsT, rhs], writes=[out])

    def transpose(self, out, in_, ident):
        return self.add("pe", lambda e: e.transpose(out, in_, ident), reads=[in_, ident], writes=[out])

    def act(self, out, in_, func, bias=None, scale=None, accum_out=None, eng="act"):
        kw = {}
        reads = [in_]
        if bias is not None:
            kw["bias"] = bias
            if not isinstance(bias, (int, float)):
                reads.append(bias)
        if scale is not None:
            kw["scale"] = scale
            if not isinstance(scale, (int, float)):
                reads.append(scale)
        writes = [out]
        if accum_out is not None:
            kw["accum_out"] = accum_out
            writes.append(accum_out)
        return self.add(eng, lambda e: e.activation(out, in_, func, **kw), reads=reads, writes=writes)

    def tt(self, out, in0, in1, op, eng="dve"):
        return self.add(eng, lambda e: e.tensor_tensor(out, in0, in1, op), reads=[in0, in1], writes=[out])

    def ts(self, out, in0, s1, s2, op0, op1=None, eng="dve", accum_out=None):
        reads = [in0]
        for s in (s1, s2):
            if s is not None and not isinstance(s, (int, float)):
                reads.append(s)
        writes = [out]
        kw = {}
        if accum_out is not None:
            kw["accum_out"] = accum_out
            writes.append(accum_out)
        if op1 is None:
            return self.add(eng, lambda e: e.tensor_scalar(out, in0, s1, None, op0, **kw), reads=reads, writes=writes)
        return self.add(eng, lambda e: e.tensor_scalar(out, in0, s1, s2, op0, op1, **kw), reads=reads, writes=writes)

    def stt(self, out, in0, scalar, in1, op0, op1, eng="dve"):
        reads = [in0, in1]
        if not isinstance(scalar, (int, float)):
            reads.append(scalar)
        return self.add(eng, lambda e: e.scalar_tensor_tensor(out, in0, scalar, in1, op0, op1),
                        reads=reads, writes=[out])

    def copy(self, out, in_, eng="dve"):
        if eng == "act":
            return self.add(eng, lambda e: e.copy(out, in_), reads=[in_], writes=[out])
        return self.add(eng, lambda e: e.tensor_copy(out, in_), reads=[in_], writes=[out])

    def reduce(self, out, in_, op, axis=AX.X, eng="dve"):
        return self.add(eng, lambda e: e.tensor_reduce(out, in_, axis, op), reads=[in_], writes=[out])

    def scan(self, out, d0, d1, initial, op0, op1):
        reads = [d0, d1]
        if not isinstance(initial, (int, float)):
            reads.append(initial)
        return self.add("dve", lambda e: e.tensor_tensor_scan(out, d0, d1, initial, op0, op1),
                        reads=reads, writes=[out])

    def recip(self, out, in_):
        return self.add("dve", lambda e: e.reciprocal(out, in_), reads=[in_], writes=[out])

    def memset(self, ap, val, eng="dve"):
        return self.add(eng, lambda e: e.memset(ap, val), reads=[], writes=[ap])

    def dma(self, out, in_, q="sp", **kw):
        return self.add(q, lambda e: e.dma_start(out=out, in_=in_, **kw), reads=[in_], writes=[out], dma=True)

    def emit(self, final_wait_dmas=()):
        nc = self.nc
        dsem = [nc.alloc_semaphore(name=f"d_{i}") for i in range(ND)]
        nsig = {}
        for e, lst in self.ins.items():
            c = 0
            for ins in lst:
                if ins.dma:
                    continue
                if ins.signal:
                    ins.sidx = c
                    c += 1
            nsig[e] = c
        nsr = {e: max(1, -(-nsig[e] // 1800)) for e in self.engs}
        self.nsr = nsr
        rings = {e: [nc.alloc_semaphore(name=f"r_{e}_{i}") for i in range(nsr[e])] for e in self.engs}
        prog = self

        def body(ename):
            def f(engine):
                for ins in prog.ins[ename]:
                    for d in ins.deps:
                        if d.dma:
                            engine.wait_ge(dsem[d.dsem], 16 * d.dval)
                        else:
                            engine.wait_ge(rings[d.eng][d.sidx % nsr[d.eng]], d.sidx // nsr[d.eng] + 1)
                    r = ins.fn(engine)
                    if ins.dma:
                        r.then_inc(dsem[ins.dsem], 16)
                    elif ins.signal:
                        r.then_inc(rings[ename][ins.sidx % nsr[ename]], 1)
                if ename == "sp":
                    for d in final_wait_dmas:
                        engine.wait_ge(dsem[d.dsem], 16 * d.dval)
            return f

        with nc.Block() as block:
            block.sync(body("sp"))
            block.tensor(body("pe"))
            block.vector(body("dve"))
            block.scalar(body("act"))
            block.gpsimd(body("pool"))

from concourse.bass_utils import run_bass_kernel_spmd
import math

D = 1024
T = 2048
LC = 256
TT = T + LC
NCH = TT // 128
DEPTH = 2
DFF = 2816
EPS = 1e-6
NEG = -1.0e30
SLOT_EL = 8192

TBLOCKS = [(0, 256, True), (256, 768, False), (768, 1280, False), (1280, 1792, False), (1792, 2304, False)]


def _piece_list():
    pcs = [("A1", 8 * 768), ("A2", 8 * 640), ("SM1", 8 * 16)]
    for h in range(4):
        pcs.append((f"MH{h}", 8 * 512))
    pcs.append(("SM2", 8 * 16))
    for g in range(2):
        pcs.append((f"SG{g}", 8 * 768))
    for i in range(8):
        pcs.append((f"MG{i}", 8 * 384 + 3 * 4 * 128))
    pcs.append(("WO", 8 * 1024))
    for q in range(11):
        pcs.append((f"FF{q}", 8 * 512 + 2 * 1024))
    return pcs


def _k8(w, cols):
    sub = w[:, cols]
    K, n = sub.shape
    return np.ascontiguousarray(sub.reshape(K // 128, 128, n).transpose(1, 0, 2)).reshape(128, -1)


def _host_layer_weights(l, w_in, w_att_out, w_ml_out, w_ssd_out, w_o, w_up, w_down):
    wi = w_in[l]
    ar = np.arange
    qcols = np.concatenate([np.concatenate([c * 64 + ar(64), (4 + c) * 64 + ar(64)]) for c in range(4)])
    perm64 = np.concatenate([16 + ar(16), ar(16), 48 + ar(16), 32 + ar(16)])
    qperm = (qcols // 64) * 64 + perm64[qcols % 64]
    kcols = 512 + ar(128)
    kperm = 512 + (ar(128) // 64) * 64 + perm64[ar(128) % 64]
    vcols = 640 + ar(128)
    out = []
    out.append(_k8(wi, np.concatenate([qcols, kcols, vcols])))
    out.append(_k8(wi, np.concatenate([qperm, kperm])))
    g0 = 2816
    out.append(_k8(wi, np.concatenate([g0 + ar(4), g0 + 8 + ar(4), g0 + 4 + ar(4), g0 + 12 + ar(4)])))
    for h in range(4):
        out.append(_k8(wi, np.concatenate([768 + h * 128 + ar(128), 1280 + h * 128 + ar(128),
                                           1792 + h * 128 + ar(128), 2304 + h * 128 + ar(128)])))
    z0, x0, d0, m0 = 2832, 3344, 4368, 4384
    out.append(_k8(wi, d0 + ar(16)))
    for g in range(2):
        out.append(_k8(wi, np.concatenate([x0 + g * 256 + ar(256), x0 + 512 + g * 128 + ar(128),
                                           x0 + 768 + g * 128 + ar(128), z0 + g * 256 + ar(256)])))
    arow = np.concatenate([np.concatenate([c * 64 + ar(64), (4 + c) * 64 + ar(64)]) for c in range(4)])
    for i in range(8):
        cc = i * 128 + ar(128)
        a = _k8(wi, np.concatenate([m0 + cc, m0 + 1024 + cc, m0 + 2048 + cc]))
        b0 = _k8(w_att_out[l][arow], cc)
        b1 = _k8(w_ml_out[l], cc)
        b2 = _k8(w_ssd_out[l], cc)
        out.append(np.concatenate([a, b0, b1, b2], axis=1))
    out.append(_k8(w_o[l], ar(1024)))
    for q in range(11):
        j0 = 2 * q * 128
        a = _k8(w_up[l], np.concatenate([j0 + ar(256), DFF + j0 + ar(256)]))
        b = _k8(w_down[l][j0:j0 + 256], ar(1024))
        out.append(np.concatenate([a, b], axis=1))
    cat = np.concatenate(out, axis=1)
    return np.ascontiguousarray(cat, dtype=np.float32)


def _host_consts():
    ar = np.arange
    t = ar(T)
    inv = (10000.0 ** (-ar(16, dtype=np.float32) / 16)).astype(np.float32)
    row = (t // 64).astype(np.float32)
    col = (t % 64).astype(np.float32)
    ang_r = row[None, :] * inv[:, None]
    ang_c = col[None, :] * inv[:, None]
    cos64 = np.concatenate([np.cos(ang_r), np.cos(ang_r), np.cos(ang_c), np.cos(ang_c)], 0)
    sin64 = np.concatenate([-np.sin(ang_r), np.sin(ang_r), -np.sin(ang_c), np.sin(ang_c)], 0)
    cosT = np.concatenate([cos64, cos64], 0).astype(np.float32)
    sinT = np.concatenate([sin64, sin64], 0).astype(np.float32)
    q = ar(128)[:, None]
    k = ar(128)[None, :]
    maskA = np.zeros((128, 384), np.float32)
    maskA[:, 0:128] = np.where(k >= q, 0.0, -1.0e5)
    maskA[:, 256:384] = np.where(k <= q, 0.0, -1.0e5)
    s = ar(128)[:, None]
    tt = ar(128)[None, :]
    maskF = np.where(s <= tt, 0.0, NEG).astype(np.float32)
    maskB = np.where(s >= tt, 0.0, NEG).astype(np.float32)
    sel = np.zeros((8, 8, 128), np.float32)
    for r in range(8):
        sel[r, r, :] = 1.0
    ident = np.eye(128, dtype=np.float32)
    return dict(cosT=cosT, sinT=sinT, maskA=maskA, maskF=maskF, maskB=maskB, sel=sel.reshape(8, 1024), ident=ident)


class Arena:
    def __init__(self, nc, nbytes):
        self.t = nc.alloc_sbuf_tensor("arena", [128, nbytes // 2], BF16)
        self.top = 0
        self.cap = nbytes
        self.peak = 0

    def alloc(self, free, dtype, parts=128):
        if isinstance(free, int):
            free = (free,)
        n = 1
        for f in free:
            n *= f
        es = 4 if dtype == F32 else 2
        off = (self.top + 63) // 64 * 64
        self.top = off + n * es
        self.peak = max(self.peak, self.top)
        assert self.top <= self.cap, f"arena overflow {self.top} > {self.cap}"
        v = self.t[0:parts, off // 2:(off + n * es) // 2]
        if dtype == F32:
            v = v.bitcast(F32)
        if len(free) == 2:
            v = v.rearrange("p (a b) -> p a b", a=free[0])
        elif len(free) == 3:
            v = v.rearrange("p (a b c) -> p a b c", a=free[0], b=free[1])
        return v

    def mark(self):
        return self.top

    def release(self, m):
        self.top = m


def build(NB=2, L=DEPTH, dbg=False, stop_after=None):
    nc = bass.Bass("TRN2", target_bir_lowering=False)
    P = Prog(nc)

    def din(name, shape, dt=F32):
        P.skip_dram.add(name)
        return nc.dram_tensor(name, list(shape), dt, kind="ExternalInput").ap()

    def dscr(name, shape, dt):
        return nc.dram_tensor(name, list(shape), dt, kind="ExternalOutput" if dbg else "Internal").ap()

    pcs = _piece_list()
    poff = {}
    o = 0
    for nm, ne in pcs:
        poff[nm] = (o, ne)
        o += ne
    WTOT = o

    xT_d = din("xT", [NB, D, T])
    ctxT_d = din("ctxT", [NB, D, LC])
    cT_d = din("cT", [128, 24])
    wmod_d = din("wmod", [L, 128, 8 * 6144])
    bmod_d = din("bmod", [L, 128, 48])
    gn_d = din("gn", [128, (2 * L + 1) * 8])
    wl_d = [din(f"wl{l}", [128, WTOT]) for l in range(L)]
    cos_d = din("cosT", [128, T]); sin_d = din("sinT", [128, T])
    maskA_d = din("maskA", [128, 384]); maskF_d = din("maskF", [128, 128]); maskB_d = din("maskB", [128, 128])
    sel_d = din("sel", [8, 1024]); ident_d = din("ident", [128, 128])
    sink_d = din("sinkb", [128, L * 8])
    mlb_d = din("mlb", [8, L * 2])
    mlg_d = din("mlgain", [128, L * 512])
    cw_d = din("convw", [128, L * 8 * 6])
    sdt_d = din("ssddt", [8, L * 4])
    sdk_d = din("ssdskip", [128, L * 512])
    sng_d = din("ssdng", [128, L * 512])
    outT_d = nc.dram_tensor("outT", [NB, D, T], F32, kind="ExternalOutput").ap()

    brs_d = [dscr(f"dr_br{i}", [512, TT], BF16) for i in range(3)]
    yt_d = dscr("dr_yt", [D, TT], BF16)
    xs_d = dscr("dr_xs", [D, TT], F32)
    hdbg_d = dscr("dr_h", [D, TT], BF16) if dbg else None

    AR = Arena(nc, 204 * 1024)
    banks = [nc.alloc_psum_tensor(f"ps{i}", [128, 512], F32) for i in range(8)]
    st = {"rr": 0, "acc": 0, "kq": 0, "rrn": 3}

    def rr():
        b = banks[st["rr"] % st["rrn"]]
        st["rr"] += 1
        return b

    def kqb():
        b = banks[2 + st["kq"] % 4]
        st["kq"] += 1
        return b

    def accb():
        b = banks[6 + st["acc"] % 2]
        st["acc"] += 1
        return b

    IDF = AR.alloc(128, F32); P.dma(IDF, ident_d)
    IDB = AR.alloc(128, BF16); P.dma(IDB, ident_d, q="pool")
    ONESB = AR.alloc(128, BF16); P.memset(ONESB, 1.0)
    COS = AR.alloc(T, BF16); P.dma(COS.rearrange("p (a b) -> p a b", b=512), cos_d.rearrange("p (a b) -> p a b", b=512), q="pool")
    SIN = AR.alloc(T, BF16); P.dma(SIN.rearrange("p (a b) -> p a b", b=512), sin_d.rearrange("p (a b) -> p a b", b=512), q="pool")
    MASKA = AR.alloc(384, F32); P.dma(MASKA, maskA_d)
    MASKF = AR.alloc(128, F32); P.dma(MASKF, maskF_d)
    MASKB = AR.alloc(128, F32); P.dma(MASKB, maskB_d)
    SEL = AR.alloc((8, 128), F32, parts=8); P.dma(SEL, sel_d.rearrange("p (a b) -> p a b", a=8))
    SINKB = AR.alloc(L * 8, F32); P.dma(SINKB, sink_d)
    MLB = AR.alloc(L * 2, F32, parts=8); P.dma(MLB, mlb_d)
    NMLB = AR.alloc(L * 2, F32, parts=8); P.ts(NMLB, MLB, -1.0, None, ALU.mult)
    MLG = AR.alloc(L * 512, F32); P.dma(MLG, mlg_d)
    CW = AR.alloc(L * 48, F32); P.dma(CW, cw_d)
    SDT = AR.alloc(L * 4, F32, parts=8); P.dma(SDT, sdt_d)
    SA = AR.alloc(L * 4, F32, parts=8)
    P.act(SA, SDT, AF.Exp)
    P.ts(SA, SA, -1.0, None, ALU.mult)
    SDK = AR.alloc(L * 512, BF16); P.dma(SDK, sdk_d, q="pool")
    SNG = AR.alloc(L * 512, F32); P.dma(SNG, sng_d)
    GN = AR.alloc((2 * L + 1) * 8, F32); P.dma(GN, gn_d)
    EPST = AR.alloc(1, F32); P.memset(EPST, EPS)
    ONEF = AR.alloc(TT, F32, parts=8); P.memset(ONEF, 1.0)
    MOD = [AR.alloc((48, 3), F32) for _ in range(L)]
    GS = [AR.alloc((2, 8, 3), F32) for _ in range(L)]
    H = AR.alloc((8, TT), BF16)
    SLOTS = [AR.alloc(SLOT_EL, BF16) for _ in range(2)]

    wq = []
    for b in range(NB):
        for l in range(L):
            for nm, ne in pcs:
                wq.append((l, nm))
    wst = {"issued": 0, "cur": -1}

    def issue_next():
        i = wst["issued"]
        if i >= len(wq):
            return
        l, nm = wq[i]
        o, ne = poff[nm]
        CH = 512 if ne % 512 == 0 else 128
        P.dma(SLOTS[i % 2][:, 0:ne].rearrange("p (a b) -> p a b", b=CH),
              wl_d[l][:, o:o + ne].rearrange("p (a b) -> p a b", b=CH), q="pool")
        wst["issued"] += 1

    def next_piece(l, nm):
        wst["cur"] += 1
        i = wst["cur"]
        assert wq[i] == (l, nm), (wq[i], l, nm)
        while wst["issued"] <= i:
            issue_next()
        return SLOTS[i % 2]

    def prefetch():
        if wst["issued"] <= wst["cur"] + 1:
            issue_next()

    def wview(slot, off, kc, n):
        return slot[:, off:off + kc * n].rearrange("p (a b) -> p a b", a=kc)

    CT = AR.alloc(24, F32); P.dma(CT, cT_d)
    SC = AR.alloc(24, F32); P.act(SC, CT, AF.Silu)
    BM = AR.alloc(L * 48, F32)
    for l in range(L):
        P.dma(BM[:, l * 48:(l + 1) * 48], bmod_d[l])
    for l in range(L):
        for pc in range(12):
            WM = SLOTS[pc % 2].bitcast(F32).rearrange("p (a b) -> p a b", a=8)
            P.dma(WM, wmod_d[l].rearrange("p (a b) -> p a b", a=8)[:, :, pc * 512:(pc + 1) * 512])
            for cc in range(4):
                ps = rr()
                for kc in range(8):
                    P.mm(ps[:, 0:3], WM[:, kc, cc * 128:(cc + 1) * 128], SC[:, kc * 3:(kc + 1) * 3],
                         start=(kc == 0), stop=(kc == 7))
                col = pc * 4 + cc
                P.ts(MOD[l][:, col, :], ps[:, 0:3], BM[:, l * 48 + col:l * 48 + col + 1], None, ALU.add)
        for n in range(2):
            for kc in range(8):
                P.ts(GS[l][:, n, kc, :], MOD[l][:, (1 + 3 * n) * 8 + kc, :], 1.0,
                     GN[:, (2 * l + n) * 8 + kc:(2 * l + n) * 8 + kc + 1], ALU.add, ALU.mult)

    def modv(l, which, kc, who):
        return MOD[l][:, which * 8 + kc, who:who + 1]

    def norm_to(Xv, l, n, b, dst, gvec=None, blocks=TBLOCKS, dst_f32=False, toff=0):
        for (t0, t1, isctx) in blocks:
            who = 2 if isctx else b
            nt = t1 - t0
            m = AR.mark()
            SQ = AR.alloc((8, nt), BF16)
            for kc in range(8):
                P.act(SQ[:, kc, :], Xv[:, kc, t0:t1], AF.Square)
            ps = rr()
            for kc in range(8):
                P.mm(ps[:, 0:nt], ONESB, SQ[:, kc, :], start=(kc == 0), stop=(kc == 7))
            RS = AR.alloc(nt, F32)
            P.act(RS, ps[:, 0:nt], AF.Sqrt, bias=EPST, scale=1.0 / D)
            P.recip(RS, RS)
            TMP = AR.alloc((2, nt), F32)
            for kc in range(8):
                tv = TMP[:, kc % 2, :]
                P.tt(tv, Xv[:, kc, t0:t1], RS, ALU.mult)
                if gvec is None:
                    P.act(dst[:, kc, t0 - toff:t1 - toff], tv, AF.Identity, scale=GS[l][:, n, kc, who:who + 1],
                          bias=modv(l, 3 * n, kc, who))
                else:
                    P.act(dst[:, kc, t0 - toff:t1 - toff], tv, AF.Identity, scale=gvec[:, kc:kc + 1])
            AR.release(m)

    def proj_fm(W, c0, ncols, t0, t1):
        ps = rr()
        for kc in range(8):
            P.mm(ps[0:ncols, 0:t1 - t0], W[:, kc, c0:c0 + ncols], H[:, kc, t0:t1], start=(kc == 0), stop=(kc == 7))
        return ps[0:ncols, 0:t1 - t0]

    def proj_tm(W, c0, ncols, c):
        ps = rr()
        for kc in range(8):
            P.mm(ps[:, 0:ncols], H[:, kc, c * 128:(c + 1) * 128], W[:, kc, c0:c0 + ncols],
                 start=(kc == 0), stop=(kc == 7))
        return ps[:, 0:ncols]

    def rows_scan(dst, src, op0, op1, d0, reverse):
        if not reverse:
            P.scan(dst[:, 0:LC], d0[:, 0:LC], src[:, 0:LC], 0.0, op0, op1)
            P.scan(dst[:, LC:TT], d0[:, LC:TT], src[:, LC:TT], dst[:, LC - 1:LC], op0, op1)
        else:
            P.scan(dst[:, 0:LC][:, ::-1], d0[:, 0:LC][:, ::-1], src[:, 0:LC][:, ::-1], 0.0, op0, op1)
            P.scan(dst[:, LC:TT][:, ::-1], d0[:, LC:TT][:, ::-1], src[:, LC:TT][:, ::-1], dst[:, 0:1], op0, op1)

    def rows_to_cols(COLS, ci, R):
        for c0 in range(0, NCH, 6):
            ps = rr()
            for c in range(c0, c0 + 6):
                P.transpose(ps[:, (c - c0) * 8:(c - c0) * 8 + 8], R[:, c * 128:(c + 1) * 128], IDF[0:8, 0:8])
            P.copy(COLS[:, c0:c0 + 6, ci, :], ps[:, 0:48].rearrange("p (a b) -> p a b", a=6))

    def contribs(t):
        out = []
        for s in range(0, t):
            out.append((s, 0, False))
        out.append((t, 0, True))
        if t >= 2:
            for s in (0, 1):
                out.append((s, 1, False))
            for s in range(t + 1, NCH):
                out.append((s, 1, False))
        else:
            for s in range(t + 1, 2):
                out.append((s, 1, False))
        out.append((t, 1, True))
        return out

    MASKD = [MASKF, MASKB]

    def build_bcast(rowtile, r, dst):
        for bi, (t0, t1, isctx) in enumerate(TBLOCKS):
            ps = rr()
            P.mm(ps[:, 0:t1 - t0], SEL[:, r, :], rowtile[:, t0:t1])
            P.copy(dst[:, t0:t1], ps[:, 0:t1 - t0], eng=("act" if bi % 2 == 0 else "dve"))

    def run_pipelined(items, stage1, stage2, fin):
        LA = 3
        cidx = [k for k, it in enumerate(items) if it[0] == "c"]
        nxt = {cidx[j]: (cidx[j + LA] if j + LA < len(cidx) else None) for j in range(len(cidx))}
        hnd = {}
        for j in range(min(LA, len(cidx))):
            hnd[cidx[j]] = stage1(items[cidx[j]])
        for k, it in enumerate(items):
            if it[0] == "c":
                n = nxt[k]
                if n is not None:
                    hnd[n] = stage1(items[n])
                stage2(it, hnd.pop(k))
            else:
                fin(it)

    def decay_T(rowtile, r, t, s, masked, d, biascol):
        ps = rr()
        if masked:
            P.mm(ps[:, 0:128], IDF, MASKD[d], start=True, stop=False)
        P.mm(ps[:, 0:128], SEL[:, r, :], rowtile[:, t * 128:(t + 1) * 128], start=(not masked), stop=True)
        return ps

    for b in range(NB):
        for l in range(L):
            last = (l == L - 1)
            need_ctx = not last
            blocks = TBLOCKS if need_ctx else TBLOCKS[1:]
            m_layer = AR.mark()
            if l == 0:
                X = AR.alloc((8, TT), F32)
                P.dma(X[:, :, 0:LC], ctxT_d[b].rearrange("(a p) t -> p a t", p=128))
                P.dma(X[:, :, LC:TT], xT_d[b].rearrange("(a p) t -> p a t", p=128))
            norm_to(X, l, 0, b, H)
            if l > 0:
                P.dma(xs_d.rearrange("(a p) t -> p a t", p=128), X)
                AR.release(m_x)
                m_layer = m_x
            if dbg and l == 0 and b == 0:
                P.dma(hdbg_d.rearrange("(a p) t -> p a t", p=128), H)
            AR.release(m_layer)
            if stop_after == "norm1":
                break

            m0 = AR.mark()
            W1 = next_piece(l, "A1"); prefetch()
            W1v = wview(W1, 0, 8, 768)
            QT = AR.alloc((4, TT), BF16)
            KT = AR.alloc(TT, BF16)
            VP = [AR.alloc((NCH, 128), BF16) for _ in range(2)]
            P.memset(VP[0], 0.0); P.memset(VP[1], 0.0)
            W2 = next_piece(l, "A2")
            W2v = wview(W2, 0, 8, 640)
            if stop_after == "attv":
                for c in range(NCH):
                    ps = proj_tm(W1v, 640, 128, c)
                    P.copy(VP[0][:, c, 0:64], ps[:, 0:64], eng="act")
                    P.copy(VP[1][:, c, 64:128], ps[:, 64:128], eng="act")
                P.dma(brs_d[0][0:128, :], VP[0].rearrange("p a b -> p (a b)"))
                break
            if stop_after == "attr":
                ps = proj_fm(W1v, 0, 128, 256, 768)
                ps2 = proj_fm(W2v, 0, 128, 256, 768)
                T1 = AR.alloc(512, F32); T2 = AR.alloc(512, F32)
                P.tt(T1, ps, COS[:, 0:512], ALU.mult)
                P.tt(T2, ps2, SIN[:, 0:512], ALU.mult)
                P.tt(QT[:, 0, 256:768], T1, T2, ALU.add)
                ps = proj_fm(W1v, 0, 128, 0, 256)
                P.copy(QT[:, 0, 0:256], ps, eng="act")
                P.dma(brs_d[0].rearrange("(a p) t -> p a t", p=128), QT)
                break
            if stop_after == "attw":
                ps = proj_fm(W1v, 0, 128, 256, 768)
                P.copy(QT[:, 0, 256:768], ps, eng="act")
                ps = proj_tm(W1v, 640, 128, 3)
                P.copy(VP[0][:, 3, 0:64], ps[:, 0:64], eng="act")
                P.dma(brs_d[0].rearrange("(a p) t -> p a t", p=128), QT)
                break
            for (t0, t1, isctx) in TBLOCKS:
                nt = t1 - t0
                for c in range(5):
                    if isctx and c < 4 and not need_ctx:
                        continue
                    dst = QT[:, c, t0:t1] if c < 4 else KT[:, t0:t1]
                    ps = proj_fm(W1v, c * 128, 128, t0, t1)
                    if isctx:
                        P.copy(dst, ps, eng="act")
                    else:
                        ps2 = proj_fm(W2v, c * 128, 128, t0, t1)
                        m = AR.mark()
                        T1 = AR.alloc(nt, F32); T2 = AR.alloc(nt, F32)
                        P.tt(T1, ps, COS[:, t0 - LC:t1 - LC], ALU.mult)
                        P.tt(T2, ps2, SIN[:, t0 - LC:t1 - LC], ALU.mult)
                        P.tt(dst, T1, T2, ALU.add)
                        AR.release(m)
            for c in range(NCH):
                ps = proj_tm(W1v, 640, 128, c)
                P.copy(VP[0][:, c, 0:64], ps[:, 0:64], eng="act")
                P.copy(VP[1][:, c, 64:128], ps[:, 64:128], eng="act")
            prefetch()
            ATs = AR.alloc((4, TT), BF16)
            if not need_ctx:
                P.memset(ATs[:, :, 0:LC], 0.0)
            if stop_after == "attproj":
                P.dma(brs_d[0].rearrange("(a p) t -> p a t", p=128), QT)
                break
            qblocks = list(range(0 if need_ctx else 2, NCH))
            if stop_after == "att1":
                qblocks = qblocks[:1]
            st["rrn"] = 6
            NSET = 3
            bufs = [dict(SM=AR.alloc(640, F32), ST4=AR.alloc(8, F32), PB=AR.alloc(640, BF16),
                         PN=AR.alloc(640, BF16), PTs=AR.alloc(640, BF16)) for _ in range(NSET)]
            its = []
            for qb in qblocks:
                lat = qb >= 2
                loc = [k for k in (qb - 1, qb, qb + 1) if 2 <= k < NCH] if lat else []
                for c in range(4):
                    for half in range(2):
                        its.append(dict(qb=qb, lat=lat, loc=loc, c=c, half=half, nl=len(loc) * 128,
                                        keych=[0, 1] + loc, mo=(0 if (not lat or qb - 1 >= 2) else 128)))
            ast = {"acc": None}

            def at_A(it):
                psl = slice(64 * it["half"], 64 * it["half"] + 64)
                lhsT = QT[psl, it["c"], it["qb"] * 128:(it["qb"] + 1) * 128]
                psB = rr()
                P.mm(psB[:, 0:256], lhsT, KT[psl, 0:256])
                psA = None
                if it["lat"]:
                    psA = rr()
                    P.mm(psA[:, 0:it["nl"]], lhsT, KT[psl, it["loc"][0] * 128:(it["loc"][-1] + 1) * 128])
                return psB, psA

            def at_BC(it, hnd, bf):
                psB, psA = hnd
                SM, ST4, PB, PN, PTs = bf["SM"], bf["ST4"], bf["PB"], bf["PN"], bf["PTs"]
                nl, mo, half, c, qb = it["nl"], it["mo"], it["half"], it["c"], it["qb"]
                head = c + 4 * half
                if half == 0:
                    ast["acc"] = accb()
                acc = ast["acc"]
                P.copy(SM[:, 0:256], psB[:, 0:256], eng="act")
                if it["lat"]:
                    P.tt(SM[:, 256:256 + nl], psA[:, 0:nl], MASKA[:, mo:mo + nl], ALU.add)
                ntot = 256 + nl
                sk = SINKB[:, l * 8 + head:l * 8 + head + 1]
                P.reduce(ST4[:, 0:1], SM[:, 0:ntot], ALU.max)
                P.ts(ST4[:, 1:2], ST4[:, 0:1], 0.125, sk, ALU.mult, ALU.max)
                P.ts(ST4[:, 2:3], ST4[:, 1:2], -1.0, None, ALU.mult)
                P.act(PB[:, 0:ntot], SM[:, 0:ntot], AF.Exp, bias=ST4[:, 2:3], scale=0.125, accum_out=ST4[:, 3:4])
                P.act(ST4[:, 4:5], ST4[:, 2:3], AF.Exp, bias=sk)
                P.tt(ST4[:, 5:6], ST4[:, 3:4], ST4[:, 4:5], ALU.add)
                P.recip(ST4[:, 6:7], ST4[:, 5:6])
                P.ts(PN[:, 0:ntot], PB[:, 0:ntot], ST4[:, 6:7], None, ALU.mult)
                psT = rr().bitcast(BF16)
                keych = it["keych"]
                nk = len(keych)
                for j in range(nk):
                    P.transpose(psT[:, j * 128:(j + 1) * 128], PN[:, j * 128:(j + 1) * 128], IDB)
                P.copy(PTs[:, 0:nk * 128], psT[:, 0:nk * 128], eng="act")
                for j, kc_ in enumerate(keych):
                    P.mm(acc[:, 0:128], VP[half][:, kc_, :], PTs[:, j * 128:(j + 1) * 128],
                         start=(half == 0 and j == 0), stop=(half == 1 and j == nk - 1))
                if half == 1:
                    P.copy(ATs[:, c, qb * 128:(qb + 1) * 128], acc[:, 0:128])

            hnd = {}
            if its:
                hnd[0] = at_A(its[0])
            for j, it in enumerate(its):
                if j + 1 < len(its):
                    hnd[j + 1] = at_A(its[j + 1])
                at_BC(it, hnd.pop(j), bufs[j % NSET])
            st["rrn"] = 3
            P.dma(brs_d[0].rearrange("(a p) t -> p a t", p=128), ATs)
            AR.release(m0)
            if stop_after in ("att", "att1"):
                break

            st["rrn"] = 2
            m0 = AR.mark()
            WS = next_piece(l, "SM1"); prefetch()
            WSv = wview(WS, 0, 8, 16)
            NMR = [AR.alloc(TT, F32, parts=8) for _ in range(2)]
            COLS = AR.alloc((NCH, 4, 8), F32)
            MTs = AR.alloc((4, TT), BF16)
            m_rows = AR.mark()
            LI = AR.alloc(TT, F32, parts=8); LF = AR.alloc(TT, F32, parts=8)
            for (t0, t1, isctx) in TBLOCKS:
                ps = proj_fm(WSv, 0, 8, t0, t1)
                P.act(LI[:, t0:t1], ps, AF.Identity, bias=MLB[:, 2 * l:2 * l + 1])
                ps = proj_fm(WSv, 8, 8, t0, t1)
                P.act(LF[:, t0:t1], ps, AF.Exp, bias=NMLB[:, 2 * l + 1:2 * l + 2], scale=-1.0)
            P.act(LF, LF, AF.Ln, bias=1.0)
            P.ts(LF, LF, -1.0, None, ALU.mult)
            m1 = AR.mark()
            for d in range(2):
                T1 = AR.alloc(TT, F32, parts=8); T4 = AR.alloc(TT, F32, parts=8)
                T2 = NMR[d]
                rows_scan(T1, LF, ALU.mult, ALU.add, ONEF, d == 1)
                pass
                P.tt(T1, LI, T1, ALU.subtract)
                rows_scan(T2, T1, ALU.max, ALU.max, T1, d == 1)
                P.tt(T4, LI, T1, ALU.subtract)
                P.tt(T4, T4, T2, ALU.add)
                P.act(T4, T4, AF.Exp, scale=-1.0)
                rows_to_cols(COLS, d * 2 + 0, T1)
                rows_to_cols(COLS, d * 2 + 1, T4)
                P.ts(T2, T2, -1.0, None, ALU.mult)
                AR.release(m1)
            AR.release(m_rows)
            if stop_after == "mlrows":
                P.dma(brs_d[1][0:128, 0:NCH * 32], COLS.rearrange("p a b c -> p (a b c)").bitcast(BF16)[:, 0:NCH * 32])
                P.dma(brs_d[1][128:136, :], NMR[0].bitcast(BF16)[:, 0:TT])
                break
            if not need_ctx:
                P.memset(MTs[:, :, 0:LC], 0.0)
            for h in range(4):
                WH = next_piece(l, f"MH{h}"); prefetch()
                WHv = wview(WH, 0, 8, 512)
                m2 = AR.mark()
                QTh = AR.alloc(TT, BF16); KTh = AR.alloc(TT, BF16)
                VA = AR.alloc((NCH, 132), BF16); OS = AR.alloc((NCH, 128), BF16)
                P.memset(VA[:, :, 128:129], 1.0)
                for (t0, t1, isctx) in TBLOCKS:
                    ps = proj_fm(WHv, 0, 128, t0, t1)
                    P.copy(QTh[:, t0:t1], ps, eng="act")
                    ps = proj_fm(WHv, 128, 128, t0, t1)
                    P.ts(KTh[:, t0:t1], ps, 128.0 ** -0.5, None, ALU.mult)
                for c in range(NCH):
                    ps = proj_tm(WHv, 256, 256, c)
                    P.copy(VA[:, c, 0:128], ps[:, 0:128])
                    P.act(OS[:, c, :], ps[:, 128:256], AF.Sigmoid)
                ring = [(AR.alloc(128, F32), AR.alloc(128, BF16)) for _ in range(6)]
                mring = [AR.alloc(128, F32) for _ in range(2)]
                BC = [AR.alloc(TT, F32) for _ in range(2)]
                for d in range(2):
                    build_bcast(NMR[d], d * 4 + h, BC[d])
                mst = {"rc": 0, "accs": None, "mc": 0}
                items = []
                for t in (range(NCH) if need_ctx else range(2, NCH)):
                    cl = contribs(t)
                    cnt = [sum(1 for x in cl if x[1] == d) for d in range(2)]
                    seen = [0, 0]
                    for ci_, (s_, d, masked) in enumerate(cl):
                        items.append(("c", t, s_, d, masked, seen[d] == 0, seen[d] == cnt[d] - 1, ci_ == 0))
                        seen[d] += 1
                    items.append(("fin", t))

                def ml_s1(it):
                    _, t, s_, d, masked, first, lastc, newt = it
                    pk = kqb()
                    P.mm(pk[:, 0:128], KTh[:, s_ * 128:(s_ + 1) * 128], QTh[:, t * 128:(t + 1) * 128])
                    return pk, None

                def ml_s2(it, hh):
                    _, t, s_, d, masked, first, lastc, newt = it
                    pk, psD = hh
                    if newt:
                        bk = accb()
                        mst["accs"] = [bk[:, 0:256], bk[:, 256:512]]
                    r = d * 4 + h
                    DT_, ST_ = ring[mst["rc"] % 6]
                    mst["rc"] += 1
                    src = BC[d][:, t * 128:(t + 1) * 128]
                    if masked:
                        tm = mring[mst["mc"] % 2]
                        mst["mc"] += 1
                        P.tt(tm, src, MASKD[d], ALU.add)
                        src = tm
                    P.act(DT_, src, AF.Exp, bias=COLS[:, s_, d * 2, r:r + 1])
                    P.tt(ST_, pk[:, 0:128], DT_, ALU.mult)
                    P.mm(mst["accs"][d][:, 0:129], ST_, VA[:, s_, 0:129], start=first, stop=lastc)

                def ml_fin(it):
                    t = it[1]
                    accs = mst["accs"]
                    m3 = AR.mark()
                    HD = []
                    for d in range(2):
                        r = d * 4 + h
                        A_ = AR.alloc(132, F32)
                        P.copy(A_[:, 0:129], accs[d][:, 0:129], eng="act")
                        S4 = AR.alloc(4, F32)
                        P.stt(S4[:, 0:1], A_[:, 128:129], -1.0, A_[:, 128:129], ALU.mult, ALU.max)
                        P.tt(S4[:, 1:2], S4[:, 0:1], COLS[:, t, d * 2 + 1, r:r + 1], ALU.max)
                        P.recip(S4[:, 2:3], S4[:, 1:2])
                        Hd = AR.alloc(128, F32)
                        P.ts(Hd, A_[:, 0:128], S4[:, 2:3], None, ALU.mult)
                        HD.append(Hd)
                    HS = AR.alloc(128, F32)
                    P.tt(HS, HD[0], HD[1], ALU.add)
                    S4 = AR.alloc(4, F32)
                    JK = AR.alloc(128, F32)
                    P.act(JK, HS, AF.Square, accum_out=S4[:, 0:1])
                    P.act(S4[:, 1:2], S4[:, 0:1], AF.Ln, bias=EPST, scale=1.0 / 128)
                    P.act(S4[:, 2:3], S4[:, 1:2], AF.Exp, scale=-0.5)
                    O1 = AR.alloc(128, F32)
                    P.stt(O1, HS, S4[:, 2:3], MLG[:, l * 512 + h * 128:l * 512 + (h + 1) * 128], ALU.mult, ALU.mult)
                    O2 = AR.alloc(128, BF16)
                    P.tt(O2, O1, OS[:, t, :], ALU.mult)
                    pT = rr().bitcast(BF16)
                    P.transpose(pT[:, 0:128], O2, IDB)
                    P.copy(MTs[:, h, t * 128:(t + 1) * 128], pT[:, 0:128], eng="act")
                    AR.release(m3)

                run_pipelined(items, ml_s1, ml_s2, ml_fin)
                AR.release(m2)
                if stop_after == "mlh1":
                    break
            P.dma(brs_d[1].rearrange("(a p) t -> p a t", p=128), MTs)
            AR.release(m0)
            if stop_after in ("ml", "mlh1"):
                break

            m0 = AR.mark()
            WS = next_piece(l, "SM2"); prefetch()
            WSv = wview(WS, 0, 8, 16)
            ACS = [AR.alloc(TT, F32, parts=8) for _ in range(2)]
            COLS = AR.alloc((NCH, 4, 8), F32)
            m1 = AR.mark()
            for d in range(2):
                DTr = AR.alloc(TT, F32, parts=8); T1 = AR.alloc(TT, F32, parts=8)
                for (t0, t1, isctx) in TBLOCKS:
                    ps = proj_fm(WSv, 8 * d, 8, t0, t1)
                    P.act(DTr[:, t0:t1], ps, AF.Exp, bias=SDT[:, 4 * l + d:4 * l + d + 1])
                P.act(DTr, DTr, AF.Ln, bias=1.0)
                P.ts(T1, DTr, SA[:, 4 * l + 2 + d:4 * l + 3 + d], None, ALU.mult)
                rows_scan(ACS[d], T1, ALU.mult, ALU.add, ONEF, d == 1)
                P.ts(T1, ACS[d], -1.0, None, ALU.mult)
                rows_to_cols(COLS, d * 2 + 0, T1)
                rows_to_cols(COLS, d * 2 + 1, DTr)
                AR.release(m1)
            YG = AR.alloc((NCH, 512), BF16)
            SSQ = AR.alloc((NCH, 8), F32)
            trange = list(range(NCH) if need_ctx else range(2, NCH))
            for g in range(2):
                WG = next_piece(l, f"SG{g}"); prefetch()
                WGv = wview(WG, 0, 8, 768)
                m2 = AR.mark()
                XTOK = AR.alloc((NCH, 256), BF16)
                BTg = AR.alloc(TT, BF16); CTg = AR.alloc(TT, BF16)
                ZS = AR.alloc((NCH, 256), BF16)
                m3 = AR.mark()
                PAD = AR.alloc(2312, F32); ACC = AR.alloc(2308, F32); XF = AR.alloc(TT, BF16)
                P.memset(PAD[:, 0:2], 0.0); P.memset(PAD[:, 258:262], 0.0); P.memset(PAD[:, 2310:2312], 0.0)
                for ch in range(4):
                    cidx = [g * 2, g * 2 + 1, 4 + g, 6 + g][ch]
                    cw = CW[:, (l * 8 + cidx) * 6:(l * 8 + cidx) * 6 + 6]
                    for (t0, t1, isctx) in TBLOCKS:
                        ps = proj_fm(WGv, ch * 128, 128, t0, t1)
                        o0 = 2 + t0 if isctx else 262 + (t0 - LC)
                        P.copy(PAD[:, o0:o0 + (t1 - t0)], ps, eng="act")
                    P.ts(ACC, PAD[:, 0:2308], cw[:, 0:1], None, ALU.mult)
                    for j in range(1, 5):
                        P.stt(ACC, PAD[:, j:j + 2308], cw[:, j:j + 1], ACC, ALU.mult, ALU.add)
                    dst = XF if ch < 2 else (BTg if ch == 2 else CTg)
                    P.act(dst[:, 0:LC], ACC[:, 0:LC], AF.Silu, bias=cw[:, 5:6])
                    P.act(dst[:, LC:TT], ACC[:, 260:2308], AF.Silu, bias=cw[:, 5:6])
                    if ch < 2:
                        for c0 in range(0, NCH, 4):
                            cs = list(range(c0, min(c0 + 4, NCH)))
                            pT = rr().bitcast(BF16)
                            for c in cs:
                                P.transpose(pT[:, (c - c0) * 128:(c - c0 + 1) * 128], XF[:, c * 128:(c + 1) * 128], IDB)
                            P.copy(XTOK[:, c0:c0 + len(cs), ch * 128:(ch + 1) * 128],
                                   pT[:, 0:len(cs) * 128].rearrange("p (a b) -> p a b", a=len(cs)))
                AR.release(m3)
                BTOK = None
                for c in range(NCH):
                    ps = proj_tm(WGv, 512, 256, c)
                    P.act(ZS[:, c, :], ps, AF.Silu)
                ring = [(AR.alloc(128, F32), AR.alloc(128, BF16)) for _ in range(6)]
                mring = [AR.alloc(128, F32) for _ in range(2)]
                BC = [AR.alloc(TT, F32) for _ in range(2)]
                sst = {"rc": 0, "acc": None, "mc": 0}
                for r4 in range(4):
                    hd = g * 4 + r4
                    for d in range(2):
                        build_bcast(ACS[d], hd, BC[d])
                    items = []
                    for t in trange:
                        cl = contribs(t)
                        for ci_, (s_, d, masked) in enumerate(cl):
                            items.append(("c", t, s_, d, masked, ci_ == 0, ci_ == len(cl) - 1))
                        items.append(("fin", t))

                    def sd_s1(it, hd=hd):
                        _, t, s_, d, masked, first, lastc = it
                        pk = kqb()
                        P.mm(pk[:, 0:128], BTg[:, s_ * 128:(s_ + 1) * 128], CTg[:, t * 128:(t + 1) * 128])
                        return pk, None

                    def sd_s2(it, hh, hd=hd, r4=r4):
                        _, t, s_, d, masked, first, lastc = it
                        pk, psD = hh
                        if first:
                            sst["acc"] = accb()
                        DT_, MT_ = ring[sst["rc"] % 6]
                        sst["rc"] += 1
                        src = BC[d][:, t * 128:(t + 1) * 128]
                        if masked:
                            tm = mring[sst["mc"] % 2]
                            sst["mc"] += 1
                            P.tt(tm, src, MASKD[d], ALU.add)
                            src = tm
                        P.act(DT_, src, AF.Exp, bias=COLS[:, s_, d * 2, hd:hd + 1])
                        P.stt(MT_, pk[:, 0:128], COLS[:, s_, d * 2 + 1, hd:hd + 1], DT_, ALU.mult, ALU.mult)
                        P.mm(sst["acc"][:, 0:64], MT_, XTOK[:, s_, r4 * 64:(r4 + 1) * 64], start=first, stop=lastc)

                    def sd_fin(it, hd=hd, r4=r4):
                        t = it[1]
                        acc = sst["acc"]
                        m4 = AR.mark()
                        Y1 = AR.alloc(64, F32)
                        P.tt(Y1, XTOK[:, t, r4 * 64:(r4 + 1) * 64],
                             SDK[:, l * 512 + hd * 64:l * 512 + (hd + 1) * 64], ALU.mult)
                        P.tt(Y1, Y1, acc[:, 0:64], ALU.add)
                        P.tt(YG[:, t, hd * 64:(hd + 1) * 64], Y1, ZS[:, t, r4 * 64:(r4 + 1) * 64], ALU.mult)
                        JK = AR.alloc(64, F32)
                        P.act(JK, YG[:, t, hd * 64:(hd + 1) * 64], AF.Square, accum_out=SSQ[:, t, hd:hd + 1])
                        AR.release(m4)

                    run_pipelined(items, sd_s1, sd_s2, sd_fin)
                AR.release(m2)
            STs = AR.alloc((4, TT), BF16)
            if not need_ctx:
                P.memset(STs[:, :, 0:LC], 0.0)
            for t in trange:
                m4 = AR.mark()
                S4 = AR.alloc(4, F32)
                P.reduce(S4[:, 0:1], SSQ[:, t, :], ALU.add)
                P.act(S4[:, 1:2], S4[:, 0:1], AF.Ln, bias=EPST, scale=1.0 / 512)
                P.act(S4[:, 2:3], S4[:, 1:2], AF.Exp, scale=-0.5)
                O2 = AR.alloc(512, BF16)
                P.stt(O2, YG[:, t, :], S4[:, 2:3], SNG[:, l * 512:(l + 1) * 512], ALU.mult, ALU.mult)
                pT = rr().bitcast(BF16)
                for ch in range(4):
                    P.transpose(pT[:, ch * 128:(ch + 1) * 128], O2[:, ch * 128:(ch + 1) * 128], IDB)
                P.copy(STs[:, :, t * 128:(t + 1) * 128], pT[:, 0:512].rearrange("p (a b) -> p a b", a=4), eng="act")
                AR.release(m4)
            P.dma(brs_d[2].rearrange("(a p) t -> p a t", p=128), STs)
            AR.release(m0)
            if stop_after == "ssd":
                break

            st["rrn"] = 3
            m0 = AR.mark()
            BR = [AR.alloc((4, TT), BF16) for _ in range(3)]
            for i in range(3):
                P.dma(BR[i], brs_d[i].rearrange("(a p) t -> p a t", p=128))
            for i in range(8):
                WMp = next_piece(l, f"MG{i}"); prefetch()
                Wg = wview(WMp, 0, 8, 384)
                Wo_ = WMp[:, 3072:3072 + 1536].rearrange("p (b a c) -> p b a c", b=3, a=4)
                m_i = AR.mark()
                YTi = AR.alloc(TT, BF16)
                for (t0, t1, isctx) in blocks:
                    nt = t1 - t0
                    m1 = AR.mark()
                    YA = AR.alloc(nt, F32); TM = AR.alloc(nt, F32)
                    for br in range(3):
                        psG = proj_fm(Wg, br * 128, 128, t0, t1)
                        SG = AR.alloc(nt, F32)
                        P.act(SG, psG, AF.Sigmoid)
                        psO = rr()
                        for kc in range(4):
                            P.mm(psO[:, 0:nt], Wo_[:, br, kc, :], BR[br][:, kc, t0:t1], start=(kc == 0), stop=(kc == 3))
                        if br == 0:
                            P.tt(YA, SG, psO[:, 0:nt], ALU.mult)
                        elif br == 1:
                            P.tt(TM, SG, psO[:, 0:nt], ALU.mult)
                            P.tt(YA, YA, TM, ALU.add, eng="pool")
                        else:
                            P.tt(TM, SG, psO[:, 0:nt], ALU.mult)
                            P.tt(YTi[:, t0:t1], YA, TM, ALU.add, eng="pool")
                    AR.release(m1)
                P.dma(yt_d[i * 128:(i + 1) * 128, blocks[0][0]:TT], YTi[:, blocks[0][0]:TT])
                AR.release(m_i)
            AR.release(m0)

            m_x = AR.mark()
            X = AR.alloc((8, TT), F32)
            if l == 0:
                P.dma(X[:, :, 0:LC], ctxT_d[b].rearrange("(a p) t -> p a t", p=128))
                P.dma(X[:, :, LC:TT], xT_d[b].rearrange("(a p) t -> p a t", p=128))
            else:
                P.dma(X, xs_d.rearrange("(a p) t -> p a t", p=128))
            WOp = next_piece(l, "WO"); prefetch()
            WOv = wview(WOp, 0, 8, 1024)
            for (t0, t1, isctx) in blocks:
                nt = t1 - t0
                who = 2 if isctx else b
                m1 = AR.mark()
                Yb = AR.alloc((8, nt), BF16)
                P.dma(Yb, yt_d.rearrange("(a p) t -> p a t", p=128)[:, :, t0:t1])
                for o in range(8):
                    ps = rr()
                    for kc in range(8):
                        P.mm(ps[:, 0:nt], WOv[:, kc, o * 128:(o + 1) * 128], Yb[:, kc, :], start=(kc == 0), stop=(kc == 7))
                    P.stt(X[:, o, t0:t1], ps[:, 0:nt], modv(l, 2, o, who), X[:, o, t0:t1], ALU.mult, ALU.add)
                AR.release(m1)
            if stop_after == "wo":
                break

            norm_to(X, l, 1, b, H, blocks=blocks)
            for q in range(11):
                WF = next_piece(l, f"FF{q}"); prefetch()
                Wu = wview(WF, 0, 8, 512)
                Wd = wview(WF, 4096, 2, 1024)
                for (t0, t1, isctx) in blocks:
                    nt = t1 - t0
                    who = 2 if isctx else b
                    m1 = AR.mark()
                    A_ = AR.alloc((2, nt), BF16)
                    for jj in range(2):
                        psg = proj_fm(Wu, jj * 128, 128, t0, t1)
                        SGt = AR.alloc(nt, F32)
                        P.act(SGt, psg, AF.Silu)
                        psu = proj_fm(Wu, 256 + jj * 128, 128, t0, t1)
                        P.tt(A_[:, jj, :], SGt, psu, ALU.mult)
                    for o in range(8):
                        ps = rr()
                        for jj in range(2):
                            P.mm(ps[:, 0:nt], Wd[:, jj, o * 128:(o + 1) * 128], A_[:, jj, :], start=(jj == 0), stop=(jj == 1))
                        P.stt(X[:, o, t0:t1], ps[:, 0:nt], modv(l, 5, o, who), X[:, o, t0:t1], ALU.mult, ALU.add)
                    AR.release(m1)
            if last:
                for (t0, t1, isctx) in [(LC + 256 * i, LC + 256 * (i + 1), False) for i in range(8)]:
                    m1 = AR.mark()
                    OUT = AR.alloc((8, t1 - t0), F32)
                    norm_to(X, l, 0, b, OUT, gvec=GN[:, 2 * L * 8:(2 * L + 1) * 8], blocks=[(t0, t1, isctx)], toff=t0)
                    P.dma(outT_d[b].rearrange("(a p) t -> p a t", p=128)[:, :, t0 - LC:t1 - LC], OUT)
                    AR.release(m1)
                AR.release(m_x)
        else:
            continue
        break

    outs = [d for d in P.dmas if False]
    final = [d for d in P.dmas]
    P.emit(final_wait_dmas=[d for lst in P.dma_pools.values() for d in lst[-16:]])
    AR.P = P
    return nc, AR


_CACHE = {}


def host_prep(inputs, NB=2, L=DEPTH):
    f32 = np.float32
    g = {k: np.asarray(v, dtype=f32) for k, v in inputs.items()}
    consts = _host_consts()
    wl = [_host_layer_weights(l, g["w_in"], g["w_att_out"], g["w_ml_out"], g["w_ssd_out"], g["w_o"], g["w_up"], g["w_down"])
          for l in range(L)]
    wmod = np.stack([_k8(g["w_mod"][l], np.arange(6144)) for l in range(L)])
    bmod = np.stack([g["b_mod"][l].reshape(48, 128).T for l in range(L)])
    gn = np.concatenate([np.concatenate([g["g_norm1"][l].reshape(8, 128).T, g["g_norm2"][l].reshape(8, 128).T], 1)
                         for l in range(L)] + [g["g_final"].reshape(8, 128).T], 1)
    sinkb = np.broadcast_to(g["att_sink"][:L].reshape(1, L * 8), (128, L * 8))
    mlb = np.concatenate([np.stack([g["ml_i_bias"][l].reshape(8), g["ml_f_bias"][l].reshape(8)], 1) for l in range(L)], 1)
    mlgain = np.broadcast_to(g["ml_head_gain"][:L].reshape(1, L * 512), (128, L * 512))
    cw = []
    for l in range(L):
        for ci in range(8):
            ch = slice(ci * 128, (ci + 1) * 128)
            cw.append(np.concatenate([g["ssd_conv_w"][l][:, ch].T, g["ssd_conv_b"][l][ch][:, None]], 1))
    convw = np.concatenate(cw, 1)
    sdt = np.concatenate([np.stack([g["ssd_dt_bias"][l][0], g["ssd_dt_bias"][l][1], g["ssd_a_log"][l][0], g["ssd_a_log"][l][1]], 1)
                          for l in range(L)], 1)
    ssdskip = np.broadcast_to(np.repeat(g["ssd_d"][:L], 64, axis=1).reshape(1, L * 512), (128, L * 512))
    ssdng = np.broadcast_to(g["ssd_norm_gain"][:L].reshape(1, L * 512), (128, L * 512))
    shared = dict(wmod=wmod, bmod=bmod, gn=gn, sinkb=sinkb, mlb=mlb, mlgain=mlgain, convw=convw, ssddt=sdt,
                  ssdskip=ssdskip, ssdng=ssdng, **consts)
    for l in range(L):
        shared[f"wl{l}"] = wl[l]
    shared = {k: np.ascontiguousarray(v, dtype=f32) for k, v in shared.items()}
    ncores = g["x"].shape[0] // NB
    maps = []
    for c in range(ncores):
        bs = slice(c * NB, (c + 1) * NB)
        m = dict(shared)
        m["xT"] = np.ascontiguousarray(g["x"][bs].transpose(0, 2, 1))
        m["ctxT"] = np.ascontiguousarray(g["ctx"][bs].transpose(0, 2, 1))
        cc = np.stack([g["c"][c * NB + i] if i < NB else g["c_ctx"] for i in range(2)] + [g["c_ctx"]], 0)
        if NB == 1:
            cc = np.stack([g["c"][c], g["c"][c], g["c_ctx"]], 0)
        silu_in = cc.reshape(3, 8, 128).transpose(2, 1, 0).reshape(128, 24)
        m["cT"] = np.ascontiguousarray(silu_in)
        maps.append(m)
    return maps


def kernel(**inputs):
    NB = 2
    if "nc" not in _CACHE:
        _CACHE["nc"] = build(NB=NB)[0]
    nc = _CACHE["nc"]
    maps = host_prep(inputs, NB=NB)
    res = run_bass_kernel_spmd(nc, maps, core_ids=list(range(len(maps))))
    outs = [r["outT"] for r in res.results]
    full = np.concatenate(outs, 0)
    return np.ascontiguousarray(full.transpose(0, 2, 1)).astype(np.float32)
```
